# Optimizing a Trainium2 kernel written in Bass

```python
import math
import jax, jax.numpy as jnp
from jax import lax
import numpy as np

D_MODEL = 2048
BATCH = 4
SEQ = 4096
DEPTH = 1

A_HEADS = 8
A_HEAD_DIM = 128
A_WIDTH = A_HEADS * A_HEAD_DIM
MOBA_BLOCK = 256
MOBA_TOPK = 3
MOBA_Q_CHUNK = 32
B_Q_HEADS = 16
B_KV_HEADS = 2
B_HEAD_DIM = 64
B_WIDTH = B_Q_HEADS * B_HEAD_DIM
B_KV_WIDTH = B_KV_HEADS * B_HEAD_DIM
WINDOW = 128
NUM_BUCKETS = 32
MAX_DISTANCE = 128
MAX_EXACT = NUM_BUCKETS // 2
N_BIAS_HEADS = A_HEADS + B_Q_HEADS
D_FF = -(-(8 * D_MODEL) // (3 * 256)) * 256
IN_SPLITS = [A_WIDTH, 2 * A_WIDTH, 3 * A_WIDTH,
             3 * A_WIDTH + B_WIDTH,
             3 * A_WIDTH + B_WIDTH + B_KV_WIDTH,
             3 * A_WIDTH + B_WIDTH + 2 * B_KV_WIDTH,
             3 * A_WIDTH + B_WIDTH + 2 * B_KV_WIDTH + D_MODEL]
IN_WIDTH = 3 * A_WIDTH + B_WIDTH + 2 * B_KV_WIDTH + 2 * D_MODEL
EPS = 1e-6
NEG = -1e30

kernel_name = "hybrid_moba_swa_sink_t5bias_block"


def rmsnorm(x, g):
    xf = x.astype(jnp.float32)
    y = xf * lax.rsqrt(jnp.mean(xf * xf, axis=-1, keepdims=True) + EPS)
    return y.astype(x.dtype) * g.astype(x.dtype)


def t5_bucket(dist):
    n = jnp.maximum(dist, 0)
    nf = jnp.maximum(n, 1).astype(jnp.float32)
    large = MAX_EXACT + (jnp.log(nf / MAX_EXACT) / math.log(MAX_DISTANCE / MAX_EXACT)
                         * (NUM_BUCKETS - MAX_EXACT)).astype(jnp.int32)
    large = jnp.minimum(large, NUM_BUCKETS - 1)
    return jnp.where(n < MAX_EXACT, n, large)


def moba_attention(q, k, v, table_a):
    B_, H, S, Dh = q.shape
    L = MOBA_BLOCK
    nb = -(-S // L)
    s_pad = nb * L
    pad = ((0, 0), (0, 0), (0, s_pad - S), (0, 0))
    kb = jnp.pad(k, pad).reshape(B_, H, nb, L, Dh)
    vb = jnp.pad(v, pad).reshape(B_, H, nb, L, Dh)
    kmean = jnp.mean(kb.astype(jnp.float32), axis=3)
    gate = jnp.einsum('bhsd,bhnd->bhsn', q.astype(jnp.float32), kmean)
    q_blk = jnp.arange(S) // L
    past = jnp.arange(nb)[None, :] < q_blk[:, None]
    gate = jnp.where(past, gate, NEG)
    k_sel = min(MOBA_TOPK, nb)
    _, sel = lax.top_k(gate, k_sel)
    sel_valid = sel < q_blk[:, None]

    C = MOBA_Q_CHUNK
    n_chunks = S // C
    q_ch = q.reshape(B_, H, n_chunks, C, Dh).transpose(2, 0, 1, 3, 4)
    sel_ch = sel.reshape(B_, H, n_chunks, C, k_sel).transpose(2, 0, 1, 3, 4)
    val_ch = sel_valid.reshape(B_, H, n_chunks, C, k_sel).transpose(2, 0, 1, 3, 4)
    b_ix = jnp.arange(B_)[:, None, None, None]
    h_ix = jnp.arange(H)[None, :, None, None]
    offs = jnp.arange(L)
    scale = Dh ** -0.5

    def chunk(args):
        c, qc, selc, validc = args
        q0 = c * C
        qpos = q0 + jnp.arange(C)
        own = q0 // L
        k_g = kb[b_ix, h_ix, selc]
        v_g = vb[b_ix, h_ix, selc]
        k_own = lax.dynamic_index_in_dim(kb, own, axis=2, keepdims=False)
        v_own = lax.dynamic_index_in_dim(vb, own, axis=2, keepdims=False)
        s_sel = jnp.einsum('bhcd,bhcnld->bhcnl', qc, k_g).reshape(B_, H, C, k_sel * L)
        s_own = jnp.einsum('bhcd,bhld->bhcl', qc, k_own)
        kpos_sel = (selc[..., None] * L + offs).reshape(B_, H, C, k_sel * L)
        dist_sel = qpos[None, None, :, None] - kpos_sel
        bias_sel = table_a[t5_bucket(dist_sel), h_ix].astype(jnp.float32)
        dist_own = qpos[:, None] - (own * L + offs)[None, :]
        bias_own = table_a[t5_bucket(dist_own)].transpose(2, 0, 1).astype(jnp.float32)
        mask_sel = jnp.broadcast_to(validc[..., None], (B_, H, C, k_sel, L)).reshape(B_, H, C, k_sel * L)
        mask_own = dist_own >= 0
        l_sel = jnp.where(mask_sel, s_sel.astype(jnp.float32) * scale + bias_sel, NEG)
        l_own = jnp.where(mask_own, s_own.astype(jnp.float32) * scale + bias_own, NEG)
        p = jax.nn.softmax(jnp.concatenate([l_sel, l_own], axis=-1), axis=-1).astype(v.dtype)
        p_sel = p[..., :k_sel * L].reshape(B_, H, C, k_sel, L)
        p_own = p[..., k_sel * L:]
        return (jnp.einsum('bhcnl,bhcnld->bhcd', p_sel, v_g)
                + jnp.einsum('bhcl,bhld->bhcd', p_own, v_own))

    out = lax.map(chunk, (jnp.arange(n_chunks), q_ch, sel_ch, val_ch))
    return out.transpose(1, 2, 0, 3, 4).reshape(B_, H, S, Dh)


def swa_sink_attention(q, k, v, sinks, table_b):
    B_, Hq, S, Dh = q.shape
    Hkv = k.shape[1]
    G = Hq // Hkv
    W = WINDOW
    nq = S // W
    qb = q.reshape(B_, Hkv, G, nq, W, Dh)
    pad = ((0, 0), (0, 0), (W, 0), (0, 0))
    kp = jnp.pad(k, pad).reshape(B_, Hkv, nq + 1, W, Dh)
    vp = jnp.pad(v, pad).reshape(B_, Hkv, nq + 1, W, Dh)
    kband = jnp.concatenate([kp[:, :, :-1], kp[:, :, 1:]], axis=3)
    vband = jnp.concatenate([vp[:, :, :-1], vp[:, :, 1:]], axis=3)
    s = jnp.einsum('bkgnqd,bknld->bkgnql', qb, kband).astype(jnp.float32) * (Dh ** -0.5)
    qpos = jnp.arange(nq)[:, None] * W + jnp.arange(W)[None, :]
    kpos = jnp.arange(nq)[:, None] * W - W + jnp.arange(2 * W)[None, :]
    dist = qpos[:, :, None] - kpos[:, None, :]
    mask = (dist >= 0) & (dist < W) & (kpos[:, None, :] >= 0)
    bias = table_b[t5_bucket(dist)].astype(jnp.float32)
    bias = bias.transpose(3, 0, 1, 2).reshape(Hkv, G, nq, W, 2 * W)
    logits = jnp.where(mask, s + bias, NEG)
    sink = jnp.broadcast_to(sinks.astype(jnp.float32).reshape(1, Hkv, G, 1, 1, 1),
                            (B_, Hkv, G, nq, W, 1))
    p = jax.nn.softmax(jnp.concatenate([logits, sink], axis=-1), axis=-1)[..., :-1]
    out = jnp.einsum('bkgnql,bknld->bkgnqd', p.astype(v.dtype), vband)
    return out.reshape(B_, Hq, S, Dh)


def setup_inputs(seed: int = 0) -> dict:
    key = jax.random.key(seed)
    ks = jax.random.split(key, 16)

    def nrm(k, shape, scale):
        return jax.random.normal(k, shape, jnp.float32) * scale

    return {
        "x": nrm(ks[0], (BATCH, SEQ, D_MODEL), 1.0),
        "norm1_g": 1.0 + nrm(ks[1], (DEPTH, D_MODEL), 0.02),
        "w_in": nrm(ks[2], (DEPTH, D_MODEL, IN_WIDTH), D_MODEL ** -0.5),
        "q_norm_a": 1.0 + nrm(ks[3], (DEPTH, A_HEAD_DIM), 0.02),
        "k_norm_a": 1.0 + nrm(ks[4], (DEPTH, A_HEAD_DIM), 0.02),
        "q_norm_b": 1.0 + nrm(ks[5], (DEPTH, B_HEAD_DIM), 0.02),
        "k_norm_b": 1.0 + nrm(ks[6], (DEPTH, B_HEAD_DIM), 0.02),
        "rel_bias": nrm(ks[7], (NUM_BUCKETS, N_BIAS_HEADS), 0.5),
        "sinks": nrm(ks[8], (DEPTH, B_Q_HEADS), 0.5),
        "w_branch_a": nrm(ks[9], (DEPTH, A_WIDTH, D_MODEL), A_WIDTH ** -0.5),
        "w_branch_b": nrm(ks[10], (DEPTH, B_WIDTH, D_MODEL), B_WIDTH ** -0.5),
        "w_out": nrm(ks[11], (DEPTH, D_MODEL, D_MODEL), D_MODEL ** -0.5),
        "norm2_g": 1.0 + nrm(ks[12], (DEPTH, D_MODEL), 0.02),
        "w_gate_up": nrm(ks[13], (DEPTH, D_MODEL, 2 * D_FF), D_MODEL ** -0.5),
        "w_down": nrm(ks[14], (DEPTH, D_FF, D_MODEL), D_FF ** -0.5),
    }


def reference(x, norm1_g, w_in, q_norm_a, k_norm_a, q_norm_b, k_norm_b, rel_bias, sinks,
              w_branch_a, w_branch_b, w_out, norm2_g, w_gate_up, w_down):
    B_, S, _ = x.shape
    table_a = rel_bias[:, :A_HEADS]
    table_b = rel_bias[:, A_HEADS:]

    def heads(t, n, d):
        return t.reshape(B_, S, n, d).transpose(0, 2, 1, 3)

    for l in range(DEPTH):
        h = rmsnorm(x, norm1_g[l])
        proj = h @ w_in[l].astype(h.dtype)
        qa, ka, va, qb, kb, vb, ga, gb = jnp.split(proj, IN_SPLITS, axis=-1)
        qa = rmsnorm(heads(qa, A_HEADS, A_HEAD_DIM), q_norm_a[l])
        ka = rmsnorm(heads(ka, A_HEADS, A_HEAD_DIM), k_norm_a[l])
        va = heads(va, A_HEADS, A_HEAD_DIM)
        qb = rmsnorm(heads(qb, B_Q_HEADS, B_HEAD_DIM), q_norm_b[l])
        kb = rmsnorm(heads(kb, B_KV_HEADS, B_HEAD_DIM), k_norm_b[l])
        vb = heads(vb, B_KV_HEADS, B_HEAD_DIM)
        ya = moba_attention(qa, ka, va, table_a).transpose(0, 2, 1, 3).reshape(B_, S, A_WIDTH)
        yb = swa_sink_attention(qb, kb, vb, sinks[l], table_b).transpose(0, 2, 1, 3).reshape(B_, S, B_WIDTH)
        merged = (jax.nn.sigmoid(ga) * (ya @ w_branch_a[l].astype(ya.dtype))
                  + jax.nn.sigmoid(gb) * (yb @ w_branch_b[l].astype(yb.dtype)))
        x = x + merged @ w_out[l].astype(merged.dtype)
        h2 = rmsnorm(x, norm2_g[l])
        g, u = jnp.split(h2 @ w_gate_up[l].astype(h2.dtype), 2, axis=-1)
        x = x + (jax.nn.silu(g) * u) @ w_down[l].astype(h2.dtype)
    return x
```

```python
import math
import numpy as np
from contextlib import ExitStack

import concourse.bass as bass
import concourse.mybir as mybir
from concourse.bass_utils import run_bass_kernel_spmd

F32 = mybir.dt.float32
BF16 = mybir.dt.bfloat16
U8 = mybir.dt.uint8
AF = mybir.ActivationFunctionType
ALU = mybir.AluOpType

D = 2048
SEQ = 4096
NB = 16
LB = 256
DFF = 5632
NEG = -30000.0
EPS = 1e-6
INW = 8448
C_QA, C_KA, C_VA, C_QB, C_KB, C_VB, C_GA, C_GB = 0, 1024, 2048, 3072, 4096, 4224, 4352, 6400
NVEC = 52

ENGINES = ("pe", "act", "dve", "pool", "sp")
SEM_EPOCH = 20000
NDMA_SEMS = 8


class Buf:
    __slots__ = ("name", "last_w", "readers", "dma_readers")

    def __init__(self, name):
        self.name = name
        self.last_w = None
        self.readers = {}
        self.dma_readers = []


class Op:
    __slots__ = ("eng", "fn", "deps", "is_dma", "ordinal", "sig", "dma_slot", "dma_val")

    def __init__(self, eng, fn, is_dma):
        self.eng = eng
        self.fn = fn
        self.deps = []
        self.is_dma = is_dma
        self.ordinal = None
        self.sig = False
        self.dma_slot = None
        self.dma_val = None


class Sched:
    def __init__(self, nc):
        self.nc = nc
        self.ops = {e: [] for e in ENGINES}
        self.ndma = {e: 0 for e in ENGINES}
        self.final_dmas = []
        self.recent_dmas = {e: [] for e in ENGINES}
        self.pending_bar = {e: None for e in ENGINES}

    def _record(self, op, reads, writes):
        deps = {}

        def add(d):
            if d is None or d is op:
                return
            deps[id(d)] = d

        bar = self.pending_bar[op.eng]
        if bar is not None:
            for d in bar:
                add(d)
            self.pending_bar[op.eng] = None
        for b in reads:
            add(b.last_w)
        for b in writes:
            add(b.last_w)
            for r in b.readers.values():
                add(r)
            for r in b.dma_readers:
                add(r)
        for b in reads:
            if op.is_dma:
                b.dma_readers.append(op)
            else:
                b.readers[op.eng] = op
        for b in writes:
            b.last_w = op
            b.readers = {}
            b.dma_readers = []
        for d in deps.values():
            if (not d.is_dma) and (not op.is_dma) and d.eng == op.eng and op.eng == "pe":
                continue
            op.deps.append(d)
            if not d.is_dma:
                d.sig = True
        self.ops[op.eng].append(op)
        return op

    def op(self, eng, fn, reads=(), writes=()):
        return self._record(Op(eng, fn, False), reads, writes)

    def dma(self, queue, fn, reads=(), writes=(), final=False):
        o = Op(queue, fn, True)
        n = self.ndma[queue]
        self.ndma[queue] = n + 1
        o.dma_slot = n % NDMA_SEMS
        o.dma_val = 16 * (n // NDMA_SEMS + 1)
        self._record(o, reads, writes)
        rd = self.recent_dmas[queue]
        rd.append(o)
        if len(rd) > NDMA_SEMS:
            rd.pop(0)
        if final:
            self.final_dmas.append(o)
        return o

    def barrier(self):
        deps = []
        for e in ENGINES:
            last = None
            for o in reversed(self.ops[e]):
                if not o.is_dma:
                    last = o
                    break
            if last is not None:
                deps.append(last)
            deps.extend(self.recent_dmas[e])
        for e in ENGINES:
            prev = self.pending_bar[e]
            self.pending_bar[e] = list(deps) + (prev or [])

    def emit(self, stack):
        nc = self.nc
        nsig = {}
        for e in ENGINES:
            k = 0
            for o in self.ops[e]:
                if (not o.is_dma) and o.sig:
                    o.ordinal = k
                    k += 1
            nsig[e] = k
        csems = {}
        for e in ENGINES:
            n_ep = (nsig[e] + SEM_EPOCH - 1) // SEM_EPOCH
            csems[e] = [stack.enter_context(nc.semaphore(f"c_{e}_{i}")) for i in range(n_ep)]
        dsems = {}
        for e in ENGINES:
            if self.ndma[e]:
                dsems[e] = [stack.enter_context(nc.semaphore(f"d_{e}_{i}")) for i in range(NDMA_SEMS)]

        def target(d):
            if d.is_dma:
                return dsems[d.eng][d.dma_slot], d.dma_val
            return csems[d.eng][d.ordinal // SEM_EPOCH], d.ordinal % SEM_EPOCH + 1

        block = stack.enter_context(nc.Block())
        engmap = {"pe": "tensor", "act": "scalar", "dve": "vector", "pool": "gpsimd", "sp": "sync"}

        def body(ename, eng):
            waited = {}

            def wait(sem, val):
                key = id(sem)
                if waited.get(key, 0) >= val:
                    return
                waited[key] = val
                eng.wait_ge(sem, val)

            for o in self.ops[ename]:
                if o.is_dma and o.dma_val > 16:
                    wait(dsems[ename][o.dma_slot], o.dma_val - 16)
                for d in o.deps:
                    s, v = target(d)
                    wait(s, v)
                inst = o.fn(eng)
                if o.is_dma:
                    inst.then_inc(dsems[ename][o.dma_slot], 16)
                elif o.sig:
                    inst.then_inc(csems[ename][o.ordinal // SEM_EPOCH], 1)
            if ename == "sp":
                for d in self.final_dmas:
                    s, v = target(d)
                    wait(s, v)

        for ename in ENGINES:
            getattr(block, engmap[ename])(lambda eng, _n=ename: body(_n, eng))


class Tile:
    def __init__(self, t, bufs):
        self.t = t
        self.b = bufs


def build_program(debug=False):
    nc = bass.Bass("TRN2", target_bir_lowering=False)

    def din(name, shape):
        return nc.dram_tensor(name, list(shape), F32, kind="ExternalInput").ap()

    xc = din("xc", [SEQ, D])
    xo = din("xo", [2048, D])
    xp = din("xp", [8 * 128, D])
    w_in = din("w_in", [D, INW])
    w_ba = din("w_ba", [1024, D])
    w_bb = din("w_bb", [1024, D])
    w_out = din("w_out", [D, D])
    w_gu = din("w_gu", [D, 2 * DFF])
    w_dn = din("w_dn", [DFF, D])
    vecs_d = din("vecs", [128, NVEC])
    mconst_d = din("mconst", [128, 8 * 4 * 16])
    ident_d = din("ident", [128, 128])
    ones_d = din("onesbd", [128, 256])
    indoh_d = din("indoh", [128, 16 * 128])
    biasA_d = din("biasA", [3 * 8 * 128, 512])
    biasB_d = din("biasB", [3 * 2 * 128, 1024])
    y_d = nc.dram_tensor("y", [2048, D], F32, kind="ExternalOutput").ap()
    dbg = {}
    if debug:
        dbg["yab"] = nc.dram_tensor("dbg_yab", [128, 16 * 1024], F32, kind="ExternalOutput").ap()

    w_in_v = w_in.rearrange("(c p) n -> p c n", p=128)
    w_ba_v = w_ba.rearrange("(c p) n -> p c n", p=128)
    w_bb_v = w_bb.rearrange("(c p) n -> p c n", p=128)
    w_out_v = w_out.rearrange("(c p) n -> p c n", p=128)
    w_gu_v = w_gu.rearrange("(c p) n -> p c n", p=128)
    w_dn_v = w_dn.rearrange("(c p) n -> p c n", p=128)
    xc_v = xc.rearrange("(t p) d -> t p d", p=128)
    xo_v = xo.rearrange("(t p) d -> t p d", p=128)
    xp_v = xp.rearrange("(t p) d -> t p d", p=128)
    y_v = y_d.rearrange("(t p) d -> t p d", p=128)
    biasA_v = biasA_d.rearrange("(i p) n -> i p n", p=128)
    biasB_v = biasB_d.rearrange("(i p) n -> i p n", p=128)

    st = ExitStack()
    with st:
        S = Sched(nc)
        ARENA = 212000
        st.enter_context(nc.sbuf_tensor("arena", [128, ARENA], U8))
        abase = nc.sbuf_base - ARENA
        uid = [0]

        class Region:
            def __init__(self, lo, hi):
                self.lo, self.hi, self.p = lo, hi, lo

            def alloc(self, name, shape, dt, nbufs=1, parts=None):
                esz = 2 if dt == BF16 else 4
                nbytes = int(np.prod(shape[1:])) * esz
                nbytes = (nbytes + 31) // 32 * 32
                off = self.p
                self.p += nbytes
                assert self.p <= self.hi, (name, self.p, self.hi)
                uid[0] += 1
                t = nc.alloc_sbuf_tensor_at(f"{name}_{uid[0]}", list(shape), dt, offset=abase + off)
                return Tile(t, [Buf(f"{name}{i}") for i in range(nbufs)])

            def rest(self):
                return Region(self.p, self.hi)

        top = Region(0, ARENA)

        PB = [st.enter_context(nc.psum_tensor(f"pb{i}", [128, 512], F32)) for i in range(6)]
        PBb = [Buf(f"pb{i}") for i in range(6)]
        TP = [st.enter_context(nc.psum_tensor(f"tp{i}", [128, 8, 128], BF16)) for i in range(2)]
        TPb = [Buf(f"tp{i}") for i in range(2)]

        vecs = top.alloc("vecs", [128, NVEC], F32)
        mconst = top.alloc("mconst", [128, 8, 4, 16], F32)
        ident = top.alloc("ident", [128, 128], BF16)
        ones_bf = top.alloc("ones_bf", [128, 128], BF16)
        onesf = top.alloc("onesf", [128, 256], F32)
        indoh = top.alloc("indoh", [128, 16, 128], BF16)
        gv = top.alloc("gv", [128, 4], F32)
        expsink = top.alloc("expsink", [128, 8], F32)
        fb = top.alloc("fb", [128, 8, 8, 16], F32)
        ssq = top.alloc("ssq", [128, 4], F32, nbufs=2)
        yab = top.alloc("yab", [128, 16, 1024], BF16, nbufs=16 * 4)

        def yab_b(chunk, blk):
            return yab.b[chunk * 4 + blk]

        def ACT(reads, writes, **kw):
            S.op("act", lambda e: e.activation(**kw), reads, writes)

        def DVE(meth, reads, writes, **kw):
            S.op("dve", lambda e: getattr(e, meth)(**kw), reads, writes)

        def TR(reads, writes, **kw):
            S.op("pe", lambda e: e.transpose(**kw), reads, writes)

        def DMA(q, out, in_, reads=(), writes=(), final=False):
            S.dma(q, lambda e: e.dma_start(out=out, in_=in_), reads, writes, final)

        def mm(out, lhsT, rhs, start, stop, reads, writes, **kw):
            S.op("pe", lambda e: e.matmul(out, lhsT=lhsT, rhs=rhs, start=start, stop=stop, **kw), reads, writes)

        def bc(ap2d, n):
            a = ap2d.shape[1]
            return ap2d.rearrange("p (a b) -> p a b", b=1).broadcast_to([128, a, n])

        DMA("sp", vecs.t[:], vecs_d[:, :], writes=vecs.b)
        DMA("sp", mconst.t[:].rearrange("p a b c -> p (a b c)"), mconst_d[:, :], writes=mconst.b)
        DMA("sp", onesf.t[:], ones_d[:, :], writes=onesf.b)
        DMA("pool", ident.t[:], ident_d[:, :], writes=ident.b)
        DMA("pool", ones_bf.t[:], ones_d[:, 0:128], writes=ones_bf.b)
        DMA("pool", indoh.t[:].rearrange("p a b -> p (a b)"), indoh_d[:, :], writes=indoh.b)
        DVE("tensor_scalar", vecs.b, gv.b, out=gv.t[:, 0:1], in0=vecs.t[:, 32:33], scalar1=128.0 ** -0.5, scalar2=None, op0=ALU.mult)
        DVE("tensor_copy", vecs.b, gv.b, out=gv.t[:, 1:2], in_=vecs.t[:, 33:34])
        DVE("tensor_scalar", vecs.b, gv.b, out=gv.t[:, 2:3], in0=vecs.t[:, 34:35], scalar1=64.0 ** -0.5, scalar2=None, op0=ALU.mult)
        DVE("tensor_copy", vecs.b, gv.b, out=gv.t[:, 3:4], in_=vecs.t[:, 35:36])
        ACT(vecs.b, expsink.b, out=expsink.t[:], in_=vecs.t[:, 44:52], func=AF.Exp)
        for h in range(8):
            DVE("tensor_scalar", vecs.b + mconst.b, fb.b, out=fb.t[:, :, h, :], in0=mconst.t[:, :, 3, :], scalar1=vecs.t[:, 36 + h:37 + h], scalar2=None, op0=ALU.mult)

        tp_rr = [0]
        ssq_rr = [0]

        def norm_tile(x_ap, x_bufs, xb, gcol0, hT, hT_bufs, tok0):
            k = ssq_rr[0] % 2
            ssq_rr[0] += 1
            sb = [ssq.b[k]]
            s0 = ssq.t[:, 2 * k:2 * k + 1]
            s1 = ssq.t[:, 2 * k + 1:2 * k + 2]
            ACT(list(x_bufs), xb.b + sb, out=xb.t[:], in_=x_ap, func=AF.Square, accum_out=s0)
            ACT(sb, sb, out=s1, in_=s0, func=AF.Sqrt, scale=1.0 / D, bias=EPS)
            DVE("reciprocal", sb, sb, out=s1, in_=s1)
            DVE("tensor_scalar", list(x_bufs) + sb, xb.b, out=xb.t[:], in0=x_ap, scalar1=s1, scalar2=None, op0=ALU.mult)
            for half in range(2):
                ti = tp_rr[0] % 2
                tp_rr[0] += 1
                for kc in range(8):
                    c = half * 8 + kc
                    TR(xb.b + ident.b, [TPb[ti]], out=TP[ti][:, kc, :], in_=xb.t[:, c * 128:(c + 1) * 128], identity=ident.t[:])
                DVE("tensor_tensor", [TPb[ti]] + vecs.b, list(hT_bufs), out=hT.t[:, half * 8:(half + 1) * 8, tok0:tok0 + 128], in0=TP[ti][:],
                    in1=bc(vecs.t[:, gcol0 + half * 8:gcol0 + half * 8 + 8], 128), op=ALU.mult)

        def qknorm(pj, pjb, N, ones_ap, invd, gcol, segs, scr, ssb=1):
            sq, rt = scr
            ACT([pjb], sq.b, out=sq.t[:, 0:N], in_=pj[:, 0:N], func=AF.Square)
            mm(PB[ssb][:, 0:N], ones_ap, sq.t[:, 0:N], True, True, sq.b + onesf.b, [PBb[ssb]])
            ACT([PBb[ssb]], rt.b, out=rt.t[:, 0:N], in_=PB[ssb][:, 0:N], func=AF.Sqrt, scale=invd, bias=EPS)
            DVE("reciprocal", rt.b, rt.b, out=rt.t[:, 0:N], in_=rt.t[:, 0:N])
            for (out_ap, c0, c1, acc_ap, obufs, abufs) in segs:
                DVE("scalar_tensor_tensor", [pjb] + gv.b + rt.b, list(obufs) + list(abufs), out=out_ap, in0=pj[:, c0:c1], scalar=gv.t[:, gcol:gcol + 1],
                    in1=rt.t[:, c0:c1], op0=ALU.mult, op1=ALU.mult, accum_out=acc_ap)

        def slot(ring, i):
            return ring.t[:, i * 2048:(i + 1) * 2048]

        def slot128(ring, i):
            return slot(ring, i).rearrange("p (c n) -> p c n", n=128)

        for th in range(2):
            for hp in range(2):
                S.barrier()
                hpR = top.rest()
                kAT = hpR.alloc("kAT", [128, 4, SEQ], BF16, nbufs=4 * 8)
                vA = hpR.alloc("vA", [128, 32, 512], BF16, nbufs=8)
                WA = hpR.alloc("WA", [128, 4, 3, 512], BF16)
                WB = hpR.alloc("WB", [128, 3, 1024], BF16)
                ksum = hpR.alloc("ksum", [128, 4, 16], F32)
                kmT = hpR.alloc("kmT", [128, 4, 16], BF16)
                A = hpR.rest()
                ring = A.alloc("ringA", [128, 8 * 2048], BF16, nbufs=8)
                xt = A.alloc("xtA", [128, 2, D], F32, nbufs=2)
                xb = A.alloc("xbA", [128, D], BF16)
                hTc = A.alloc("hTc", [128, 16, 512], BF16, nbufs=4)
                sq = A.alloc("sqA", [128, 512], F32)
                rt = A.alloc("rtA", [128, 512], F32)
                DVE("memset", [], ksum.b, ap=ksum.t[:], constant=0.0)
                for hl in range(4):
                    for i in range(3):
                        stg = xt.t[:, 0, 0:512]
                        DMA("sp", stg, biasA_v[i * 8 + 4 * hp + hl, :, :], writes=[xt.b[0]])
                        ACT([xt.b[0]], WA.b, out=WA.t[:, hl, i, :], in_=stg, func=AF.Exp)
                for kind in range(3):
                    stg = xt.t[:, 1, 0:1024]
                    DMA("sp", stg, biasB_v[kind * 2 + hp, :, :], writes=[xt.b[1]])
                    ACT([xt.b[1]], WB.b, out=WB.t[:, kind, :], in_=stg, func=AF.Exp)
                for hl in range(4):
                    c0 = C_KA + (4 * hp + hl) * 128
                    DMA("pool", slot128(ring, hl), w_in_v[:, :, c0:c0 + 128], writes=[ring.b[hl]])
                c0 = C_VA + hp * 512
                wv_view = ring.t[:, 4 * 2048:8 * 2048].rearrange("p (c n) -> p c n", n=512)
                DMA("pool", wv_view, w_in_v[:, :, c0:c0 + 512], writes=ring.b[4:8])
                n_ct = 4 if th == 0 else 8
                pj_rr = 0
                pv_rr = 0
                for ct in range(n_ct):
                    for s in range(4):
                        k = s % 2
                        DMA("sp", xt.t[:, k, :], xc_v[ct * 4 + s], writes=[xt.b[k]])
                        norm_tile(xt.t[:, k, :], [xt.b[k]], xb, 0, hTc, [hTc.b[s]], s * 128)
                    for hl in range(4):
                        pi = pj_rr % 2
                        pj_rr += 1
                        wk = slot128(ring, hl)
                        for kc in range(16):
                            mm(PB[pi][:, 0:512], wk[:, kc, :], hTc.t[:, kc, :], kc == 0, kc == 15, [ring.b[hl]] + hTc.b, [PBb[pi]])
                        segs = []
                        for bb in range(2):
                            blk = ct * 2 + bb
                            segs.append((kAT.t[:, hl, ct * 512 + bb * 256:ct * 512 + (bb + 1) * 256], bb * 256, (bb + 1) * 256,
                                         ksum.t[:, hl, blk:blk + 1], [kAT.b[hl * 8 + ct]], ksum.b))
                        qknorm(PB[pi], PBb[pi], 512, onesf.t[:, 0:128], 1.0 / 128, 1, segs, (sq, rt), ssb=2)
                    for s in range(4):
                        pi = 3 + pv_rr % 2
                        pv_rr += 1
                        for kc in range(16):
                            mm(PB[pi][:, 0:512], hTc.t[:, kc, s * 128:(s + 1) * 128], wv_view[:, kc, :], kc == 0, kc == 15, [hTc.b[s]] + ring.b[4:8], [PBb[pi]])
                        ACT([PBb[pi]], [vA.b[ct]], out=vA.t[:, ct * 4 + s, :], in_=PB[pi][:, 0:512], func=AF.Copy)
                DVE("tensor_scalar", ksum.b, kmT.b, out=kmT.t[:], in0=ksum.t[:], scalar1=1.0 / 256, scalar2=None, op0=ALU.mult)

                S.barrier()
                B = hpR.rest()
                ring = B.alloc("ringB", [128, 4 * 2048], BF16, nbufs=4)
                xt = B.alloc("xtB", [128, D], F32)
                xb = B.alloc("xbB", [128, D], BF16)
                hTo = B.alloc("hTo", [128, 16, 384], BF16, nbufs=3)
                sq = B.alloc("sqB", [128, 384], F32)
                rt = B.alloc("rtB", [128, 384], F32)
                QAT = B.alloc("QAT", [128, 4, 256], BF16, nbufs=4)
                QBT = B.alloc("QBT", [128, 4, 256], BF16, nbufs=4)
                kB2T = B.alloc("kB2T", [128, 384], BF16)
                vB = B.alloc("vB", [128, 3, 64], BF16)
                MT = B.alloc("MT", [128, 4, 256], BF16, nbufs=4)
                Mbf = B.alloc("Mbf", [128, 2, 128], BF16, nbufs=2)
                gm = B.alloc("gm", [128, 16], F32)
                t8 = B.alloc("t8", [128, 8], F32)
                selt = B.alloc("selt", [128, 16], F32)
                m1t = B.alloc("m1t", [128, 16], F32)
                Pm = B.alloc("Pm", [128, 3, 512], BF16, nbufs=3)
                Eb = B.alloc("Eb", [128, 2, 512], BF16, nbufs=2)
                Ps = B.alloc("Ps", [128, 2, 512], BF16, nbufs=2)
                rden = B.alloc("rden", [128, 512], F32)
                ring_rr = [0]
                DVE("memset", [], Mbf.b, ap=Mbf.t[:], constant=0.0)

                def ring_next():
                    i = ring_rr[0] % 4
                    ring_rr[0] += 1
                    return i

                for jl in range(4):
                    j = 4 * th + jl
                    tok0 = jl * 256
                    for r in range(3):
                        src_ap = xp_v[j] if r == 0 else xo_v[j * 2 + r - 1]
                        DMA("sp", xt.t[:], src_ap, writes=xt.b)
                        norm_tile(xt.t[:], xt.b, xb, 0, hTo, [hTo.b[r]], r * 128)
                    for hl in range(4):
                        si = ring_next()
                        c0 = C_QA + (4 * hp + hl) * 128
                        wq = slot128(ring, si)
                        DMA("pool", wq, w_in_v[:, :, c0:c0 + 128], writes=[ring.b[si]])
                        for kc in range(16):
                            mm(PB[0][:, 0:256], wq[:, kc, :], hTo.t[:, kc, 128:384], kc == 0, kc == 15, [ring.b[si]] + hTo.b[1:3], [PBb[0]])
                        qknorm(PB[0], PBb[0], 256, onesf.t[:, 0:128], 1.0 / 128, 0, [(QAT.t[:, hl, :], 0, 256, None, [QAT.b[hl]], [])], (sq, rt))
                    for c in range(4):
                        si = ring_next()
                        c0 = C_QB + 512 * hp + 128 * c
                        wq = slot128(ring, si)
                        DMA("pool", wq, w_in_v[:, :, c0:c0 + 128], writes=[ring.b[si]])
                        for kc in range(16):
                            mm(PB[0][:, 0:256], wq[:, kc, :], hTo.t[:, kc, 128:384], kc == 0, kc == 15, [ring.b[si]] + hTo.b[1:3], [PBb[0]])
                        qknorm(PB[0], PBb[0], 256, onesf.t[:, 128:256], 1.0 / 64, 2, [(QBT.t[:, c, :], 0, 256, None, [QBT.b[c]], [])], (sq, rt))
                    si = ring_next()
                    c0 = C_KB + 64 * hp
                    kview = slot128(ring, si)
                    DMA("pool", kview[:, :, 0:64], w_in_v[:, :, c0:c0 + 64], writes=[ring.b[si]])
                    DMA("pool", kview[:, :, 64:128], w_in_v[:, :, c0:c0 + 64], writes=[ring.b[si]])
                    for kc in range(16):
                        mm(PB[0][:, 0:384], kview[:, kc, :], hTo.t[:, kc, 0:384], kc == 0, kc == 15, [ring.b[si]] + hTo.b, [PBb[0]])
                    qknorm(PB[0], PBb[0], 384, onesf.t[:, 128:256], 1.0 / 64, 3, [(kB2T.t[:, :], 0, 384, None, kB2T.b, [])], (sq, rt))
                    si = ring_next()
                    c0 = C_VB + 64 * hp
                    vview = slot(ring, si)[:, 0:1024].rearrange("p (c n) -> p c n", n=64)
                    DMA("pool", vview, w_in_v[:, :, c0:c0 + 64], writes=[ring.b[si]])
                    for r in range(3):
                        for kc in range(16):
                            mm(PB[0][:, 0:64], hTo.t[:, kc, r * 128:(r + 1) * 128], vview[:, kc, :], kc == 0, kc == 15, [ring.b[si], hTo.b[r]], [PBb[0]])
                        ACT([PBb[0]], vB.b, out=vB.t[:, r, :], in_=PB[0][:, 0:64], func=AF.Copy)

                    for s in range(2):
                        for ri in range(2):
                            r = s + ri
                            kind = 1 if ri == 0 else 0
                            if j == 0 and s == 0 and ri == 0:
                                kind = 2
                            for e_ in range(2):
                                mm(PB[2 + e_][:, 0:512].rearrange("p (c q) -> p c q", c=4), kB2T.t[64 * e_:64 * e_ + 64, r * 128:(r + 1) * 128],
                                   QBT.t[64 * e_:64 * e_ + 64, :, s * 128:(s + 1) * 128], True, True, kB2T.b + QBT.b, [PBb[2 + e_]])
                            for e_ in range(2):
                                ACT([PBb[2 + e_]], [Eb.b[e_]], out=Eb.t[:, e_, :], in_=PB[2 + e_][:, 0:512], func=AF.Exp)
                            for e_ in range(2):
                                DVE("tensor_tensor", [Eb.b[e_]] + WB.b, [Ps.b[e_]], out=Ps.t[:, e_, :], in0=Eb.t[:, e_, :], in1=WB.t[:, kind, 512 * e_:512 * (e_ + 1)], op=ALU.mult)
                            for e_ in range(2):
                                mm(PB[4][64 * e_:64 * e_ + 64, 0:512], vB.t[:, r, :], Ps.t[:, e_, :], ri == 0, ri == 1, vB.b + [Ps.b[e_]], [PBb[4]])
                                mm(PB[5][64 * e_:64 * e_ + 64, 0:512], ones_bf.t[:, 0:64], Ps.t[:, e_, :], ri == 0, ri == 1, ones_bf.b + [Ps.b[e_]], [PBb[5]])
                        r4 = rden.t[:, :].rearrange("p (c q) -> p c q", c=4)
                        DVE("tensor_tensor", [PBb[5]] + expsink.b, rden.b, out=r4, in0=PB[5][:, 0:512].rearrange("p (c q) -> p c q", c=4),
                            in1=bc(expsink.t[:, 4 * hp:4 * hp + 4], 128), op=ALU.add)
                        DVE("reciprocal", rden.b, rden.b, out=rden.t[:, :], in_=rden.t[:, :])
                        ch0 = 8 + 4 * hp
                        DVE("tensor_tensor", [PBb[4]] + rden.b, [yab_b(ch0 + c, jl) for c in range(4)],
                            out=yab.t[:, ch0:ch0 + 4, tok0 + s * 128:tok0 + (s + 1) * 128], in0=PB[4][:, 0:512].rearrange("p (c q) -> p c q", c=4), in1=r4, op=ALU.mult)

                    for hl in range(4):
                        hg = 4 * hp + hl
                        for qs in range(2):
                            mm(PB[1][:, 384:400], QAT.t[:, hl, qs * 128:(qs + 1) * 128], kmT.t[:, hl, :], True, True, [QAT.b[hl]] + kmT.b, [PBb[1]])
                            DVE("tensor_tensor", [PBb[1]] + mconst.b, gm.b, out=gm.t[:], in0=PB[1][:, 384:400], in1=mconst.t[:, j, 0, :], op=ALU.min)
                            DVE("max", gm.b, t8.b, out=t8.t[:], in_=gm.t[:])
                            DVE("tensor_scalar", gm.b + t8.b, selt.b, out=selt.t[:], in0=gm.t[:], scalar1=t8.t[:, 2:3], scalar2=None, op0=ALU.is_ge)
                            DVE("tensor_tensor", selt.b + mconst.b, selt.b, out=selt.t[:], in0=selt.t[:], in1=mconst.t[:, j, 1, :], op=ALU.mult)
                            DVE("tensor_tensor", selt.b + mconst.b, selt.b, out=selt.t[:], in0=selt.t[:], in1=mconst.t[:, j, 2, :], op=ALU.add)
                            DVE("tensor_tensor", selt.b + fb.b, m1t.b, out=m1t.t[:], in0=selt.t[:], in1=fb.t[:, j, hg, :], op=ALU.mult)
                            DVE("tensor_scalar", selt.b, selt.b, out=selt.t[:], in0=selt.t[:], scalar1=-NEG, scalar2=NEG, op0=ALU.mult, op1=ALU.add)
                            DVE("tensor_tensor", selt.b + m1t.b, [Mbf.b[qs]], out=Mbf.t[:, qs, 0:16], in0=selt.t[:], in1=m1t.t[:], op=ALU.add)
                            TR([Mbf.b[qs]] + ident.b, [TPb[1]], out=TP[1][:, qs, :], in_=Mbf.t[:, qs, :], identity=ident.t[:])
                        ACT([TPb[1]], [MT.b[hl]], out=MT.t[:, hl, :].rearrange("p (a q) -> p a q", a=2), in_=TP[1][:, 0:2, :], func=AF.Copy)

                    nkb = 2 * j + 2
                    units = [(hl, kb) for hl in range(4) for kb in range(nkb)]

                    def qk(u, ui, kAT=kAT, QAT=QAT, MT=MT):
                        hl, kb = u
                        sbk = 2 + ui % 2
                        for kt in range(2):
                            ktile = kb * 2 + kt
                            mm(PB[sbk][:, kt * 256:(kt + 1) * 256], kAT.t[:, hl, ktile * 128:(ktile + 1) * 128], QAT.t[:, hl, :], True, False,
                               [kAT.b[hl * 8 + ktile // 4], QAT.b[hl]], [PBb[sbk]])
                            mm(PB[sbk][:, kt * 256:(kt + 1) * 256], indoh.t[:, kb, :], MT.t[:, hl, :], False, True, indoh.b + [MT.b[hl]], [PBb[sbk]])

                    def rest(u, ui, j=j, nkb=nkb, hp=hp, jl=jl, tok0=tok0, Pm=Pm, WA=WA, vA=vA, rden=rden):
                        hl, kb = u
                        sbk = 2 + ui % 2
                        pi = ui % 3
                        odb = 4 + hl % 2
                        ACT([PBb[sbk]], [Pm.b[pi]], out=Pm.t[:, pi, :], in_=PB[sbk][:, 0:512], func=AF.Exp)
                        isp = kb - (2 * j - 1)
                        if 0 <= isp <= 2:
                            DVE("tensor_tensor", [Pm.b[pi]] + WA.b, [Pm.b[pi]], out=Pm.t[:, pi, :], in0=Pm.t[:, pi, :], in1=WA.t[:, hl, isp, :], op=ALU.mult)
                        for kt in range(2):
                            ktile = kb * 2 + kt
                            first = (kb == 0 and kt == 0)
                            last = (kb == nkb - 1 and kt == 1)
                            mm(PB[odb][:, 0:256], vA.t[:, ktile, hl * 128:(hl + 1) * 128], Pm.t[:, pi, kt * 256:(kt + 1) * 256], first, last,
                               [vA.b[ktile // 4], Pm.b[pi]], [PBb[odb]])
                            mm(PB[odb][:, 256:512], ones_bf.t[:], Pm.t[:, pi, kt * 256:(kt + 1) * 256], False, last,
                               ones_bf.b + [Pm.b[pi]], [PBb[odb]], skip_group_check=True)
                        if kb == nkb - 1:
                            DVE("reciprocal", [PBb[odb]], rden.b, out=rden.t[:, 0:256], in_=PB[odb][:, 256:512])
                            DVE("tensor_tensor", [PBb[odb]] + rden.b, [yab_b(4 * hp + hl, jl)], out=yab.t[:, 4 * hp + hl, tok0:tok0 + 256], in0=PB[odb][:, 0:256], in1=rden.t[:, 0:256], op=ALU.mult)

                    qk(units[0], 0)
                    for ui, u in enumerate(units):
                        if ui + 1 < len(units):
                            qk(units[ui + 1], ui + 1)
                        rest(u, ui)

            S.barrier()
            C = top.rest()
            x1 = C.alloc("x1", [128, 4, D], F32, nbufs=16)
            aT = C.alloc("aT", [128, 44, 512], BF16, nbufs=44)
            hT = C.alloc("hT", [128, 16, 512], BF16, nbufs=4)
            mT = C.alloc("mT", [128, 16, 512], BF16, nbufs=16)
            xb = C.alloc("xbC", [128, D], BF16)
            ring = C.alloc("ringC", [128, 8 * 2048], BF16, nbufs=8)
            sg = C.alloc("sg", [128, 2, 512], F32, nbufs=2)
            mm1 = C.alloc("mm1", [128, 2, 512], F32, nbufs=2)
            rr = [0]
            pb = [0]

            def rnext():
                i = rr[0] % 8
                rr[0] += 1
                return i

            def pnext():
                i = pb[0] % 6
                pb[0] += 1
                return i

            for tl in range(2):
                t = 2 * th + tl
                tk0 = tl * 512
                for s in range(4):
                    DMA("sp", x1.t[:, s, :], xo_v[t * 4 + s], writes=x1.b[s * 4:(s + 1) * 4])
                    norm_tile(x1.t[:, s, :], x1.b[s * 4:(s + 1) * 4], xb, 0, hT, [hT.b[s]], s * 128)
                for fc in range(16):
                    sa = rnext()
                    DMA("pool", slot128(ring, sa), w_in_v[:, :, C_GA + fc * 128:C_GA + (fc + 1) * 128], writes=[ring.b[sa]])
                    sb_ = rnext()
                    DMA("pool", slot128(ring, sb_), w_in_v[:, :, C_GB + fc * 128:C_GB + (fc + 1) * 128], writes=[ring.b[sb_]])
                    sc = rnext()
                    cview = slot128(ring, sc)
                    DMA("pool", cview[:, 0:8, :], w_ba_v[:, :, fc * 128:(fc + 1) * 128], writes=[ring.b[sc]])
                    DMA("pool", cview[:, 8:16, :], w_bb_v[:, :, fc * 128:(fc + 1) * 128], writes=[ring.b[sc]])
                    pg = [pnext() for _ in range(4)]
                    for br, (si, ch0) in enumerate(((sa, 0), (sb_, 8))):
                        g_b = pg[2 * br]
                        z_b = pg[2 * br + 1]
                        wg = slot128(ring, si)
                        for kc in range(16):
                            mm(PB[g_b][:, :], wg[:, kc, :], hT.t[:, kc, :], kc == 0, kc == 15, [ring.b[si]] + hT.b, [PBb[g_b]])
                        for c in range(8):
                            mm(PB[z_b][:, :], cview[:, 8 * br + c, :], yab.t[:, ch0 + c, tk0:tk0 + 512], c == 0, c == 7,
                               [ring.b[sc], yab_b(ch0 + c, 2 * tl), yab_b(ch0 + c, 2 * tl + 1)], [PBb[z_b]])
                        ACT([PBb[g_b]], [sg.b[br]], out=sg.t[:, br, :], in_=PB[g_b][:, :], func=AF.Sigmoid)
                        DVE("tensor_tensor", [sg.b[br], PBb[z_b]], [mm1.b[br]], out=mm1.t[:, br, :], in0=sg.t[:, br, :], in1=PB[z_b][:, :], op=ALU.mult)
                    DVE("tensor_tensor", mm1.b, [mT.b[fc]], out=mT.t[:, fc, :], in0=mm1.t[:, 0, :], in1=mm1.t[:, 1, :], op=ALU.add)
                for cg in range(4):
                    sl = [rnext() for _ in range(4)]
                    for i, si in enumerate(sl):
                        DMA("pool", slot(ring, si).rearrange("p (c n) -> p c n", n=512), w_out_v[:, 4 * i:4 * i + 4, cg * 512:(cg + 1) * 512], writes=[ring.b[si]])
                    for s in range(4):
                        pi = pnext()
                        for kc in range(16):
                            si = sl[kc // 4]
                            mm(PB[pi][:, :], mT.t[:, kc, s * 128:(s + 1) * 128], slot(ring, si)[:, (kc % 4) * 512:(kc % 4 + 1) * 512], kc == 0, kc == 15, [mT.b[kc], ring.b[si]], [PBb[pi]])
                        xs = x1.t[:, s, cg * 512:(cg + 1) * 512]
                        DVE("tensor_tensor", [x1.b[s * 4 + cg], PBb[pi]], [x1.b[s * 4 + cg]], out=xs, in0=xs, in1=PB[pi][:, :], op=ALU.add)
                for s in range(4):
                    norm_tile(x1.t[:, s, :], x1.b[s * 4:(s + 1) * 4], xb, 16, hT, [hT.b[s]], s * 128)
                for fc in range(DFF // 128):
                    sgi = rnext()
                    DMA("pool", slot128(ring, sgi), w_gu_v[:, :, fc * 128:(fc + 1) * 128], writes=[ring.b[sgi]])
                    sui = rnext()
                    DMA("pool", slot128(ring, sui), w_gu_v[:, :, DFF + fc * 128:DFF + (fc + 1) * 128], writes=[ring.b[sui]])
                    g_b = pnext()
                    u_b = pnext()
                    wg = slot128(ring, sgi)
                    wu = slot128(ring, sui)
                    for kc in range(16):
                        mm(PB[g_b][:, :], wg[:, kc, :], hT.t[:, kc, :], kc == 0, kc == 15, [ring.b[sgi]] + hT.b, [PBb[g_b]])
                    for kc in range(16):
                        mm(PB[u_b][:, :], wu[:, kc, :], hT.t[:, kc, :], kc == 0, kc == 15, [ring.b[sui]] + hT.b, [PBb[u_b]])
                    k = fc % 2
                    ACT([PBb[g_b]], [sg.b[k]], out=sg.t[:, k, :], in_=PB[g_b][:, :], func=AF.Silu)
                    DVE("tensor_tensor", [sg.b[k], PBb[u_b]], [aT.b[fc]], out=aT.t[:, fc, :], in0=sg.t[:, k, :], in1=PB[u_b][:, :], op=ALU.mult)
                for fg in range(DFF // 512):
                    sl = [rnext() for _ in range(4)]
                    for i, si in enumerate(sl):
                        DMA("pool", slot(ring, si), w_dn_v[:, fg * 4 + i, :], writes=[ring.b[si]])
                    for s in range(4):
                        for cg in range(4):
                            pi = pnext()
                            for i, si in enumerate(sl):
                                mm(PB[pi][:, :], aT.t[:, fg * 4 + i, s * 128:(s + 1) * 128], slot(ring, si)[:, cg * 512:(cg + 1) * 512], i == 0, i == 3, [aT.b[fg * 4 + i], ring.b[si]], [PBb[pi]])
                            xs = x1.t[:, s, cg * 512:(cg + 1) * 512]
                            DVE("tensor_tensor", [x1.b[s * 4 + cg], PBb[pi]], [x1.b[s * 4 + cg]], out=xs, in0=xs, in1=PB[pi][:, :], op=ALU.add)
                for s in range(4):
                    DMA("sp", y_v[t * 4 + s], x1.t[:, s, :], reads=x1.b[s * 4:(s + 1) * 4], final=True)

        S.emit(st)
    return nc


def _t5_bucket(dist):
    n = np.maximum(dist, 0).astype(np.int32)
    nf = np.maximum(n, 1).astype(np.float32)
    large = 16 + (np.log(nf / np.float32(16)) / np.float32(math.log(128 / 16)) * np.float32(16)).astype(np.int32)
    large = np.minimum(large, 31)
    return np.where(n < 16, n, large)


def _host_consts(rel_bias, sinks, g):
    ta = rel_bias[:, :8]
    tb = rel_bias[:, 8:]
    negf = np.float32(NEG)
    kk = np.arange(256)[:, None]
    qq = np.arange(256)[None, :]
    d_prev = 256 + qq - kk
    d_own = qq - kk
    bk_prev = _t5_bucket(d_prev)
    bk_own = _t5_bucket(d_own)
    prev = np.stack([ta[bk_prev, h] for h in range(8)], 0)
    own = np.stack([np.where(d_own >= 0, ta[bk_own, h], negf) for h in range(8)], 0)
    zero = np.zeros_like(prev)
    negall = np.full_like(prev, negf)
    kinds = (prev, own, negall) if g == 0 else (zero, prev, own)
    biasA = np.stack(kinds, 0).astype(np.float32)
    biasA = biasA.reshape(3, 8, 2, 128, 256).transpose(0, 1, 3, 2, 4).reshape(3 * 8 * 128, 512)
    k1 = np.arange(128)[:, None]
    q1 = np.arange(128)[None, :]
    do = q1 - k1
    dp = 128 + q1 - k1
    bo = _t5_bucket(do)
    bp = _t5_bucket(dp)
    biasB = np.empty((3, 2, 128, 2, 4, 128), np.float32)
    for hp in range(2):
        for e in range(2):
            for c in range(4):
                h = 8 * hp + 2 * c + e
                o = np.where(do >= 0, tb[bo, h], negf)
                p = np.where(dp < 128, tb[bp, h], negf)
                biasB[0, hp, :, e, c, :] = o
                biasB[1, hp, :, e, c, :] = p
                biasB[2, hp, :, e, c, :] = p if g == 1 else negf
    biasB = biasB.reshape(3 * 2 * 128, 1024)
    mc = np.zeros((8, 4, 16), np.float32)
    for j in range(8):
        ownb = 2 * j + g
        n = np.arange(16)
        mc[j, 0] = np.where(n < ownb, 1e30, -1e30)
        mc[j, 1] = (n < ownb)
        mc[j, 2] = (n == ownb)
        mc[j, 3] = (n < ownb - 1)
    mconst = np.broadcast_to(mc.reshape(1, -1), (128, 8 * 4 * 16)).astype(np.float32).copy()
    return biasA, biasB, mconst


def _prep_inputs(x, norm1_g, w_in, q_norm_a, k_norm_a, q_norm_b, k_norm_b, rel_bias, sinks,
                 w_branch_a, w_branch_b, w_out, norm2_g, w_gate_up, w_down):
    f = lambda a: np.ascontiguousarray(np.asarray(a, dtype=np.float32))
    x = f(x)
    w_in0, w_ba, w_bb, w_o, w_gu, w_dn = f(w_in[0]), f(w_branch_a[0]), f(w_branch_b[0]), f(w_out[0]), f(w_gate_up[0]), f(w_down[0])
    rel_bias = f(rel_bias)
    sinks0 = f(sinks[0])
    vecs_base = np.zeros((128, NVEC), np.float32)
    vecs_base[:, 0:16] = f(norm1_g[0]).reshape(16, 128).T
    vecs_base[:, 16:32] = f(norm2_g[0]).reshape(16, 128).T
    vecs_base[:, 32] = f(q_norm_a[0])
    vecs_base[:, 33] = f(k_norm_a[0])
    vecs_base[:, 34] = np.tile(f(q_norm_b[0]), 2)
    vecs_base[:, 35] = np.tile(f(k_norm_b[0]), 2)
    vecs_base[:, 36:44] = rel_bias[31, 0:8][None, :]
    for hp in range(2):
        for c in range(4):
            vecs_base[0:64, 44 + 4 * hp + c] = sinks0[8 * hp + 2 * c]
            vecs_base[64:128, 44 + 4 * hp + c] = sinks0[8 * hp + 2 * c + 1]
    ident = np.eye(128, dtype=np.float32)
    onesbd = np.ones((128, 256), np.float32)
    bd = np.zeros((128, 128), np.float32)
    bd[:64, :64] = 1.0
    bd[64:, 64:] = 1.0
    onesbd[:, 128:] = bd
    indoh = np.zeros((128, 16, 128), np.float32)
    for n in range(16):
        indoh[n, n, :] = 1.0
    indoh = indoh.reshape(128, 16 * 128)
    consts = {g: _host_consts(rel_bias, sinks0, g) for g in range(2)}
    in_maps = []
    for c in range(8):
        b, g = c // 2, c % 2
        xb = x[b]
        xblk = xb.reshape(NB, LB, D)
        xo = np.ascontiguousarray(xblk[g::2].reshape(2048, D))
        xp = np.zeros((8, 128, D), np.float32)
        for j in range(8):
            ob = 2 * j + g
            if ob > 0:
                xp[j] = xb[ob * LB - 128:ob * LB]
        biasA, biasB, mconst = consts[g]
        in_maps.append({
            "xc": xb, "xo": xo, "xp": xp.reshape(8 * 128, D),
            "w_in": w_in0, "w_ba": w_ba, "w_bb": w_bb, "w_out": w_o, "w_gu": w_gu, "w_dn": w_dn,
            "vecs": vecs_base, "mconst": mconst, "ident": ident, "onesbd": onesbd, "indoh": indoh,
            "biasA": biasA, "biasB": biasB,
        })
    return in_maps


_NC_CACHE = {}


def kernel(**inputs):
    in_maps = _prep_inputs(**inputs)
    if "nc" not in _NC_CACHE:
        _NC_CACHE["nc"] = build_program()
    nc = _NC_CACHE["nc"]
    res = run_bass_kernel_spmd(nc, in_maps, core_ids=list(range(8)))
    out = np.empty((4, SEQ, D), np.float32)
    for c in range(8):
        b, g = c // 2, c % 2
        y = np.asarray(res.results[c]["y"], dtype=np.float32).reshape(8, LB, D)
        out[b].reshape(NB, LB, D)[g::2] = y
    return out
```

```python
import math
import numpy as np
from contextlib import ExitStack

import concourse.bass as bass
import concourse.mybir as mybir
from concourse.bass_utils import run_bass_kernel_spmd

F32 = mybir.dt.float32
BF16 = mybir.dt.bfloat16
U8 = mybir.dt.uint8
AF = mybir.ActivationFunctionType
ALU = mybir.AluOpType

D = 2048
SEQ = 4096
NB = 16
LB = 256
DFF = 5632
NEG = -30000.0
EPS = 1e-6
INW = 8448
C_QA, C_KA, C_VA, C_QB, C_KB, C_VB, C_GA, C_GB = 0, 1024, 2048, 3072, 4096, 4224, 4352, 6400
NVEC = 52

ENGINES = ("pe", "act", "dve", "pool", "sp")
SEM_EPOCH = 20000
NDMA_SEMS = 8


class Buf:
    __slots__ = ("name", "last_w", "readers", "dma_readers")

    def __init__(self, name):
        self.name = name
        self.last_w = None
        self.readers = {}
        self.dma_readers = []


class Op:
    __slots__ = ("eng", "fn", "deps", "is_dma", "ordinal", "sig", "dma_slot", "dma_val")

    def __init__(self, eng, fn, is_dma):
        self.eng = eng
        self.fn = fn
        self.deps = []
        self.is_dma = is_dma
        self.ordinal = None
        self.sig = False
        self.dma_slot = None
        self.dma_val = None


class Sched:
    def __init__(self, nc):
        self.nc = nc
        self.ops = {e: [] for e in ENGINES}
        self.ndma = {e: 0 for e in ENGINES}
        self.final_dmas = []
        self.recent_dmas = {e: [] for e in ENGINES}
        self.pending_bar = {e: None for e in ENGINES}

    def _record(self, op, reads, writes):
        deps = {}

        def add(d):
            if d is None or d is op:
                return
            deps[id(d)] = d

        bar = self.pending_bar[op.eng]
        if bar is not None:
            for d in bar:
                add(d)
            self.pending_bar[op.eng] = None
        for b in reads:
            add(b.last_w)
        for b in writes:
            add(b.last_w)
            for r in b.readers.values():
                add(r)
            for r in b.dma_readers:
                add(r)
        for b in reads:
            if op.is_dma:
                b.dma_readers.append(op)
            else:
                b.readers[op.eng] = op
        for b in writes:
            b.last_w = op
            b.readers = {}
            b.dma_readers = []
        for d in deps.values():
            if (not d.is_dma) and (not op.is_dma) and d.eng == op.eng and op.eng == "pe":
                continue
            op.deps.append(d)
            if not d.is_dma:
                d.sig = True
        self.ops[op.eng].append(op)
        return op

    def op(self, eng, fn, reads=(), writes=()):
        return self._record(Op(eng, fn, False), reads, writes)

    def dma(self, queue, fn, reads=(), writes=(), final=False):
        o = Op(queue, fn, True)
        n = self.ndma[queue]
        self.ndma[queue] = n + 1
        o.dma_slot = n % NDMA_SEMS
        o.dma_val = 16 * (n // NDMA_SEMS + 1)
        self._record(o, reads, writes)
        rd = self.recent_dmas[queue]
        rd.append(o)
        if len(rd) > NDMA_SEMS:
            rd.pop(0)
        if final:
            self.final_dmas.append(o)
        return o

    def barrier(self):
        deps = []
        for e in ENGINES:
            last = None
            for o in reversed(self.ops[e]):
                if not o.is_dma:
                    last = o
                    break
            if last is not None:
                deps.append(last)
            deps.extend(self.recent_dmas[e])
        for e in ENGINES:
            prev = self.pending_bar[e]
            self.pending_bar[e] = list(deps) + (prev or [])

    def emit(self, stack):
        nc = self.nc
        nsig = {}
        for e in ENGINES:
            k = 0
            for o in self.ops[e]:
                if (not o.is_dma) and o.sig:
                    o.ordinal = k
                    k += 1
            nsig[e] = k
        csems = {}
        for e in ENGINES:
            n_ep = (nsig[e] + SEM_EPOCH - 1) // SEM_EPOCH
            csems[e] = [stack.enter_context(nc.semaphore(f"c_{e}_{i}")) for i in range(n_ep)]
        dsems = {}
        for e in ENGINES:
            if self.ndma[e]:
                dsems[e] = [stack.enter_context(nc.semaphore(f"d_{e}_{i}")) for i in range(NDMA_SEMS)]

        def target(d):
            if d.is_dma:
                return dsems[d.eng][d.dma_slot], d.dma_val
            return csems[d.eng][d.ordinal // SEM_EPOCH], d.ordinal % SEM_EPOCH + 1

        block = stack.enter_context(nc.Block())
        engmap = {"pe": "tensor", "act": "scalar", "dve": "vector", "pool": "gpsimd", "sp": "sync"}

        def body(ename, eng):
            waited = {}

            def wait(sem, val):
                key = id(sem)
                if waited.get(key, 0) >= val:
                    return
                waited[key] = val
                eng.wait_ge(sem, val)

            for o in self.ops[ename]:
                if o.is_dma and o.dma_val > 16:
                    wait(dsems[ename][o.dma_slot], o.dma_val - 16)
                for d in o.deps:
                    s, v = target(d)
                    wait(s, v)
                inst = o.fn(eng)
                if o.is_dma:
                    inst.then_inc(dsems[ename][o.dma_slot], 16)
                elif o.sig:
                    inst.then_inc(csems[ename][o.ordinal // SEM_EPOCH], 1)
            if ename == "sp":
                for d in self.final_dmas:
                    s, v = target(d)
                    wait(s, v)

        for ename in ENGINES:
            getattr(block, engmap[ename])(lambda eng, _n=ename: body(_n, eng))


class Tile:
    def __init__(self, t, bufs):
        self.t = t
        self.b = bufs


def build_program(debug=False):
    nc = bass.Bass("TRN2", target_bir_lowering=False)

    def din(name, shape):
        return nc.dram_tensor(name, list(shape), F32, kind="ExternalInput").ap()

    xc = din("xc", [SEQ, D])
    xo = din("xo", [2048, D])
    xp = din("xp", [8 * 128, D])
    w_in = din("w_in", [D, INW])
    w_ba = din("w_ba", [1024, D])
    w_bb = din("w_bb", [1024, D])
    w_out = din("w_out", [D, D])
    w_gu = din("w_gu", [D, 2 * DFF])
    w_dn = din("w_dn", [DFF, D])
    vecs_d = din("vecs", [128, NVEC])
    mconst_d = din("mconst", [128, 8 * 4 * 16])
    ident_d = din("ident", [128, 128])
    ones_d = din("onesbd", [128, 256])
    indoh_d = din("indoh", [128, 16 * 128])
    biasA_d = din("biasA", [3 * 8 * 128, 512])
    biasB_d = din("biasB", [3 * 2 * 128, 1024])
    y_d = nc.dram_tensor("y", [2048, D], F32, kind="ExternalOutput").ap()
    dbg = {}
    if debug:
        dbg["yab"] = nc.dram_tensor("dbg_yab", [128, 16 * 1024], F32, kind="ExternalOutput").ap()

    w_in_v = w_in.rearrange("(c p) n -> p c n", p=128)
    w_ba_v = w_ba.rearrange("(c p) n -> p c n", p=128)
    w_bb_v = w_bb.rearrange("(c p) n -> p c n", p=128)
    w_out_v = w_out.rearrange("(c p) n -> p c n", p=128)
    w_gu_v = w_gu.rearrange("(c p) n -> p c n", p=128)
    w_dn_v = w_dn.rearrange("(c p) n -> p c n", p=128)
    xc_v = xc.rearrange("(t p) d -> t p d", p=128)
    xo_v = xo.rearrange("(t p) d -> t p d", p=128)
    xp_v = xp.rearrange("(t p) d -> t p d", p=128)
    y_v = y_d.rearrange("(t p) d -> t p d", p=128)
    biasA_v = biasA_d.rearrange("(i p) n -> i p n", p=128)
    biasB_v = biasB_d.rearrange("(i p) n -> i p n", p=128)

    st = ExitStack()
    with st:
        S = Sched(nc)
        ARENA = 212800
        st.enter_context(nc.sbuf_tensor("arena", [128, ARENA], U8))
        abase = nc.sbuf_base - ARENA
        uid = [0]

        class Region:
            def __init__(self, lo, hi):
                self.lo, self.hi, self.p = lo, hi, lo

            def alloc(self, name, shape, dt, nbufs=1, parts=None):
                esz = 2 if dt == BF16 else 4
                nbytes = int(np.prod(shape[1:])) * esz
                nbytes = (nbytes + 31) // 32 * 32
                off = self.p
                self.p += nbytes
                assert self.p <= self.hi, (name, self.p, self.hi)
                uid[0] += 1
                t = nc.alloc_sbuf_tensor_at(f"{name}_{uid[0]}", list(shape), dt, offset=abase + off)
                return Tile(t, [Buf(f"{name}{i}") for i in range(nbufs)])

            def rest(self):
                return Region(self.p, self.hi)

        top = Region(0, ARENA)

        PB = [st.enter_context(nc.psum_tensor(f"pb{i}", [128, 512], F32)) for i in range(6)]
        PBb = [Buf(f"pb{i}") for i in range(6)]
        TP = [st.enter_context(nc.psum_tensor(f"tp{i}", [128, 8, 128], BF16)) for i in range(2)]
        TPb = [Buf(f"tp{i}") for i in range(2)]

        vecs = top.alloc("vecs", [128, NVEC], F32)
        mconst = top.alloc("mconst", [128, 8, 4, 16], F32)
        ident = top.alloc("ident", [128, 128], BF16)
        ones_bf = top.alloc("ones_bf", [128, 128], BF16)
        onesf = top.alloc("onesf", [128, 256], F32)
        indoh = top.alloc("indoh", [128, 16, 128], BF16)
        gv = top.alloc("gv", [128, 4], F32)
        expsink = top.alloc("expsink", [128, 8], F32)
        fb = top.alloc("fb", [128, 8, 8, 16], F32)
        ssq = top.alloc("ssq", [128, 4], F32, nbufs=2)
        yab = top.alloc("yab", [128, 16, 1024], BF16, nbufs=16 * 4)

        def yab_b(chunk, blk):
            return yab.b[chunk * 4 + blk]

        def ACT(reads, writes, **kw):
            S.op("act", lambda e: e.activation(**kw), reads, writes)

        def DVE(meth, reads, writes, **kw):
            S.op("dve", lambda e: getattr(e, meth)(**kw), reads, writes)

        def TR(reads, writes, **kw):
            S.op("pe", lambda e: e.transpose(**kw), reads, writes)

        def DMA(q, out, in_, reads=(), writes=(), final=False):
            S.dma(q, lambda e: e.dma_start(out=out, in_=in_), reads, writes, final)

        def mm(out, lhsT, rhs, start, stop, reads, writes, **kw):
            S.op("pe", lambda e: e.matmul(out, lhsT=lhsT, rhs=rhs, start=start, stop=stop, **kw), reads, writes)

        def bc(ap2d, n):
            a = ap2d.shape[1]
            return ap2d.rearrange("p (a b) -> p a b", b=1).broadcast_to([128, a, n])

        DMA("sp", vecs.t[:], vecs_d[:, :], writes=vecs.b)
        DMA("sp", mconst.t[:].rearrange("p a b c -> p (a b c)"), mconst_d[:, :], writes=mconst.b)
        DMA("sp", onesf.t[:], ones_d[:, :], writes=onesf.b)
        DMA("pool", ident.t[:], ident_d[:, :], writes=ident.b)
        DMA("pool", ones_bf.t[:], ones_d[:, 0:128], writes=ones_bf.b)
        DMA("pool", indoh.t[:].rearrange("p a b -> p (a b)"), indoh_d[:, :], writes=indoh.b)
        DVE("tensor_scalar", vecs.b, gv.b, out=gv.t[:, 0:1], in0=vecs.t[:, 32:33], scalar1=128.0 ** -0.5, scalar2=None, op0=ALU.mult)
        DVE("tensor_copy", vecs.b, gv.b, out=gv.t[:, 1:2], in_=vecs.t[:, 33:34])
        DVE("tensor_scalar", vecs.b, gv.b, out=gv.t[:, 2:3], in0=vecs.t[:, 34:35], scalar1=64.0 ** -0.5, scalar2=None, op0=ALU.mult)
        DVE("tensor_copy", vecs.b, gv.b, out=gv.t[:, 3:4], in_=vecs.t[:, 35:36])
        ACT(vecs.b, expsink.b, out=expsink.t[:], in_=vecs.t[:, 44:52], func=AF.Exp)
        for h in range(8):
            DVE("tensor_scalar", vecs.b + mconst.b, fb.b, out=fb.t[:, :, h, :], in0=mconst.t[:, :, 3, :], scalar1=vecs.t[:, 36 + h:37 + h], scalar2=None, op0=ALU.mult)

        tp_rr = [0]
        ssq_rr = [0]

        def norm_front(x_ap, x_bufs, xb_ap, xb_bufs):
            k = ssq_rr[0] % 2
            ssq_rr[0] += 1
            sb = [ssq.b[k]]
            s0 = ssq.t[:, 2 * k:2 * k + 1]
            s1 = ssq.t[:, 2 * k + 1:2 * k + 2]
            ACT(list(x_bufs), list(xb_bufs) + sb, out=xb_ap, in_=x_ap, func=AF.Square, accum_out=s0)
            ACT(sb, sb, out=s1, in_=s0, func=AF.Sqrt, scale=1.0 / D, bias=EPS)
            DVE("reciprocal", sb, sb, out=s1, in_=s1)
            DVE("tensor_scalar", list(x_bufs) + sb, list(xb_bufs), out=xb_ap, in0=x_ap, scalar1=s1, scalar2=None, op0=ALU.mult)

        def norm_back(xb_ap, xb_bufs, gcol0, hT, hT_bufs, tok0):
            for half in range(2):
                ti = tp_rr[0] % 2
                tp_rr[0] += 1
                for kc in range(8):
                    c = half * 8 + kc
                    TR(list(xb_bufs) + ident.b, [TPb[ti]], out=TP[ti][:, kc, :], in_=xb_ap[:, c * 128:(c + 1) * 128], identity=ident.t[:])
                DVE("tensor_tensor", [TPb[ti]] + vecs.b, list(hT_bufs), out=hT.t[:, half * 8:(half + 1) * 8, tok0:tok0 + 128], in0=TP[ti][:],
                    in1=bc(vecs.t[:, gcol0 + half * 8:gcol0 + half * 8 + 8], 128), op=ALU.mult)

        def norm_tile(x_ap, x_bufs, xb, gcol0, hT, hT_bufs, tok0):
            norm_front(x_ap, x_bufs, xb.t[:], xb.b)
            norm_back(xb.t[:], xb.b, gcol0, hT, hT_bufs, tok0)

        def slot(ring, i):
            return ring.t[:, i * 2048:(i + 1) * 2048]

        def slot128(ring, i):
            return slot(ring, i).rearrange("p (c n) -> p c n", n=128)

        class Pipe:
            def __init__(self):
                self.pend = None
                self.bg = []

            def after_x(self, tail):
                if self.pend is not None:
                    self.pend()
                self.pend = tail
                if self.bg:
                    self.bg.pop(0)()

            def flush(self):
                if self.pend is not None:
                    self.pend()
                    self.pend = None

            def drain_bg(self):
                while self.bg:
                    self.bg.pop(0)()

        def qk_tail(pj_ap, pjb, N, sq, rt, ssb, ones_ap, invd, gcol, out_ap, obufs, accs=None):
            def run():
                mm(PB[ssb][:, 0:N], ones_ap, sq.t[:, 0:N], True, True, sq.b + onesf.b, [PBb[ssb]])
                ACT([PBb[ssb]], rt.b, out=rt.t[:, 0:N], in_=PB[ssb][:, 0:N], func=AF.Sqrt, scale=invd, bias=EPS)
                DVE("reciprocal", rt.b, rt.b, out=rt.t[:, 0:N], in_=rt.t[:, 0:N])
                if accs is None:
                    DVE("scalar_tensor_tensor", [pjb] + gv.b + rt.b, list(obufs), out=out_ap, in0=pj_ap, scalar=gv.t[:, gcol:gcol + 1],
                        in1=rt.t[:, 0:N], op0=ALU.mult, op1=ALU.mult)
                else:
                    for (o_ap, c0, c1, acc_ap, abufs) in accs:
                        DVE("scalar_tensor_tensor", [pjb] + gv.b + rt.b, list(obufs) + list(abufs), out=o_ap, in0=pj_ap[:, c0:c1], scalar=gv.t[:, gcol:gcol + 1],
                            in1=rt.t[:, c0:c1], op0=ALU.mult, op1=ALU.mult, accum_out=acc_ap)
            return run

        for th in range(2):
            for hp in range(2):
                S.barrier()
                hpR = top.rest()
                kAT = hpR.alloc("kAT", [128, 4, SEQ], BF16, nbufs=4 * 8)
                vA = hpR.alloc("vA", [128, 32, 512], BF16, nbufs=8)
                ksum = hpR.alloc("ksum", [128, 4, 16], F32)
                kmT = hpR.alloc("kmT", [128, 4, 16], BF16)
                A = hpR.rest()
                ring = A.alloc("ringA", [128, 8 * 2048], BF16, nbufs=8)
                xt = A.alloc("xtA", [128, 2, D], F32, nbufs=2)
                xb2 = A.alloc("xbA", [128, 2, D], BF16, nbufs=2)
                hTc2 = [A.alloc(f"hTc{i}", [128, 16, 512], BF16, nbufs=4) for i in range(2)]
                sq2 = [A.alloc(f"sqA{i}", [128, 512], F32) for i in range(2)]
                rt2 = [A.alloc(f"rtA{i}", [128, 512], F32) for i in range(2)]
                DVE("memset", [], ksum.b, ap=ksum.t[:], constant=0.0)
                for hl in range(4):
                    c0 = C_KA + (4 * hp + hl) * 128
                    DMA("pool", slot128(ring, hl), w_in_v[:, :, c0:c0 + 128], writes=[ring.b[hl]])
                c0 = C_VA + hp * 512
                wv_view = ring.t[:, 4 * 2048:8 * 2048].rearrange("p (c n) -> p c n", n=512)
                DMA("pool", wv_view, w_in_v[:, :, c0:c0 + 512], writes=ring.b[4:8])
                n_ct = 4 if th == 0 else 8
                xcnt = [0]

                def a_tasks(ct, xt=xt, xb2=xb2, hTc2=hTc2, xcnt=xcnt):
                    ks = []

                    def F(s):
                        k = xcnt[0] % 2
                        xcnt[0] += 1
                        ks.append(k)
                        DMA("sp", xt.t[:, k, :], xc_v[ct * 4 + s], writes=[xt.b[k]])
                        norm_front(xt.t[:, k, :], [xt.b[k]], xb2.t[:, k, :], [xb2.b[k]])

                    def Bk(s):
                        k = ks[s]
                        hT = hTc2[ct % 2]
                        norm_back(xb2.t[:, k, :], [xb2.b[k]], 0, hT, [hT.b[s]], s * 128)

                    return [lambda: F(0), lambda: (F(1), Bk(0)), lambda: (F(2), Bk(1)), lambda: (F(3), Bk(2)), lambda: Bk(3)]

                pipe = Pipe()
                pipe.bg = a_tasks(0)
                pipe.drain_bg()
                pj_rr = 0
                pv_rr = 0
                yi = 0
                for ct in range(n_ct):
                    hTc = hTc2[ct % 2]
                    if ct + 1 < n_ct:
                        pipe.bg = a_tasks(ct + 1)
                    for hl in range(4):
                        pi = pj_rr % 2
                        pj_rr += 1
                        wk = slot128(ring, hl)
                        for kc in range(16):
                            mm(PB[pi][:, 0:512], wk[:, kc, :], hTc.t[:, kc, :], kc == 0, kc == 15, [ring.b[hl]] + hTc.b, [PBb[pi]])
                        sq, rt = sq2[yi % 2], rt2[yi % 2]
                        ssb = (2, 5)[yi % 2]
                        yi += 1
                        ACT([PBb[pi]], sq.b, out=sq.t[:, 0:512], in_=PB[pi][:, 0:512], func=AF.Square)
                        accs = []
                        for bb in range(2):
                            blk = ct * 2 + bb
                            accs.append((kAT.t[:, hl, ct * 512 + bb * 256:ct * 512 + (bb + 1) * 256], bb * 256, (bb + 1) * 256, ksum.t[:, hl, blk:blk + 1], ksum.b))
                        pipe.after_x(qk_tail(PB[pi], PBb[pi], 512, sq, rt, ssb, onesf.t[:, 0:128], 1.0 / 128, 1, None, [kAT.b[hl * 8 + ct]], accs))
                    for s in range(4):
                        pi = 3 + pv_rr % 2
                        pv_rr += 1
                        for kc in range(16):
                            mm(PB[pi][:, 0:512], hTc.t[:, kc, s * 128:(s + 1) * 128], wv_view[:, kc, :], kc == 0, kc == 15, [hTc.b[s]] + ring.b[4:8], [PBb[pi]])
                        ACT([PBb[pi]], [vA.b[ct]], out=vA.t[:, ct * 4 + s, :], in_=PB[pi][:, 0:512], func=AF.Copy)
                        if s == 0:
                            pipe.after_x(None)
                        elif pipe.bg:
                            pipe.bg.pop(0)()
                    pipe.drain_bg()
                pipe.flush()
                DVE("tensor_scalar", ksum.b, kmT.b, out=kmT.t[:], in0=ksum.t[:], scalar1=1.0 / 256, scalar2=None, op0=ALU.mult)

                S.barrier()
                Bp = hpR.rest()
                WA = Bp.alloc("WA", [128, 4, 3, 512], BF16)
                WB = Bp.alloc("WB", [128, 3, 1024], BF16)
                QAT = Bp.alloc("QAT", [128, 2, 4, 256], BF16, nbufs=4)
                QBT = Bp.alloc("QBT", [128, 2, 4, 256], BF16, nbufs=4)
                kB2T = Bp.alloc("kB2T", [128, 2, 384], BF16, nbufs=2)
                vB = Bp.alloc("vB", [128, 2, 3, 64], BF16, nbufs=2)
                MT = Bp.alloc("MT", [128, 2, 4, 256], BF16, nbufs=2)
                B1 = Bp.rest()
                ring = B1.alloc("ringB", [128, 4 * 2048], BF16, nbufs=4)
                xt = B1.alloc("xtB", [128, D], F32)
                xb2 = B1.alloc("xbB", [128, 2, D], BF16, nbufs=2)
                hTo2 = [B1.alloc(f"hTo{i}", [128, 16, 384], BF16, nbufs=3) for i in range(2)]
                sq2 = [B1.alloc(f"sqB{i}", [128, 512], F32) for i in range(2)]
                rtB = B1.alloc("rtB", [128, 512], F32)
                rt2 = [rtB, rtB]
                Mbf = B1.alloc("Mbf", [128, 8, 128], BF16)
                gm = B1.alloc("gm", [128, 8, 16], F32)
                t8 = B1.alloc("t8", [128, 8, 8], F32)
                selt = B1.alloc("selt", [128, 8, 16], F32)
                m1t = B1.alloc("m1t", [128, 8, 16], F32)
                B2 = Bp.rest()
                Pm = B2.alloc("Pm", [128, 3, 512], BF16, nbufs=3)
                Eb = B2.alloc("Eb", [128, 2, 2, 512], BF16, nbufs=4)
                Ps = B2.alloc("Ps", [128, 2, 2, 512], BF16, nbufs=4)
                rden = B2.alloc("rden", [128, 2, 512], F32, nbufs=2)
                for hl in range(4):
                    for i in range(3):
                        stg = xt.t[:, 0:512]
                        DMA("sp", stg, biasA_v[i * 8 + 4 * hp + hl, :, :], writes=xt.b)
                        ACT(xt.b, WA.b, out=WA.t[:, hl, i, :], in_=stg, func=AF.Exp)
                for kind in range(3):
                    stg = xt.t[:, 0:1024]
                    DMA("sp", stg, biasB_v[kind * 2 + hp, :, :], writes=xt.b)
                    ACT(xt.b, WB.b, out=WB.t[:, kind, :], in_=stg, func=AF.Exp)
                DVE("memset", [], Mbf.b, ap=Mbf.t[:], constant=0.0)
                ring_rr = [0]
                xcnt = [0]

                def ring_next(ring_rr=ring_rr):
                    i = ring_rr[0] % 4
                    ring_rr[0] += 1
                    return i

                for rnd in range(2):
                    def b_tasks(bi, rnd=rnd, xt=xt, xb2=xb2, hTo2=hTo2, xcnt=xcnt):
                        j = 4 * th + 2 * rnd + bi
                        ks = []

                        def F(r):
                            k = xcnt[0] % 2
                            xcnt[0] += 1
                            ks.append(k)
                            src_ap = xp_v[j] if r == 0 else xo_v[j * 2 + r - 1]
                            DMA("sp", xt.t[:], src_ap, writes=xt.b)
                            norm_front(xt.t[:], xt.b, xb2.t[:, k, :], [xb2.b[k]])

                        def Bk(r):
                            k = ks[r]
                            hT = hTo2[bi]
                            norm_back(xb2.t[:, k, :], [xb2.b[k]], 0, hT, [hT.b[r]], r * 128)

                        return [lambda: F(0), lambda: (F(1), Bk(0)), lambda: (F(2), Bk(1)), lambda: Bk(2)]

                    pipe = Pipe()
                    pipe.bg = b_tasks(0)
                    pipe.drain_bg()
                    pj_rr = 0
                    yi = 0
                    pend_gate = None
                    for bi in range(2):
                        j = 4 * th + 2 * rnd + bi
                        hTo = hTo2[bi]
                        if bi == 0:
                            pipe.bg = b_tasks(1)
                        projs = [("qa", 0), ("qa", 1), ("qb", 0), ("qb", 1), ("kb", 0), ("vb", 0)]
                        for pidx, (kind, p) in enumerate(projs):
                            pi = pj_rr % 3
                            pj_rr += 1
                            if kind in ("qa", "qb"):
                                for hh in range(2):
                                    si = ring_next()
                                    if kind == "qa":
                                        c0 = C_QA + (4 * hp + 2 * p + hh) * 128
                                    else:
                                        c0 = C_QB + 512 * hp + 128 * (2 * p + hh)
                                    wq = slot128(ring, si)
                                    DMA("pool", wq, w_in_v[:, :, c0:c0 + 128], writes=[ring.b[si]])
                                    for kc in range(16):
                                        mm(PB[pi][:, hh * 256:(hh + 1) * 256], wq[:, kc, :], hTo.t[:, kc, 128:384], kc == 0, kc == 15, [ring.b[si]] + hTo.b[1:3], [PBb[pi]])
                                sq, rt = sq2[yi % 2], rt2[yi % 2]
                                ssb = (3, 4)[yi % 2]
                                yi += 1
                                ACT([PBb[pi]], sq.b, out=sq.t[:, 0:512], in_=PB[pi][:, 0:512], func=AF.Square)
                                if kind == "qa":
                                    tail = qk_tail(PB[pi][:, 0:512], PBb[pi], 512, sq, rt, ssb, onesf.t[:, 0:128], 1.0 / 128, 0,
                                                   QAT.t[:, bi, 2 * p:2 * p + 2, :].rearrange("p a q -> p (a q)"), [QAT.b[bi * 2 + p]])
                                else:
                                    tail = qk_tail(PB[pi][:, 0:512], PBb[pi], 512, sq, rt, ssb, onesf.t[:, 128:256], 1.0 / 64, 2,
                                                   QBT.t[:, bi, 2 * p:2 * p + 2, :].rearrange("p a q -> p (a q)"), [QBT.b[bi * 2 + p]])
                                pipe.after_x(tail)
                            elif kind == "kb":
                                si = ring_next()
                                c0 = C_KB + 64 * hp
                                kview = slot128(ring, si)
                                DMA("pool", kview[:, :, 0:64], w_in_v[:, :, c0:c0 + 64], writes=[ring.b[si]])
                                DMA("pool", kview[:, :, 64:128], w_in_v[:, :, c0:c0 + 64], writes=[ring.b[si]])
                                for kc in range(16):
                                    mm(PB[pi][:, 0:384], kview[:, kc, :], hTo.t[:, kc, 0:384], kc == 0, kc == 15, [ring.b[si]] + hTo.b, [PBb[pi]])
                                sq, rt = sq2[yi % 2], rt2[yi % 2]
                                ssb = (3, 4)[yi % 2]
                                yi += 1
                                ACT([PBb[pi]], sq.b, out=sq.t[:, 0:384], in_=PB[pi][:, 0:384], func=AF.Square)
                                pipe.after_x(qk_tail(PB[pi][:, 0:384], PBb[pi], 384, sq, rt, ssb, onesf.t[:, 128:256], 1.0 / 64, 3, kB2T.t[:, bi, :], [kB2T.b[bi]]))
                            else:
                                si = ring_next()
                                c0 = C_VB + 64 * hp
                                vview = slot(ring, si)[:, 0:1024].rearrange("p (c n) -> p c n", n=64)
                                DMA("pool", vview, w_in_v[:, :, c0:c0 + 64], writes=[ring.b[si]])
                                for r in range(3):
                                    for kc in range(16):
                                        mm(PB[pi][:, r * 64:(r + 1) * 64], hTo.t[:, kc, r * 128:(r + 1) * 128], vview[:, kc, :], kc == 0, kc == 15, [ring.b[si], hTo.b[r]], [PBb[pi]])
                                ACT([PBb[pi]], [vB.b[bi]], out=vB.t[:, bi, :, :].rearrange("p r d -> p (r d)"), in_=PB[pi][:, 0:192], func=AF.Copy)
                                pipe.after_x(None)
                            if pidx == 1 and pend_gate is not None:
                                pend_gate()
                                pend_gate = None
                        pipe.flush()
                        for hl in range(4):
                            for qs in range(2):
                                i = hl * 2 + qs
                                mm(PB[5][:, i * 16:(i + 1) * 16], QAT.t[:, bi, hl, qs * 128:(qs + 1) * 128], kmT.t[:, hl, :], True, True, [QAT.b[bi * 2 + hl // 2]] + kmT.b, [PBb[5]])
                        g3 = PB[5][:, 0:128].rearrange("p (i n) -> p i n", n=16)
                        DVE("tensor_tensor", [PBb[5]] + mconst.b, gm.b, out=gm.t[:], in0=g3, in1=mconst.t[:, j, 0:1, :].broadcast_to([128, 8, 16]), op=ALU.min)
                        for i in range(8):
                            DVE("max", gm.b, t8.b, out=t8.t[:, i, :], in_=gm.t[:, i, :])
                        DVE("tensor_tensor", gm.b + t8.b, selt.b, out=selt.t[:], in0=gm.t[:], in1=t8.t[:, :, 2:3].broadcast_to([128, 8, 16]), op=ALU.is_ge)
                        DVE("tensor_tensor", selt.b + mconst.b, selt.b, out=selt.t[:], in0=selt.t[:], in1=mconst.t[:, j, 1:2, :].broadcast_to([128, 8, 16]), op=ALU.mult)
                        DVE("tensor_tensor", selt.b + mconst.b, selt.b, out=selt.t[:], in0=selt.t[:], in1=mconst.t[:, j, 2:3, :].broadcast_to([128, 8, 16]), op=ALU.add)
                        fb4 = fb.t[:, j, 4 * hp:4 * hp + 4, :].rearrange("p h (o n) -> p h o n", o=1).broadcast_to([128, 4, 2, 16])
                        DVE("tensor_tensor", selt.b + fb.b, m1t.b, out=m1t.t[:].rearrange("p (h o) n -> p h o n", o=2), in0=selt.t[:].rearrange("p (h o) n -> p h o n", o=2), in1=fb4, op=ALU.mult)
                        DVE("tensor_scalar", selt.b, selt.b, out=selt.t[:], in0=selt.t[:], scalar1=-NEG, scalar2=NEG, op0=ALU.mult, op1=ALU.add)
                        DVE("tensor_tensor", selt.b + m1t.b, Mbf.b, out=Mbf.t[:, :, 0:16], in0=selt.t[:], in1=m1t.t[:], op=ALU.add)

                        def gate_back(bi=bi, Mbf=Mbf, MT=MT):
                            for i in range(8):
                                TR(Mbf.b + ident.b, [TPb[1]], out=TP[1][:, i, :], in_=Mbf.t[:, i, :], identity=ident.t[:])
                            ACT([TPb[1]], [MT.b[bi]], out=MT.t[:, bi, :, :].rearrange("p h q -> p (h q)"), in_=TP[1][:].rearrange("p i q -> p (i q)"), func=AF.Copy)

                        if bi == 0:
                            pend_gate = gate_back
                        else:
                            gate_back()
                    pipe.drain_bg()
                    if pend_gate is not None:
                        pend_gate()
                        pend_gate = None

                    S.barrier()
                    units = []
                    s_rr = [0]
                    p_rr = [0]
                    for bi in range(2):
                        jl = 2 * rnd + bi
                        j = 4 * th + jl
                        tok0 = jl * 256
                        for s in range(2):
                            for ri in range(2):
                                def mk_swa(bi=bi, j=j, jl=jl, tok0=tok0, s=s, ri=ri):
                                    r = s + ri
                                    kind = 1 if ri == 0 else 0
                                    if j == 0 and s == 0 and ri == 0:
                                        kind = 2
                                    it = s * 2 + ri
                                    sb0 = (0, 2)[it % 2]
                                    ek = it % 2

                                    def qk():
                                        for e_ in range(2):
                                            mm(PB[sb0 + e_][:, 0:512].rearrange("p (c q) -> p c q", c=4), kB2T.t[64 * e_:64 * e_ + 64, bi, r * 128:(r + 1) * 128],
                                               QBT.t[64 * e_:64 * e_ + 64, bi, :, s * 128:(s + 1) * 128], True, True, [kB2T.b[bi]] + QBT.b[bi * 2:bi * 2 + 2], [PBb[sb0 + e_]])

                                    def rest():
                                        for e_ in range(2):
                                            ACT([PBb[sb0 + e_]], [Eb.b[ek * 2 + e_]], out=Eb.t[:, ek, e_, :], in_=PB[sb0 + e_][:, 0:512], func=AF.Exp)
                                        for e_ in range(2):
                                            DVE("tensor_tensor", [Eb.b[ek * 2 + e_]] + WB.b, [Ps.b[ek * 2 + e_]], out=Ps.t[:, ek, e_, :], in0=Eb.t[:, ek, e_, :], in1=WB.t[:, kind, 512 * e_:512 * (e_ + 1)], op=ALU.mult)
                                        for e_ in range(2):
                                            mm(PB[4][64 * e_:64 * e_ + 64, 0:512], vB.t[:, bi, r, :], Ps.t[:, ek, e_, :], ri == 0, ri == 1, [vB.b[bi], Ps.b[ek * 2 + e_]], [PBb[4]])
                                            mm(PB[5][64 * e_:64 * e_ + 64, 0:512], ones_bf.t[:, 0:64], Ps.t[:, ek, e_, :], ri == 0, ri == 1, ones_bf.b + [Ps.b[ek * 2 + e_]], [PBb[5]])
                                        if ri == 1:
                                            rk = s % 2
                                            r4 = rden.t[:, rk, :].rearrange("p (c q) -> p c q", c=4)
                                            DVE("tensor_tensor", [PBb[5]] + expsink.b, [rden.b[rk]], out=r4, in0=PB[5][:, 0:512].rearrange("p (c q) -> p c q", c=4),
                                                in1=bc(expsink.t[:, 4 * hp:4 * hp + 4], 128), op=ALU.add)
                                            DVE("reciprocal", [rden.b[rk]], [rden.b[rk]], out=rden.t[:, rk, :], in_=rden.t[:, rk, :])
                                            ch0 = 8 + 4 * hp
                                            DVE("tensor_tensor", [PBb[4], rden.b[rk]], [yab_b(ch0 + c, jl) for c in range(4)],
                                                out=yab.t[:, ch0:ch0 + 4, tok0 + s * 128:tok0 + (s + 1) * 128], in0=PB[4][:, 0:512].rearrange("p (c q) -> p c q", c=4), in1=r4, op=ALU.mult)
                                    return qk, rest
                                units.append(mk_swa())
                        nkb = 2 * j + 2
                        for hl in range(4):
                            for kb in range(nkb):
                                def mk_moba(bi=bi, j=j, jl=jl, tok0=tok0, hl=hl, kb=kb, nkb=nkb):
                                    sbk = s_rr[0] % 4
                                    s_rr[0] += 1
                                    pi = p_rr[0] % 3
                                    p_rr[0] += 1
                                    odb = 4 + hl % 2

                                    def qk():
                                        for kt in range(2):
                                            ktile = kb * 2 + kt
                                            mm(PB[sbk][:, kt * 256:(kt + 1) * 256], kAT.t[:, hl, ktile * 128:(ktile + 1) * 128], QAT.t[:, bi, hl, :], True, False,
                                               [kAT.b[hl * 8 + ktile // 4], QAT.b[bi * 2 + hl // 2]], [PBb[sbk]])
                                            mm(PB[sbk][:, kt * 256:(kt + 1) * 256], indoh.t[:, kb, :], MT.t[:, bi, hl, :], False, True, indoh.b + [MT.b[bi]], [PBb[sbk]])

                                    def rest():
                                        ACT([PBb[sbk]], [Pm.b[pi]], out=Pm.t[:, pi, :], in_=PB[sbk][:, 0:512], func=AF.Exp)
                                        isp = kb - (2 * j - 1)
                                        if 0 <= isp <= 2:
                                            DVE("tensor_tensor", [Pm.b[pi]] + WA.b, [Pm.b[pi]], out=Pm.t[:, pi, :], in0=Pm.t[:, pi, :], in1=WA.t[:, hl, isp, :], op=ALU.mult)
                                        for kt in range(2):
                                            ktile = kb * 2 + kt
                                            first = (kb == 0 and kt == 0)
                                            last = (kb == nkb - 1 and kt == 1)
                                            mm(PB[odb][:, 0:256], vA.t[:, ktile, hl * 128:(hl + 1) * 128], Pm.t[:, pi, kt * 256:(kt + 1) * 256], first, last,
                                               [vA.b[ktile // 4], Pm.b[pi]], [PBb[odb]])
                                            mm(PB[odb][:, 256:512], ones_bf.t[:], Pm.t[:, pi, kt * 256:(kt + 1) * 256], False, last,
                                               ones_bf.b + [Pm.b[pi]], [PBb[odb]], skip_group_check=True)
                                        if kb == nkb - 1:
                                            rk = hl % 2
                                            DVE("reciprocal", [PBb[odb]], [rden.b[rk]], out=rden.t[:, rk, 0:256], in_=PB[odb][:, 256:512])
                                            DVE("tensor_tensor", [PBb[odb], rden.b[rk]], [yab_b(4 * hp + hl, jl)], out=yab.t[:, 4 * hp + hl, tok0:tok0 + 256], in0=PB[odb][:, 0:256], in1=rden.t[:, rk, 0:256], op=ALU.mult)
                                    return qk, rest
                                units.append(mk_moba())
                    units[0][0]()
                    for ui in range(len(units)):
                        if ui + 1 < len(units):
                            units[ui + 1][0]()
                        units[ui][1]()
                    if rnd == 0:
                        S.barrier()

            S.barrier()
            C = top.rest()
            x1 = C.alloc("x1", [128, 4, D], F32, nbufs=16)
            aT = C.alloc("aT", [128, 44, 512], BF16, nbufs=44)
            hT = C.alloc("hT", [128, 16, 512], BF16, nbufs=4)
            mT = C.alloc("mT", [128, 16, 512], BF16, nbufs=16)
            xb = C.alloc("xbC", [128, D], BF16)
            ring = C.alloc("ringC", [128, 8 * 2048], BF16, nbufs=8)
            sg = C.alloc("sg", [128, 2, 512], F32, nbufs=2)
            mm1 = C.alloc("mm1", [128, 2, 512], F32, nbufs=2)
            rr = [0]
            pb = [0]

            def rnext():
                i = rr[0] % 8
                rr[0] += 1
                return i

            def pnext():
                i = pb[0] % 6
                pb[0] += 1
                return i

            for tl in range(2):
                t = 2 * th + tl
                tk0 = tl * 512
                for s in range(4):
                    DMA("sp", x1.t[:, s, :], xo_v[t * 4 + s], writes=x1.b[s * 4:(s + 1) * 4])
                    norm_tile(x1.t[:, s, :], x1.b[s * 4:(s + 1) * 4], xb, 0, hT, [hT.b[s]], s * 128)
                for fc in range(16):
                    sa = rnext()
                    DMA("pool", slot128(ring, sa), w_in_v[:, :, C_GA + fc * 128:C_GA + (fc + 1) * 128], writes=[ring.b[sa]])
                    sb_ = rnext()
                    DMA("pool", slot128(ring, sb_), w_in_v[:, :, C_GB + fc * 128:C_GB + (fc + 1) * 128], writes=[ring.b[sb_]])
                    sc = rnext()
                    cview = slot128(ring, sc)
                    DMA("pool", cview[:, 0:8, :], w_ba_v[:, :, fc * 128:(fc + 1) * 128], writes=[ring.b[sc]])
                    DMA("pool", cview[:, 8:16, :], w_bb_v[:, :, fc * 128:(fc + 1) * 128], writes=[ring.b[sc]])
                    pg = [pnext() for _ in range(4)]
                    for br, (si, ch0) in enumerate(((sa, 0), (sb_, 8))):
                        g_b = pg[2 * br]
                        z_b = pg[2 * br + 1]
                        wg = slot128(ring, si)
                        for kc in range(16):
                            mm(PB[g_b][:, :], wg[:, kc, :], hT.t[:, kc, :], kc == 0, kc == 15, [ring.b[si]] + hT.b, [PBb[g_b]])
                        for c in range(8):
                            mm(PB[z_b][:, :], cview[:, 8 * br + c, :], yab.t[:, ch0 + c, tk0:tk0 + 512], c == 0, c == 7,
                               [ring.b[sc], yab_b(ch0 + c, 2 * tl), yab_b(ch0 + c, 2 * tl + 1)], [PBb[z_b]])
                        ACT([PBb[g_b]], [sg.b[br]], out=sg.t[:, br, :], in_=PB[g_b][:, :], func=AF.Sigmoid)
                        DVE("tensor_tensor", [sg.b[br], PBb[z_b]], [mm1.b[br]], out=mm1.t[:, br, :], in0=sg.t[:, br, :], in1=PB[z_b][:, :], op=ALU.mult)
                    DVE("tensor_tensor", mm1.b, [mT.b[fc]], out=mT.t[:, fc, :], in0=mm1.t[:, 0, :], in1=mm1.t[:, 1, :], op=ALU.add)
                for cg in range(4):
                    sl = [rnext() for _ in range(4)]
                    for i, si in enumerate(sl):
                        DMA("pool", slot(ring, si).rearrange("p (c n) -> p c n", n=512), w_out_v[:, 4 * i:4 * i + 4, cg * 512:(cg + 1) * 512], writes=[ring.b[si]])
                    for s in range(4):
                        pi = pnext()
                        for kc in range(16):
                            si = sl[kc // 4]
                            mm(PB[pi][:, :], mT.t[:, kc, s * 128:(s + 1) * 128], slot(ring, si)[:, (kc % 4) * 512:(kc % 4 + 1) * 512], kc == 0, kc == 15, [mT.b[kc], ring.b[si]], [PBb[pi]])
                        xs = x1.t[:, s, cg * 512:(cg + 1) * 512]
                        DVE("tensor_tensor", [x1.b[s * 4 + cg], PBb[pi]], [x1.b[s * 4 + cg]], out=xs, in0=xs, in1=PB[pi][:, :], op=ALU.add)
                for s in range(4):
                    norm_tile(x1.t[:, s, :], x1.b[s * 4:(s + 1) * 4], xb, 16, hT, [hT.b[s]], s * 128)
                for fc in range(DFF // 128):
                    sgi = rnext()
                    DMA("pool", slot128(ring, sgi), w_gu_v[:, :, fc * 128:(fc + 1) * 128], writes=[ring.b[sgi]])
                    sui = rnext()
                    DMA("pool", slot128(ring, sui), w_gu_v[:, :, DFF + fc * 128:DFF + (fc + 1) * 128], writes=[ring.b[sui]])
                    g_b = pnext()
                    u_b = pnext()
                    wg = slot128(ring, sgi)
                    wu = slot128(ring, sui)
                    for kc in range(16):
                        mm(PB[g_b][:, :], wg[:, kc, :], hT.t[:, kc, :], kc == 0, kc == 15, [ring.b[sgi]] + hT.b, [PBb[g_b]])
                    for kc in range(16):
                        mm(PB[u_b][:, :], wu[:, kc, :], hT.t[:, kc, :], kc == 0, kc == 15, [ring.b[sui]] + hT.b, [PBb[u_b]])
                    k = fc % 2
                    ACT([PBb[g_b]], [sg.b[k]], out=sg.t[:, k, :], in_=PB[g_b][:, :], func=AF.Silu)
                    DVE("tensor_tensor", [sg.b[k], PBb[u_b]], [aT.b[fc]], out=aT.t[:, fc, :], in0=sg.t[:, k, :], in1=PB[u_b][:, :], op=ALU.mult)
                for fg in range(DFF // 512):
                    sl = [rnext() for _ in range(4)]
                    for i, si in enumerate(sl):
                        DMA("pool", slot(ring, si), w_dn_v[:, fg * 4 + i, :], writes=[ring.b[si]])
                    for s in range(4):
                        for cg in range(4):
                            pi = pnext()
                            for i, si in enumerate(sl):
                                mm(PB[pi][:, :], aT.t[:, fg * 4 + i, s * 128:(s + 1) * 128], slot(ring, si)[:, cg * 512:(cg + 1) * 512], i == 0, i == 3, [aT.b[fg * 4 + i], ring.b[si]], [PBb[pi]])
                            xs = x1.t[:, s, cg * 512:(cg + 1) * 512]
                            DVE("tensor_tensor", [x1.b[s * 4 + cg], PBb[pi]], [x1.b[s * 4 + cg]], out=xs, in0=xs, in1=PB[pi][:, :], op=ALU.add)
                for s in range(4):
                    DMA("sp", y_v[t * 4 + s], x1.t[:, s, :], reads=x1.b[s * 4:(s + 1) * 4], final=True)

        S.emit(st)
    return nc


def _t5_bucket(dist):
    n = np.maximum(dist, 0).astype(np.int32)
    nf = np.maximum(n, 1).astype(np.float32)
    large = 16 + (np.log(nf / np.float32(16)) / np.float32(math.log(128 / 16)) * np.float32(16)).astype(np.int32)
    large = np.minimum(large, 31)
    return np.where(n < 16, n, large)


def _host_consts(rel_bias, sinks, g):
    ta = rel_bias[:, :8]
    tb = rel_bias[:, 8:]
    negf = np.float32(NEG)
    kk = np.arange(256)[:, None]
    qq = np.arange(256)[None, :]
    d_prev = 256 + qq - kk
    d_own = qq - kk
    bk_prev = _t5_bucket(d_prev)
    bk_own = _t5_bucket(d_own)
    prev = np.stack([ta[bk_prev, h] for h in range(8)], 0)
    own = np.stack([np.where(d_own >= 0, ta[bk_own, h], negf) for h in range(8)], 0)
    zero = np.zeros_like(prev)
    negall = np.full_like(prev, negf)
    kinds = (prev, own, negall) if g == 0 else (zero, prev, own)
    biasA = np.stack(kinds, 0).astype(np.float32)
    biasA = biasA.reshape(3, 8, 2, 128, 256).transpose(0, 1, 3, 2, 4).reshape(3 * 8 * 128, 512)
    k1 = np.arange(128)[:, None]
    q1 = np.arange(128)[None, :]
    do = q1 - k1
    dp = 128 + q1 - k1
    bo = _t5_bucket(do)
    bp = _t5_bucket(dp)
    biasB = np.empty((3, 2, 128, 2, 4, 128), np.float32)
    for hp in range(2):
        for e in range(2):
            for c in range(4):
                h = 8 * hp + 2 * c + e
                o = np.where(do >= 0, tb[bo, h], negf)
                p = np.where(dp < 128, tb[bp, h], negf)
                biasB[0, hp, :, e, c, :] = o
                biasB[1, hp, :, e, c, :] = p
                biasB[2, hp, :, e, c, :] = p if g == 1 else negf
    biasB = biasB.reshape(3 * 2 * 128, 1024)
    mc = np.zeros((8, 4, 16), np.float32)
    for j in range(8):
        ownb = 2 * j + g
        n = np.arange(16)
        mc[j, 0] = np.where(n < ownb, 1e30, -1e30)
        mc[j, 1] = (n < ownb)
        mc[j, 2] = (n == ownb)
        mc[j, 3] = (n < ownb - 1)
    mconst = np.broadcast_to(mc.reshape(1, -1), (128, 8 * 4 * 16)).astype(np.float32).copy()
    return biasA, biasB, mconst


def _prep_inputs(x, norm1_g, w_in, q_norm_a, k_norm_a, q_norm_b, k_norm_b, rel_bias, sinks,
                 w_branch_a, w_branch_b, w_out, norm2_g, w_gate_up, w_down):
    f = lambda a: np.ascontiguousarray(np.asarray(a, dtype=np.float32))
    x = f(x)
    w_in0, w_ba, w_bb, w_o, w_gu, w_dn = f(w_in[0]), f(w_branch_a[0]), f(w_branch_b[0]), f(w_out[0]), f(w_gate_up[0]), f(w_down[0])
    rel_bias = f(rel_bias)
    sinks0 = f(sinks[0])
    vecs_base = np.zeros((128, NVEC), np.float32)
    vecs_base[:, 0:16] = f(norm1_g[0]).reshape(16, 128).T
    vecs_base[:, 16:32] = f(norm2_g[0]).reshape(16, 128).T
    vecs_base[:, 32] = f(q_norm_a[0])
    vecs_base[:, 33] = f(k_norm_a[0])
    vecs_base[:, 34] = np.tile(f(q_norm_b[0]), 2)
    vecs_base[:, 35] = np.tile(f(k_norm_b[0]), 2)
    vecs_base[:, 36:44] = rel_bias[31, 0:8][None, :]
    for hp in range(2):
        for c in range(4):
            vecs_base[0:64, 44 + 4 * hp + c] = sinks0[8 * hp + 2 * c]
            vecs_base[64:128, 44 + 4 * hp + c] = sinks0[8 * hp + 2 * c + 1]
    ident = np.eye(128, dtype=np.float32)
    onesbd = np.ones((128, 256), np.float32)
    bd = np.zeros((128, 128), np.float32)
    bd[:64, :64] = 1.0
    bd[64:, 64:] = 1.0
    onesbd[:, 128:] = bd
    indoh = np.zeros((128, 16, 128), np.float32)
    for n in range(16):
        indoh[n, n, :] = 1.0
    indoh = indoh.reshape(128, 16 * 128)
    consts = {g: _host_consts(rel_bias, sinks0, g) for g in range(2)}
    in_maps = []
    for c in range(8):
        b, g = c // 2, c % 2
        xb = x[b]
        xblk = xb.reshape(NB, LB, D)
        xo = np.ascontiguousarray(xblk[g::2].reshape(2048, D))
        xp = np.zeros((8, 128, D), np.float32)
        for j in range(8):
            ob = 2 * j + g
            if ob > 0:
                xp[j] = xb[ob * LB - 128:ob * LB]
        biasA, biasB, mconst = consts[g]
        in_maps.append({
            "xc": xb, "xo": xo, "xp": xp.reshape(8 * 128, D),
            "w_in": w_in0, "w_ba": w_ba, "w_bb": w_bb, "w_out": w_o, "w_gu": w_gu, "w_dn": w_dn,
            "vecs": vecs_base, "mconst": mconst, "ident": ident, "onesbd": onesbd, "indoh": indoh,
            "biasA": biasA, "biasB": biasB,
        })
    return in_maps


_NC_CACHE = {}


def kernel(**inputs):
    in_maps = _prep_inputs(**inputs)
    if "nc" not in _NC_CACHE:
        _NC_CACHE["nc"] = build_program()
    nc = _NC_CACHE["nc"]
    res = run_bass_kernel_spmd(nc, in_maps, core_ids=list(range(8)))
    out = np.empty((4, SEQ, D), np.float32)
    for c in range(8):
        b, g = c // 2, c % 2
        y = np.asarray(res.results[c]["y"], dtype=np.float32).reshape(8, LB, D)
        out[b].reshape(NB, LB, D)[g::2] = y
    return out
```

```python
import math
import numpy as np
from contextlib import ExitStack

import concourse.bass as bass
import concourse.mybir as mybir
from concourse.bass_utils import run_bass_kernel_spmd

F32 = mybir.dt.float32
BF16 = mybir.dt.bfloat16
U8 = mybir.dt.uint8
AF = mybir.ActivationFunctionType
ALU = mybir.AluOpType

D = 2048
SEQ = 4096
NB = 16
LB = 256
DFF = 5632
NEG = -30000.0
EPS = 1e-6
INW = 8448
C_QA, C_KA, C_VA, C_QB, C_KB, C_VB, C_GA, C_GB = 0, 1024, 2048, 3072, 4096, 4224, 4352, 6400
NVEC = 52

ENGINES = ("pe", "act", "dve", "pool", "sp")
SEM_EPOCH = 20000
NDMA_SEMS = 8


class Buf:
    __slots__ = ("name", "last_w", "readers", "dma_readers")

    def __init__(self, name):
        self.name = name
        self.last_w = None
        self.readers = {}
        self.dma_readers = []


class Op:
    __slots__ = ("eng", "fn", "deps", "is_dma", "ordinal", "sig", "dma_slot", "dma_val")

    def __init__(self, eng, fn, is_dma):
        self.eng = eng
        self.fn = fn
        self.deps = []
        self.is_dma = is_dma
        self.ordinal = None
        self.sig = False
        self.dma_slot = None
        self.dma_val = None


class Sched:
    def __init__(self, nc):
        self.nc = nc
        self.ops = {e: [] for e in ENGINES}
        self.ndma = {e: 0 for e in ENGINES}
        self.final_dmas = []
        self.recent_dmas = {e: [] for e in ENGINES}
        self.pending_bar = {e: None for e in ENGINES}

    def _record(self, op, reads, writes):
        deps = {}

        def add(d):
            if d is None or d is op:
                return
            deps[id(d)] = d

        bar = self.pending_bar[op.eng]
        if bar is not None:
            for d in bar:
                add(d)
            self.pending_bar[op.eng] = None
        for b in reads:
            add(b.last_w)
        for b in writes:
            add(b.last_w)
            for r in b.readers.values():
                add(r)
            for r in b.dma_readers:
                add(r)
        for b in reads:
            if op.is_dma:
                b.dma_readers.append(op)
            else:
                b.readers[op.eng] = op
        for b in writes:
            b.last_w = op
            b.readers = {}
            b.dma_readers = []
        for d in deps.values():
            if (not d.is_dma) and (not op.is_dma) and d.eng == op.eng and op.eng == "pe":
                continue
            op.deps.append(d)
            if not d.is_dma:
                d.sig = True
        self.ops[op.eng].append(op)
        return op

    def op(self, eng, fn, reads=(), writes=()):
        return self._record(Op(eng, fn, False), reads, writes)

    def dma(self, queue, fn, reads=(), writes=(), final=False):
        o = Op(queue, fn, True)
        n = self.ndma[queue]
        self.ndma[queue] = n + 1
        o.dma_slot = n % NDMA_SEMS
        o.dma_val = 16 * (n // NDMA_SEMS + 1)
        self._record(o, reads, writes)
        rd = self.recent_dmas[queue]
        rd.append(o)
        if len(rd) > NDMA_SEMS:
            rd.pop(0)
        if final:
            self.final_dmas.append(o)
        return o

    def barrier(self):
        deps = []
        for e in ENGINES:
            last = None
            for o in reversed(self.ops[e]):
                if not o.is_dma:
                    last = o
                    break
            if last is not None:
                deps.append(last)
            deps.extend(self.recent_dmas[e])
        for e in ENGINES:
            prev = self.pending_bar[e]
            self.pending_bar[e] = list(deps) + (prev or [])

    def emit(self, stack):
        nc = self.nc
        nsig = {}
        for e in ENGINES:
            k = 0
            for o in self.ops[e]:
                if (not o.is_dma) and o.sig:
                    o.ordinal = k
                    k += 1
            nsig[e] = k
        csems = {}
        for e in ENGINES:
            n_ep = (nsig[e] + SEM_EPOCH - 1) // SEM_EPOCH
            csems[e] = [stack.enter_context(nc.semaphore(f"c_{e}_{i}")) for i in range(n_ep)]
        dsems = {}
        for e in ENGINES:
            if self.ndma[e]:
                dsems[e] = [stack.enter_context(nc.semaphore(f"d_{e}_{i}")) for i in range(NDMA_SEMS)]

        def target(d):
            if d.is_dma:
                return dsems[d.eng][d.dma_slot], d.dma_val
            return csems[d.eng][d.ordinal // SEM_EPOCH], d.ordinal % SEM_EPOCH + 1

        block = stack.enter_context(nc.Block())
        engmap = {"pe": "tensor", "act": "scalar", "dve": "vector", "pool": "gpsimd", "sp": "sync"}

        def body(ename, eng):
            waited = {}

            def wait(sem, val):
                key = id(sem)
                if waited.get(key, 0) >= val:
                    return
                waited[key] = val
                eng.wait_ge(sem, val)

            for o in self.ops[ename]:
                if o.is_dma and o.dma_val > 16:
                    wait(dsems[ename][o.dma_slot], o.dma_val - 16)
                for d in o.deps:
                    s, v = target(d)
                    wait(s, v)
                inst = o.fn(eng)
                if o.is_dma:
                    inst.then_inc(dsems[ename][o.dma_slot], 16)
                elif o.sig:
                    inst.then_inc(csems[ename][o.ordinal // SEM_EPOCH], 1)
            if ename == "sp":
                for d in self.final_dmas:
                    s, v = target(d)
                    wait(s, v)

        for ename in ENGINES:
            getattr(block, engmap[ename])(lambda eng, _n=ename: body(_n, eng))


class Tile:
    def __init__(self, t, bufs):
        self.t = t
        self.b = bufs


def build_program(debug=False):
    nc = bass.Bass("TRN2", target_bir_lowering=False)

    def din(name, shape):
        return nc.dram_tensor(name, list(shape), F32, kind="ExternalInput").ap()

    xc = din("xc", [SEQ, D])
    xo = din("xo", [2048, D])
    xp = din("xp", [8 * 128, D])
    w_in = din("w_in", [D, INW])
    w_ba = din("w_ba", [1024, D])
    w_bb = din("w_bb", [1024, D])
    w_out = din("w_out", [D, D])
    w_gu = din("w_gu", [D, 2 * DFF])
    w_dn = din("w_dn", [DFF, D])
    vecs_d = din("vecs", [128, NVEC])
    mconst_d = din("mconst", [128, 8 * 4 * 16])
    ident_d = din("ident", [128, 128])
    ones_d = din("onesbd", [128, 256])
    indoh_d = din("indoh", [128, 16 * 128])
    biasA_d = din("biasA", [3 * 8 * 128, 512])
    biasB_d = din("biasB", [3 * 2 * 128, 1024])
    y_d = nc.dram_tensor("y", [2048, D], F32, kind="ExternalOutput").ap()
    dbg = {}
    if debug:
        dbg["yab"] = nc.dram_tensor("dbg_yab", [128, 16 * 1024], F32, kind="ExternalOutput").ap()

    w_in_v = w_in.rearrange("(c p) n -> p c n", p=128)
    w_ba_v = w_ba.rearrange("(c p) n -> p c n", p=128)
    w_bb_v = w_bb.rearrange("(c p) n -> p c n", p=128)
    w_out_v = w_out.rearrange("(c p) n -> p c n", p=128)
    w_gu_v = w_gu.rearrange("(c p) n -> p c n", p=128)
    w_dn_v = w_dn.rearrange("(c p) n -> p c n", p=128)
    xc_v = xc.rearrange("(t p) d -> t p d", p=128)
    xo_v = xo.rearrange("(t p) d -> t p d", p=128)
    xp_v = xp.rearrange("(t p) d -> t p d", p=128)
    y_v = y_d.rearrange("(t p) d -> t p d", p=128)
    biasA_v = biasA_d.rearrange("(i p) n -> i p n", p=128)
    biasB_v = biasB_d.rearrange("(i p) n -> i p n", p=128)

    st = ExitStack()
    with st:
        S = Sched(nc)
        ARENA = 212800
        st.enter_context(nc.sbuf_tensor("arena", [128, ARENA], U8))
        abase = nc.sbuf_base - ARENA
        uid = [0]

        class Region:
            def __init__(self, lo, hi):
                self.lo, self.hi, self.p = lo, hi, lo

            def alloc(self, name, shape, dt, nbufs=1, parts=None):
                esz = 2 if dt == BF16 else 4
                nbytes = int(np.prod(shape[1:])) * esz
                nbytes = (nbytes + 31) // 32 * 32
                off = self.p
                self.p += nbytes
                assert self.p <= self.hi, (name, self.p, self.hi)
                uid[0] += 1
                t = nc.alloc_sbuf_tensor_at(f"{name}_{uid[0]}", list(shape), dt, offset=abase + off)
                return Tile(t, [Buf(f"{name}{i}") for i in range(nbufs)])

            def rest(self):
                return Region(self.p, self.hi)

        top = Region(0, ARENA)

        PB = [st.enter_context(nc.psum_tensor(f"pb{i}", [128, 512], F32)) for i in range(6)]
        PBb = [Buf(f"pb{i}") for i in range(6)]
        TP = [st.enter_context(nc.psum_tensor(f"tp{i}", [128, 8, 128], BF16)) for i in range(2)]
        TPb = [Buf(f"tp{i}") for i in range(2)]

        vecs = top.alloc("vecs", [128, NVEC], F32)
        mconst = top.alloc("mconst", [128, 8, 4, 16], F32)
        ident = top.alloc("ident", [128, 128], BF16)
        ones_bf = top.alloc("ones_bf", [128, 128], BF16)
        onesf = top.alloc("onesf", [128, 256], F32)
        indoh = top.alloc("indoh", [128, 16, 128], BF16)
        gv = top.alloc("gv", [128, 4], F32)
        expsink = top.alloc("expsink", [128, 8], F32)
        fb = top.alloc("fb", [128, 8, 8, 16], F32)
        ssq = top.alloc("ssq", [128, 4], F32, nbufs=2)
        yab = top.alloc("yab", [128, 16, 1024], BF16, nbufs=16 * 4)

        def yab_b(chunk, blk):
            return yab.b[chunk * 4 + blk]

        def ACT(reads, writes, **kw):
            S.op("act", lambda e: e.activation(**kw), reads, writes)

        def DVE(meth, reads, writes, **kw):
            S.op("dve", lambda e: getattr(e, meth)(**kw), reads, writes)

        def TR(reads, writes, **kw):
            S.op("pe", lambda e: e.transpose(**kw), reads, writes)

        def DMA(q, out, in_, reads=(), writes=(), final=False):
            S.dma(q, lambda e: e.dma_start(out=out, in_=in_), reads, writes, final)

        def mm(out, lhsT, rhs, start, stop, reads, writes, **kw):
            S.op("pe", lambda e: e.matmul(out, lhsT=lhsT, rhs=rhs, start=start, stop=stop, **kw), reads, writes)

        def bc(ap2d, n):
            a = ap2d.shape[1]
            return ap2d.rearrange("p (a b) -> p a b", b=1).broadcast_to([128, a, n])

        DMA("sp", vecs.t[:], vecs_d[:, :], writes=vecs.b)
        DMA("sp", mconst.t[:].rearrange("p a b c -> p (a b c)"), mconst_d[:, :], writes=mconst.b)
        DMA("sp", onesf.t[:], ones_d[:, :], writes=onesf.b)
        DMA("pool", ident.t[:], ident_d[:, :], writes=ident.b)
        DMA("pool", ones_bf.t[:], ones_d[:, 0:128], writes=ones_bf.b)
        DMA("pool", indoh.t[:].rearrange("p a b -> p (a b)"), indoh_d[:, :], writes=indoh.b)
        DVE("tensor_scalar", vecs.b, gv.b, out=gv.t[:, 0:1], in0=vecs.t[:, 32:33], scalar1=128.0 ** -0.5, scalar2=None, op0=ALU.mult)
        DVE("tensor_copy", vecs.b, gv.b, out=gv.t[:, 1:2], in_=vecs.t[:, 33:34])
        DVE("tensor_scalar", vecs.b, gv.b, out=gv.t[:, 2:3], in0=vecs.t[:, 34:35], scalar1=64.0 ** -0.5, scalar2=None, op0=ALU.mult)
        DVE("tensor_copy", vecs.b, gv.b, out=gv.t[:, 3:4], in_=vecs.t[:, 35:36])
        ACT(vecs.b, expsink.b, out=expsink.t[:], in_=vecs.t[:, 44:52], func=AF.Exp)
        for h in range(8):
            DVE("tensor_scalar", vecs.b + mconst.b, fb.b, out=fb.t[:, :, h, :], in0=mconst.t[:, :, 3, :], scalar1=vecs.t[:, 36 + h:37 + h], scalar2=None, op0=ALU.mult)

        tp_rr = [0]
        ssq_rr = [0]

        def norm_front(x_ap, x_bufs, xb_ap, xb_bufs, junk=None):
            k = ssq_rr[0] % 2
            ssq_rr[0] += 1
            sb = [ssq.b[k]]
            s0 = ssq.t[:, 2 * k:2 * k + 1]
            s1 = ssq.t[:, 2 * k + 1:2 * k + 2]
            if junk is None:
                ACT(list(x_bufs), list(xb_bufs) + sb, out=xb_ap, in_=x_ap, func=AF.Square, accum_out=s0)
            else:
                ACT(list(x_bufs), junk.b + sb, out=junk.t[:], in_=x_ap, func=AF.Square, accum_out=s0)
            ACT(sb, sb, out=s1, in_=s0, func=AF.Sqrt, scale=1.0 / D, bias=EPS)
            DVE("reciprocal", sb, sb, out=s1, in_=s1)
            DVE("tensor_scalar", list(x_bufs) + sb, list(xb_bufs), out=xb_ap, in0=x_ap, scalar1=s1, scalar2=None, op0=ALU.mult)

        def norm_back(xb_ap, xb_bufs, gcol0, hT, hT_bufs, tok0):
            for half in range(2):
                ti = tp_rr[0] % 2
                tp_rr[0] += 1
                for kc in range(8):
                    c = half * 8 + kc
                    TR(list(xb_bufs) + ident.b, [TPb[ti]], out=TP[ti][:, kc, :], in_=xb_ap[:, c * 128:(c + 1) * 128], identity=ident.t[:])
                DVE("tensor_tensor", [TPb[ti]] + vecs.b, list(hT_bufs), out=hT.t[:, half * 8:(half + 1) * 8, tok0:tok0 + 128], in0=TP[ti][:],
                    in1=bc(vecs.t[:, gcol0 + half * 8:gcol0 + half * 8 + 8], 128), op=ALU.mult)

        def norm_tile(x_ap, x_bufs, xb, gcol0, hT, hT_bufs, tok0):
            norm_front(x_ap, x_bufs, xb.t[:], xb.b)
            norm_back(xb.t[:], xb.b, gcol0, hT, hT_bufs, tok0)

        def slot(ring, i):
            return ring.t[:, i * 2048:(i + 1) * 2048]

        def slot128(ring, i):
            return slot(ring, i).rearrange("p (c n) -> p c n", n=128)

        class Pipe:
            def __init__(self):
                self.pend = None
                self.bg = []

            def after_x(self, tail):
                if self.pend is not None:
                    self.pend()
                self.pend = tail
                if self.bg:
                    self.bg.pop(0)()

            def flush(self):
                if self.pend is not None:
                    self.pend()
                    self.pend = None

            def drain_bg(self):
                while self.bg:
                    self.bg.pop(0)()

        def qk_tail(pj_ap, pjb, N, sq, rt, ssb, ones_ap, invd, gcol, out_ap, obufs, accs=None):
            def run():
                mm(PB[ssb][:, 0:N], ones_ap, sq.t[:, 0:N], True, True, sq.b + onesf.b, [PBb[ssb]])
                ACT([PBb[ssb]], rt.b, out=rt.t[:, 0:N], in_=PB[ssb][:, 0:N], func=AF.Sqrt, scale=invd, bias=EPS)
                DVE("reciprocal", rt.b, rt.b, out=rt.t[:, 0:N], in_=rt.t[:, 0:N])
                if accs is None:
                    DVE("scalar_tensor_tensor", [pjb] + gv.b + rt.b, list(obufs), out=out_ap, in0=pj_ap, scalar=gv.t[:, gcol:gcol + 1],
                        in1=rt.t[:, 0:N], op0=ALU.mult, op1=ALU.mult)
                else:
                    for (o_ap, c0, c1, acc_ap, abufs) in accs:
                        DVE("scalar_tensor_tensor", [pjb] + gv.b + rt.b, list(obufs) + list(abufs), out=o_ap, in0=pj_ap[:, c0:c1], scalar=gv.t[:, gcol:gcol + 1],
                            in1=rt.t[:, c0:c1], op0=ALU.mult, op1=ALU.mult, accum_out=acc_ap)
            return run

        wtab_d = nc.dram_tensor("wtab", [128, 18432], BF16, kind="Internal").ap()
        wtab_b = Buf("wtab")
        P0 = top.rest()
        stA = P0.alloc("stA", [128, 8, 3, 512], F32, nbufs=2)
        stB = P0.alloc("stB", [128, 2, 3, 1024], F32)
        wA0 = P0.alloc("wA0", [128, 12288], BF16, nbufs=2)
        wB0 = P0.alloc("wB0", [128, 6144], BF16)
        bA4 = biasA_d.rearrange("(i h p) n -> p h i n", i=3, h=8, p=128)
        bB4 = biasB_d.rearrange("(k h p) n -> p h k n", k=3, h=2, p=128)
        for hh in range(2):
            for i in range(3):
                DMA("sp", stA.t[:, 4 * hh:4 * hh + 4, i, :], bA4[:, 4 * hh:4 * hh + 4, i, :], writes=[stA.b[hh]])
        for k in range(3):
            DMA("sp", stB.t[:, :, k, :], bB4[:, :, k, :], writes=stB.b)
        for hh in range(2):
            ACT([stA.b[hh]], [wA0.b[hh]], out=wA0.t[:, 6144 * hh:6144 * (hh + 1)], in_=stA.t[:, 4 * hh:4 * hh + 4, :, :].rearrange("p h i n -> p (h i n)"), func=AF.Exp)
        ACT(stB.b, wB0.b, out=wB0.t[:], in_=stB.t[:].rearrange("p h k n -> p (h k n)"), func=AF.Exp)
        for hh in range(2):
            DMA("sp", wtab_d[:, 6144 * hh:6144 * (hh + 1)], wA0.t[:, 6144 * hh:6144 * (hh + 1)], reads=[wA0.b[hh]], writes=[wtab_b])
        DMA("sp", wtab_d[:, 12288:18432], wB0.t[:], reads=wB0.b, writes=[wtab_b])

        for th in range(2):
            for hp in range(2):
                S.barrier()
                hpR = top.rest()
                kAT = hpR.alloc("kAT", [128, 4, SEQ], BF16, nbufs=4 * 8)
                vA = hpR.alloc("vA", [128, 32, 512], BF16, nbufs=8)
                ksum = hpR.alloc("ksum", [128, 4, 16], F32)
                kmT = hpR.alloc("kmT", [128, 4, 16], BF16)
                A = hpR.rest()
                ring = A.alloc("ringA", [128, 8 * 2048], BF16, nbufs=8)
                xt = A.alloc("xtA", [128, 2, D], F32, nbufs=2)
                xb2 = A.alloc("xbA", [128, 2, D], BF16, nbufs=2)
                hTc2 = [A.alloc(f"hTc{i}", [128, 16, 512], BF16, nbufs=4) for i in range(2)]
                sq2 = [A.alloc(f"sqA{i}", [128, 512], F32) for i in range(2)]
                rtA = A.alloc("rtA", [128, 512], F32)
                rt2 = [rtA, rtA]
                junk = None
                DVE("memset", [], ksum.b, ap=ksum.t[:], constant=0.0)
                for hl in range(4):
                    c0 = C_KA + (4 * hp + hl) * 128
                    DMA("pool", slot128(ring, hl), w_in_v[:, :, c0:c0 + 128], writes=[ring.b[hl]])
                c0 = C_VA + hp * 512
                wv_view = ring.t[:, 4 * 2048:8 * 2048].rearrange("p (c n) -> p c n", n=512)
                DMA("pool", wv_view, w_in_v[:, :, c0:c0 + 512], writes=ring.b[4:8])
                n_ct = 4 if th == 0 else 8
                G = n_ct * 4

                def ev(g, xt=xt, xb2=xb2, hTc2=hTc2, junk=junk, G=G):
                    gb = g - 2
                    if 0 <= gb < G:
                        k = gb % 2
                        hT = hTc2[(gb // 4) % 2]
                        norm_back(xb2.t[:, k, :], [xb2.b[k]], 0, hT, [hT.b[gb % 4]], (gb % 4) * 128)
                    if g < G:
                        k = g % 2
                        DMA("sp", xt.t[:, k, :], xc_v[g], writes=[xt.b[k]])
                        norm_front(xt.t[:, k, :], [xt.b[k]], xb2.t[:, k, :], [xb2.b[k]], junk=junk)

                for g in range(6):
                    ev(g)
                nxt = [6]
                nb = [0]

                def boundary(nxt=nxt, nb=nb, G=G):
                    if nb[0] % 2 == 0 and nxt[0] < G + 2:
                        ev(nxt[0])
                        nxt[0] += 1
                    nb[0] += 1

                pipe = Pipe()
                pj_rr = 0
                pv_rr = 0
                yi = 0
                for ct in range(n_ct):
                    hTc = hTc2[ct % 2]
                    for hl in range(4):
                        boundary()
                        pi = pj_rr % 2
                        pj_rr += 1
                        wk = slot128(ring, hl)
                        for kc in range(16):
                            mm(PB[pi][:, 0:512], wk[:, kc, :], hTc.t[:, kc, :], kc == 0, kc == 15, [ring.b[hl]] + hTc.b, [PBb[pi]])
                        sq, rt = sq2[yi % 2], rt2[yi % 2]
                        ssb = (2, 5)[yi % 2]
                        yi += 1
                        ACT([PBb[pi]], sq.b, out=sq.t[:, 0:512], in_=PB[pi][:, 0:512], func=AF.Square)
                        accs = []
                        for bb in range(2):
                            blk = ct * 2 + bb
                            accs.append((kAT.t[:, hl, ct * 512 + bb * 256:ct * 512 + (bb + 1) * 256], bb * 256, (bb + 1) * 256, ksum.t[:, hl, blk:blk + 1], ksum.b))
                        pipe.after_x(qk_tail(PB[pi], PBb[pi], 512, sq, rt, ssb, onesf.t[:, 0:128], 1.0 / 128, 1, None, [kAT.b[hl * 8 + ct]], accs))
                    for s in range(4):
                        boundary()
                        pi = 3 + pv_rr % 2
                        pv_rr += 1
                        for kc in range(16):
                            mm(PB[pi][:, 0:512], hTc.t[:, kc, s * 128:(s + 1) * 128], wv_view[:, kc, :], kc == 0, kc == 15, [hTc.b[s]] + ring.b[4:8], [PBb[pi]])
                        ACT([PBb[pi]], [vA.b[ct]], out=vA.t[:, ct * 4 + s, :], in_=PB[pi][:, 0:512], func=AF.Copy)
                        if s == 0:
                            pipe.after_x(None)
                while nxt[0] < G + 2:
                    ev(nxt[0])
                    nxt[0] += 1
                pipe.flush()
                DVE("tensor_scalar", ksum.b, kmT.b, out=kmT.t[:], in0=ksum.t[:], scalar1=1.0 / 256, scalar2=None, op0=ALU.mult)

                S.barrier()
                Bp = hpR.rest()
                WA = Bp.alloc("WA", [128, 4, 3, 512], BF16)
                WB = Bp.alloc("WB", [128, 3, 1024], BF16)
                QAT = Bp.alloc("QAT", [128, 2, 4, 256], BF16, nbufs=4)
                QBT = Bp.alloc("QBT", [128, 2, 4, 256], BF16, nbufs=4)
                kB2T = Bp.alloc("kB2T", [128, 2, 384], BF16, nbufs=2)
                vB = Bp.alloc("vB", [128, 2, 3, 64], BF16, nbufs=2)
                MT = Bp.alloc("MT", [128, 2, 4, 256], BF16, nbufs=2)
                B1 = Bp.rest()
                ring = B1.alloc("ringB", [128, 4 * 2048], BF16, nbufs=4)
                xt = B1.alloc("xtB", [128, D], F32)
                xb2 = B1.alloc("xbB", [128, 2, D], BF16, nbufs=2)
                hTo2 = [B1.alloc(f"hTo{i}", [128, 16, 384], BF16, nbufs=3) for i in range(2)]
                sq2 = [B1.alloc(f"sqB{i}", [128, 512], F32) for i in range(2)]
                rtB = B1.alloc("rtB", [128, 512], F32)
                rt2 = [rtB, rtB]
                Mbf = B1.alloc("Mbf", [128, 8, 128], BF16)
                gm = B1.alloc("gm", [128, 8, 16], F32)
                t8 = B1.alloc("t8", [128, 8, 8], F32)
                selt = B1.alloc("selt", [128, 8, 16], F32)
                m1t = B1.alloc("m1t", [128, 8, 16], F32)
                junkB = None
                B2 = Bp.rest()
                Pm = B2.alloc("Pm", [128, 3, 512], BF16, nbufs=3)
                Eb = B2.alloc("Eb", [128, 2, 2, 512], BF16, nbufs=4)
                Ps = B2.alloc("Ps", [128, 2, 2, 512], BF16, nbufs=4)
                rden = B2.alloc("rden", [128, 2, 512], F32, nbufs=2)
                DMA("sp", WA.t[:].rearrange("p a b c -> p (a b c)"), wtab_d[:, 6144 * hp:6144 * (hp + 1)], reads=[wtab_b], writes=WA.b)
                DMA("sp", WB.t[:].rearrange("p a b -> p (a b)"), wtab_d[:, 12288 + 3072 * hp:12288 + 3072 * (hp + 1)], reads=[wtab_b], writes=WB.b)
                DVE("memset", [], Mbf.b, ap=Mbf.t[:], constant=0.0)
                ring_rr = [0]
                xcnt = [0]

                def ring_next(ring_rr=ring_rr):
                    i = ring_rr[0] % 4
                    ring_rr[0] += 1
                    return i

                for rnd in range(2):
                    def b_tasks(bi, rnd=rnd, xt=xt, xb2=xb2, hTo2=hTo2, xcnt=xcnt):
                        j = 4 * th + 2 * rnd + bi
                        ks = []

                        def F(r):
                            k = xcnt[0] % 2
                            xcnt[0] += 1
                            ks.append(k)
                            src_ap = xp_v[j] if r == 0 else xo_v[j * 2 + r - 1]
                            DMA("sp", xt.t[:], src_ap, writes=xt.b)
                            norm_front(xt.t[:], xt.b, xb2.t[:, k, :], [xb2.b[k]], junk=junkB)

                        def Bk(r):
                            k = ks[r]
                            hT = hTo2[bi]
                            norm_back(xb2.t[:, k, :], [xb2.b[k]], 0, hT, [hT.b[r]], r * 128)

                        return [lambda: F(0), lambda: F(1), lambda: (Bk(0), F(2)), lambda: Bk(1), lambda: Bk(2)]

                    pipe = Pipe()
                    pipe.bg = b_tasks(0)
                    pipe.drain_bg()
                    pj_rr = 0
                    yi = 0
                    pend_gate = None
                    for bi in range(2):
                        j = 4 * th + 2 * rnd + bi
                        hTo = hTo2[bi]
                        if bi == 0:
                            pipe.bg = b_tasks(1)
                        projs = [("qa", 0), ("qa", 1), ("qb", 0), ("qb", 1), ("kb", 0), ("vb", 0)]
                        for pidx, (kind, p) in enumerate(projs):
                            pi = pj_rr % 3
                            pj_rr += 1
                            if kind in ("qa", "qb"):
                                for hh in range(2):
                                    si = ring_next()
                                    if kind == "qa":
                                        c0 = C_QA + (4 * hp + 2 * p + hh) * 128
                                    else:
                                        c0 = C_QB + 512 * hp + 128 * (2 * p + hh)
                                    wq = slot128(ring, si)
                                    DMA("pool", wq, w_in_v[:, :, c0:c0 + 128], writes=[ring.b[si]])
                                    for kc in range(16):
                                        mm(PB[pi][:, hh * 256:(hh + 1) * 256], wq[:, kc, :], hTo.t[:, kc, 128:384], kc == 0, kc == 15, [ring.b[si]] + hTo.b[1:3], [PBb[pi]])
                                sq, rt = sq2[yi % 2], rt2[yi % 2]
                                ssb = (3, 4)[yi % 2]
                                yi += 1
                                ACT([PBb[pi]], sq.b, out=sq.t[:, 0:512], in_=PB[pi][:, 0:512], func=AF.Square)
                                if kind == "qa":
                                    tail = qk_tail(PB[pi][:, 0:512], PBb[pi], 512, sq, rt, ssb, onesf.t[:, 0:128], 1.0 / 128, 0,
                                                   QAT.t[:, bi, 2 * p:2 * p + 2, :].rearrange("p a q -> p (a q)"), [QAT.b[bi * 2 + p]])
                                else:
                                    tail = qk_tail(PB[pi][:, 0:512], PBb[pi], 512, sq, rt, ssb, onesf.t[:, 128:256], 1.0 / 64, 2,
                                                   QBT.t[:, bi, 2 * p:2 * p + 2, :].rearrange("p a q -> p (a q)"), [QBT.b[bi * 2 + p]])
                                pipe.after_x(tail)
                            elif kind == "kb":
                                si = ring_next()
                                c0 = C_KB + 64 * hp
                                kview = slot128(ring, si)
                                DMA("pool", kview[:, :, 0:64], w_in_v[:, :, c0:c0 + 64], writes=[ring.b[si]])
                                DMA("pool", kview[:, :, 64:128], w_in_v[:, :, c0:c0 + 64], writes=[ring.b[si]])
                                for kc in range(16):
                                    mm(PB[pi][:, 0:384], kview[:, kc, :], hTo.t[:, kc, 0:384], kc == 0, kc == 15, [ring.b[si]] + hTo.b, [PBb[pi]])
                                sq, rt = sq2[yi % 2], rt2[yi % 2]
                                ssb = (3, 4)[yi % 2]
                                yi += 1
                                ACT([PBb[pi]], sq.b, out=sq.t[:, 0:384], in_=PB[pi][:, 0:384], func=AF.Square)
                                pipe.after_x(qk_tail(PB[pi][:, 0:384], PBb[pi], 384, sq, rt, ssb, onesf.t[:, 128:256], 1.0 / 64, 3, kB2T.t[:, bi, :], [kB2T.b[bi]]))
                            else:
                                si = ring_next()
                                c0 = C_VB + 64 * hp
                                vview = slot(ring, si)[:, 0:1024].rearrange("p (c n) -> p c n", n=64)
                                DMA("pool", vview, w_in_v[:, :, c0:c0 + 64], writes=[ring.b[si]])
                                for r in range(3):
                                    for kc in range(16):
                                        mm(PB[pi][:, r * 64:(r + 1) * 64], hTo.t[:, kc, r * 128:(r + 1) * 128], vview[:, kc, :], kc == 0, kc == 15, [ring.b[si], hTo.b[r]], [PBb[pi]])
                                ACT([PBb[pi]], [vB.b[bi]], out=vB.t[:, bi, :, :].rearrange("p r d -> p (r d)"), in_=PB[pi][:, 0:192], func=AF.Copy)
                                pipe.after_x(None)
                            if pidx == 1 and pend_gate is not None:
                                pend_gate()
                                pend_gate = None
                        pipe.flush()
                        for hl in range(4):
                            for qs in range(2):
                                i = hl * 2 + qs
                                mm(PB[5][:, i * 16:(i + 1) * 16], QAT.t[:, bi, hl, qs * 128:(qs + 1) * 128], kmT.t[:, hl, :], True, True, [QAT.b[bi * 2 + hl // 2]] + kmT.b, [PBb[5]])
                        g3 = PB[5][:, 0:128].rearrange("p (i n) -> p i n", n=16)
                        DVE("tensor_tensor", [PBb[5]] + mconst.b, gm.b, out=gm.t[:], in0=g3, in1=mconst.t[:, j, 0:1, :].broadcast_to([128, 8, 16]), op=ALU.min)
                        for i in range(8):
                            DVE("max", gm.b, t8.b, out=t8.t[:, i, :], in_=gm.t[:, i, :])
                        DVE("tensor_tensor", gm.b + t8.b, selt.b, out=selt.t[:], in0=gm.t[:], in1=t8.t[:, :, 2:3].broadcast_to([128, 8, 16]), op=ALU.is_ge)
                        DVE("tensor_tensor", selt.b + mconst.b, selt.b, out=selt.t[:], in0=selt.t[:], in1=mconst.t[:, j, 1:2, :].broadcast_to([128, 8, 16]), op=ALU.mult)
                        DVE("tensor_tensor", selt.b + mconst.b, selt.b, out=selt.t[:], in0=selt.t[:], in1=mconst.t[:, j, 2:3, :].broadcast_to([128, 8, 16]), op=ALU.add)
                        fb4 = fb.t[:, j, 4 * hp:4 * hp + 4, :].rearrange("p h (o n) -> p h o n", o=1).broadcast_to([128, 4, 2, 16])
                        DVE("tensor_tensor", selt.b + fb.b, m1t.b, out=m1t.t[:].rearrange("p (h o) n -> p h o n", o=2), in0=selt.t[:].rearrange("p (h o) n -> p h o n", o=2), in1=fb4, op=ALU.mult)
                        DVE("tensor_scalar", selt.b, selt.b, out=selt.t[:], in0=selt.t[:], scalar1=-NEG, scalar2=NEG, op0=ALU.mult, op1=ALU.add)
                        DVE("tensor_tensor", selt.b + m1t.b, Mbf.b, out=Mbf.t[:, :, 0:16], in0=selt.t[:], in1=m1t.t[:], op=ALU.add)

                        def gate_back(bi=bi, Mbf=Mbf, MT=MT):
                            for i in range(8):
                                TR(Mbf.b + ident.b, [TPb[1]], out=TP[1][:, i, :], in_=Mbf.t[:, i, :], identity=ident.t[:])
                            ACT([TPb[1]], [MT.b[bi]], out=MT.t[:, bi, :, :].rearrange("p h q -> p (h q)"), in_=TP[1][:].rearrange("p i q -> p (i q)"), func=AF.Copy)

                        if bi == 0:
                            pend_gate = gate_back
                        else:
                            gate_back()
                    pipe.drain_bg()
                    if pend_gate is not None:
                        pend_gate()
                        pend_gate = None

                    S.barrier()
                    units = []
                    s_rr = [0]
                    p_rr = [0]
                    for bi in range(2):
                        jl = 2 * rnd + bi
                        j = 4 * th + jl
                        tok0 = jl * 256
                        for s in range(2):
                            for ri in range(2):
                                def mk_swa(bi=bi, j=j, jl=jl, tok0=tok0, s=s, ri=ri):
                                    r = s + ri
                                    kind = 1 if ri == 0 else 0
                                    if j == 0 and s == 0 and ri == 0:
                                        kind = 2
                                    it = s * 2 + ri
                                    sb0 = (0, 2)[it % 2]
                                    ek = it % 2

                                    def qk():
                                        for e_ in range(2):
                                            mm(PB[sb0 + e_][:, 0:512].rearrange("p (c q) -> p c q", c=4), kB2T.t[64 * e_:64 * e_ + 64, bi, r * 128:(r + 1) * 128],
                                               QBT.t[64 * e_:64 * e_ + 64, bi, :, s * 128:(s + 1) * 128], True, True, [kB2T.b[bi]] + QBT.b[bi * 2:bi * 2 + 2], [PBb[sb0 + e_]])

                                    def rest():
                                        for e_ in range(2):
                                            ACT([PBb[sb0 + e_]], [Eb.b[ek * 2 + e_]], out=Eb.t[:, ek, e_, :], in_=PB[sb0 + e_][:, 0:512], func=AF.Exp)
                                        for e_ in range(2):
                                            DVE("tensor_tensor", [Eb.b[ek * 2 + e_]] + WB.b, [Ps.b[ek * 2 + e_]], out=Ps.t[:, ek, e_, :], in0=Eb.t[:, ek, e_, :], in1=WB.t[:, kind, 512 * e_:512 * (e_ + 1)], op=ALU.mult)
                                        for e_ in range(2):
                                            mm(PB[4][64 * e_:64 * e_ + 64, 0:512], vB.t[:, bi, r, :], Ps.t[:, ek, e_, :], ri == 0, ri == 1, [vB.b[bi], Ps.b[ek * 2 + e_]], [PBb[4]])
                                            mm(PB[5][64 * e_:64 * e_ + 64, 0:512], ones_bf.t[:, 0:64], Ps.t[:, ek, e_, :], ri == 0, ri == 1, ones_bf.b + [Ps.b[ek * 2 + e_]], [PBb[5]])
                                        if ri == 1:
                                            rk = s % 2
                                            r4 = rden.t[:, rk, :].rearrange("p (c q) -> p c q", c=4)
                                            DVE("tensor_tensor", [PBb[5]] + expsink.b, [rden.b[rk]], out=r4, in0=PB[5][:, 0:512].rearrange("p (c q) -> p c q", c=4),
                                                in1=bc(expsink.t[:, 4 * hp:4 * hp + 4], 128), op=ALU.add)
                                            DVE("reciprocal", [rden.b[rk]], [rden.b[rk]], out=rden.t[:, rk, :], in_=rden.t[:, rk, :])
                                            ch0 = 8 + 4 * hp
                                            DVE("tensor_tensor", [PBb[4], rden.b[rk]], [yab_b(ch0 + c, jl) for c in range(4)],
                                                out=yab.t[:, ch0:ch0 + 4, tok0 + s * 128:tok0 + (s + 1) * 128], in0=PB[4][:, 0:512].rearrange("p (c q) -> p c q", c=4), in1=r4, op=ALU.mult)
                                    return qk, rest
                                units.append(mk_swa())
                        nkb = 2 * j + 2
                        for hl in range(4):
                            for kb in range(nkb):
                                def mk_moba(bi=bi, j=j, jl=jl, tok0=tok0, hl=hl, kb=kb, nkb=nkb):
                                    sbk = s_rr[0] % 4
                                    s_rr[0] += 1
                                    pi = p_rr[0] % 3
                                    p_rr[0] += 1
                                    odb = 4 + hl % 2

                                    def qk():
                                        for kt in range(2):
                                            ktile = kb * 2 + kt
                                            mm(PB[sbk][:, kt * 256:(kt + 1) * 256], kAT.t[:, hl, ktile * 128:(ktile + 1) * 128], QAT.t[:, bi, hl, :], True, False,
                                               [kAT.b[hl * 8 + ktile // 4], QAT.b[bi * 2 + hl // 2]], [PBb[sbk]])
                                            mm(PB[sbk][:, kt * 256:(kt + 1) * 256], indoh.t[:, kb, :], MT.t[:, bi, hl, :], False, True, indoh.b + [MT.b[bi]], [PBb[sbk]])

                                    def rest():
                                        ACT([PBb[sbk]], [Pm.b[pi]], out=Pm.t[:, pi, :], in_=PB[sbk][:, 0:512], func=AF.Exp)
                                        isp = kb - (2 * j - 1)
                                        if 0 <= isp <= 2:
                                            DVE("tensor_tensor", [Pm.b[pi]] + WA.b, [Pm.b[pi]], out=Pm.t[:, pi, :], in0=Pm.t[:, pi, :], in1=WA.t[:, hl, isp, :], op=ALU.mult)
                                        for kt in range(2):
                                            ktile = kb * 2 + kt
                                            first = (kb == 0 and kt == 0)
                                            last = (kb == nkb - 1 and kt == 1)
                                            mm(PB[odb][:, 0:256], vA.t[:, ktile, hl * 128:(hl + 1) * 128], Pm.t[:, pi, kt * 256:(kt + 1) * 256], first, last,
                                               [vA.b[ktile // 4], Pm.b[pi]], [PBb[odb]])
                                            mm(PB[odb][:, 256:512], ones_bf.t[:], Pm.t[:, pi, kt * 256:(kt + 1) * 256], False, last,
                                               ones_bf.b + [Pm.b[pi]], [PBb[odb]], skip_group_check=True)
                                        if kb == nkb - 1:
                                            rk = hl % 2
                                            DVE("reciprocal", [PBb[odb]], [rden.b[rk]], out=rden.t[:, rk, 0:256], in_=PB[odb][:, 256:512])
                                            DVE("tensor_tensor", [PBb[odb], rden.b[rk]], [yab_b(4 * hp + hl, jl)], out=yab.t[:, 4 * hp + hl, tok0:tok0 + 256], in0=PB[odb][:, 0:256], in1=rden.t[:, rk, 0:256], op=ALU.mult)
                                    return qk, rest
                                units.append(mk_moba())
                    units[0][0]()
                    for ui in range(len(units)):
                        if ui + 1 < len(units):
                            units[ui + 1][0]()
                        units[ui][1]()
                    if rnd == 0:
                        S.barrier()

            S.barrier()
            C = top.rest()
            x1 = C.alloc("x1", [128, 4, D], F32, nbufs=16)
            aT = C.alloc("aT", [128, 44, 512], BF16, nbufs=44)
            hT = C.alloc("hT", [128, 16, 512], BF16, nbufs=4)
            mT = C.alloc("mT", [128, 16, 512], BF16, nbufs=16)
            xb = C.alloc("xbC", [128, D], BF16)
            ring = C.alloc("ringC", [128, 8 * 2048], BF16, nbufs=8)
            sg = C.alloc("sg", [128, 2, 512], F32, nbufs=2)
            mm1 = C.alloc("mm1", [128, 2, 512], F32, nbufs=2)
            rr = [0]
            pb = [0]

            def rnext():
                i = rr[0] % 8
                rr[0] += 1
                return i

            def pnext():
                i = pb[0] % 6
                pb[0] += 1
                return i

            for tl in range(2):
                t = 2 * th + tl
                tk0 = tl * 512
                for s in range(4):
                    DMA("sp", x1.t[:, s, :], xo_v[t * 4 + s], writes=x1.b[s * 4:(s + 1) * 4])
                    norm_tile(x1.t[:, s, :], x1.b[s * 4:(s + 1) * 4], xb, 0, hT, [hT.b[s]], s * 128)
                for fc in range(16):
                    sa = rnext()
                    DMA("pool", slot128(ring, sa), w_in_v[:, :, C_GA + fc * 128:C_GA + (fc + 1) * 128], writes=[ring.b[sa]])
                    sb_ = rnext()
                    DMA("pool", slot128(ring, sb_), w_in_v[:, :, C_GB + fc * 128:C_GB + (fc + 1) * 128], writes=[ring.b[sb_]])
                    sc = rnext()
                    cview = slot128(ring, sc)
                    DMA("pool", cview[:, 0:8, :], w_ba_v[:, :, fc * 128:(fc + 1) * 128], writes=[ring.b[sc]])
                    DMA("pool", cview[:, 8:16, :], w_bb_v[:, :, fc * 128:(fc + 1) * 128], writes=[ring.b[sc]])
                    pg = [pnext() for _ in range(4)]
                    for br, (si, ch0) in enumerate(((sa, 0), (sb_, 8))):
                        g_b = pg[2 * br]
                        z_b = pg[2 * br + 1]
                        wg = slot128(ring, si)
                        for kc in range(16):
                            mm(PB[g_b][:, :], wg[:, kc, :], hT.t[:, kc, :], kc == 0, kc == 15, [ring.b[si]] + hT.b, [PBb[g_b]])
                        for c in range(8):
                            mm(PB[z_b][:, :], cview[:, 8 * br + c, :], yab.t[:, ch0 + c, tk0:tk0 + 512], c == 0, c == 7,
                               [ring.b[sc], yab_b(ch0 + c, 2 * tl), yab_b(ch0 + c, 2 * tl + 1)], [PBb[z_b]])
                        ACT([PBb[g_b]], [sg.b[br]], out=sg.t[:, br, :], in_=PB[g_b][:, :], func=AF.Sigmoid)
                        DVE("tensor_tensor", [sg.b[br], PBb[z_b]], [mm1.b[br]], out=mm1.t[:, br, :], in0=sg.t[:, br, :], in1=PB[z_b][:, :], op=ALU.mult)
                    DVE("tensor_tensor", mm1.b, [mT.b[fc]], out=mT.t[:, fc, :], in0=mm1.t[:, 0, :], in1=mm1.t[:, 1, :], op=ALU.add)
                for cg in range(4):
                    sl = [rnext() for _ in range(4)]
                    for i, si in enumerate(sl):
                        DMA("pool", slot(ring, si).rearrange("p (c n) -> p c n", n=512), w_out_v[:, 4 * i:4 * i + 4, cg * 512:(cg + 1) * 512], writes=[ring.b[si]])
                    for s in range(4):
                        pi = pnext()
                        for kc in range(16):
                            si = sl[kc // 4]
                            mm(PB[pi][:, :], mT.t[:, kc, s * 128:(s + 1) * 128], slot(ring, si)[:, (kc % 4) * 512:(kc % 4 + 1) * 512], kc == 0, kc == 15, [mT.b[kc], ring.b[si]], [PBb[pi]])
                        xs = x1.t[:, s, cg * 512:(cg + 1) * 512]
                        DVE("tensor_tensor", [x1.b[s * 4 + cg], PBb[pi]], [x1.b[s * 4 + cg]], out=xs, in0=xs, in1=PB[pi][:, :], op=ALU.add)
                for s in range(4):
                    norm_tile(x1.t[:, s, :], x1.b[s * 4:(s + 1) * 4], xb, 16, hT, [hT.b[s]], s * 128)
                for fc in range(DFF // 128):
                    sgi = rnext()
                    DMA("pool", slot128(ring, sgi), w_gu_v[:, :, fc * 128:(fc + 1) * 128], writes=[ring.b[sgi]])
                    sui = rnext()
                    DMA("pool", slot128(ring, sui), w_gu_v[:, :, DFF + fc * 128:DFF + (fc + 1) * 128], writes=[ring.b[sui]])
                    g_b = pnext()
                    u_b = pnext()
                    wg = slot128(ring, sgi)
                    wu = slot128(ring, sui)
                    for kc in range(16):
                        mm(PB[g_b][:, :], wg[:, kc, :], hT.t[:, kc, :], kc == 0, kc == 15, [ring.b[sgi]] + hT.b, [PBb[g_b]])
                    for kc in range(16):
                        mm(PB[u_b][:, :], wu[:, kc, :], hT.t[:, kc, :], kc == 0, kc == 15, [ring.b[sui]] + hT.b, [PBb[u_b]])
                    k = fc % 2
                    ACT([PBb[g_b]], [sg.b[k]], out=sg.t[:, k, :], in_=PB[g_b][:, :], func=AF.Silu)
                    DVE("tensor_tensor", [sg.b[k], PBb[u_b]], [aT.b[fc]], out=aT.t[:, fc, :], in0=sg.t[:, k, :], in1=PB[u_b][:, :], op=ALU.mult)
                for fg in range(DFF // 512):
                    sl = [rnext() for _ in range(4)]
                    for i, si in enumerate(sl):
                        DMA("pool", slot(ring, si), w_dn_v[:, fg * 4 + i, :], writes=[ring.b[si]])
                    for s in range(4):
                        for cg in range(4):
                            pi = pnext()
                            for i, si in enumerate(sl):
                                mm(PB[pi][:, :], aT.t[:, fg * 4 + i, s * 128:(s + 1) * 128], slot(ring, si)[:, cg * 512:(cg + 1) * 512], i == 0, i == 3, [aT.b[fg * 4 + i], ring.b[si]], [PBb[pi]])
                            xs = x1.t[:, s, cg * 512:(cg + 1) * 512]
                            DVE("tensor_tensor", [x1.b[s * 4 + cg], PBb[pi]], [x1.b[s * 4 + cg]], out=xs, in0=xs, in1=PB[pi][:, :], op=ALU.add)
                for s in range(4):
                    DMA("sp", y_v[t * 4 + s], x1.t[:, s, :], reads=x1.b[s * 4:(s + 1) * 4], final=True)

        S.emit(st)
    return nc


def _t5_bucket(dist):
    n = np.maximum(dist, 0).astype(np.int32)
    nf = np.maximum(n, 1).astype(np.float32)
    large = 16 + (np.log(nf / np.float32(16)) / np.float32(math.log(128 / 16)) * np.float32(16)).astype(np.int32)
    large = np.minimum(large, 31)
    return np.where(n < 16, n, large)


def _host_consts(rel_bias, sinks, g):
    ta = rel_bias[:, :8]
    tb = rel_bias[:, 8:]
    negf = np.float32(NEG)
    kk = np.arange(256)[:, None]
    qq = np.arange(256)[None, :]
    d_prev = 256 + qq - kk
    d_own = qq - kk
    bk_prev = _t5_bucket(d_prev)
    bk_own = _t5_bucket(d_own)
    prev = np.stack([ta[bk_prev, h] for h in range(8)], 0)
    own = np.stack([np.where(d_own >= 0, ta[bk_own, h], negf) for h in range(8)], 0)
    zero = np.zeros_like(prev)
    negall = np.full_like(prev, negf)
    kinds = (prev, own, negall) if g == 0 else (zero, prev, own)
    biasA = np.stack(kinds, 0).astype(np.float32)
    biasA = biasA.reshape(3, 8, 2, 128, 256).transpose(0, 1, 3, 2, 4).reshape(3 * 8 * 128, 512)
    k1 = np.arange(128)[:, None]
    q1 = np.arange(128)[None, :]
    do = q1 - k1
    dp = 128 + q1 - k1
    bo = _t5_bucket(do)
    bp = _t5_bucket(dp)
    biasB = np.empty((3, 2, 128, 2, 4, 128), np.float32)
    for hp in range(2):
        for e in range(2):
            for c in range(4):
                h = 8 * hp + 2 * c + e
                o = np.where(do >= 0, tb[bo, h], negf)
                p = np.where(dp < 128, tb[bp, h], negf)
                biasB[0, hp, :, e, c, :] = o
                biasB[1, hp, :, e, c, :] = p
                biasB[2, hp, :, e, c, :] = p if g == 1 else negf
    biasB = biasB.reshape(3 * 2 * 128, 1024)
    mc = np.zeros((8, 4, 16), np.float32)
    for j in range(8):
        ownb = 2 * j + g
        n = np.arange(16)
        mc[j, 0] = np.where(n < ownb, 1e30, -1e30)
        mc[j, 1] = (n < ownb)
        mc[j, 2] = (n == ownb)
        mc[j, 3] = (n < ownb - 1)
    mconst = np.broadcast_to(mc.reshape(1, -1), (128, 8 * 4 * 16)).astype(np.float32).copy()
    return biasA, biasB, mconst


def _prep_inputs(x, norm1_g, w_in, q_norm_a, k_norm_a, q_norm_b, k_norm_b, rel_bias, sinks,
                 w_branch_a, w_branch_b, w_out, norm2_g, w_gate_up, w_down):
    f = lambda a: np.ascontiguousarray(np.asarray(a, dtype=np.float32))
    x = f(x)
    w_in0, w_ba, w_bb, w_o, w_gu, w_dn = f(w_in[0]), f(w_branch_a[0]), f(w_branch_b[0]), f(w_out[0]), f(w_gate_up[0]), f(w_down[0])
    rel_bias = f(rel_bias)
    sinks0 = f(sinks[0])
    vecs_base = np.zeros((128, NVEC), np.float32)
    vecs_base[:, 0:16] = f(norm1_g[0]).reshape(16, 128).T
    vecs_base[:, 16:32] = f(norm2_g[0]).reshape(16, 128).T
    vecs_base[:, 32] = f(q_norm_a[0])
    vecs_base[:, 33] = f(k_norm_a[0])
    vecs_base[:, 34] = np.tile(f(q_norm_b[0]), 2)
    vecs_base[:, 35] = np.tile(f(k_norm_b[0]), 2)
    vecs_base[:, 36:44] = rel_bias[31, 0:8][None, :]
    for hp in range(2):
        for c in range(4):
            vecs_base[0:64, 44 + 4 * hp + c] = sinks0[8 * hp + 2 * c]
            vecs_base[64:128, 44 + 4 * hp + c] = sinks0[8 * hp + 2 * c + 1]
    ident = np.eye(128, dtype=np.float32)
    onesbd = np.ones((128, 256), np.float32)
    bd = np.zeros((128, 128), np.float32)
    bd[:64, :64] = 1.0
    bd[64:, 64:] = 1.0
    onesbd[:, 128:] = bd
    indoh = np.zeros((128, 16, 128), np.float32)
    for n in range(16):
        indoh[n, n, :] = 1.0
    indoh = indoh.reshape(128, 16 * 128)
    consts = {g: _host_consts(rel_bias, sinks0, g) for g in range(2)}
    in_maps = []
    for c in range(8):
        b, g = c // 2, c % 2
        xb = x[b]
        xblk = xb.reshape(NB, LB, D)
        xo = np.ascontiguousarray(xblk[g::2].reshape(2048, D))
        xp = np.zeros((8, 128, D), np.float32)
        for j in range(8):
            ob = 2 * j + g
            if ob > 0:
                xp[j] = xb[ob * LB - 128:ob * LB]
        biasA, biasB, mconst = consts[g]
        in_maps.append({
            "xc": xb, "xo": xo, "xp": xp.reshape(8 * 128, D),
            "w_in": w_in0, "w_ba": w_ba, "w_bb": w_bb, "w_out": w_o, "w_gu": w_gu, "w_dn": w_dn,
            "vecs": vecs_base, "mconst": mconst, "ident": ident, "onesbd": onesbd, "indoh": indoh,
            "biasA": biasA, "biasB": biasB,
        })
    return in_maps


_NC_CACHE = {}


def kernel(**inputs):
    in_maps = _prep_inputs(**inputs)
    if "nc" not in _NC_CACHE:
        _NC_CACHE["nc"] = build_program()
    nc = _NC_CACHE["nc"]
    res = run_bass_kernel_spmd(nc, in_maps, core_ids=list(range(8)))
    out = np.empty((4, SEQ, D), np.float32)
    for c in range(8):
        b, g = c // 2, c % 2
        y = np.asarray(res.results[c]["y"], dtype=np.float32).reshape(8, LB, D)
        out[b].reshape(NB, LB, D)[g::2] = y
    return out
```

```python
import math
import numpy as np
from contextlib import ExitStack

import concourse.bass as bass
import concourse.mybir as mybir
from concourse.bass_utils import run_bass_kernel_spmd

F32 = mybir.dt.float32
BF16 = mybir.dt.bfloat16
U8 = mybir.dt.uint8
AF = mybir.ActivationFunctionType
ALU = mybir.AluOpType

D = 2048
SEQ = 4096
NB = 16
LB = 256
DFF = 5632
NEG = -30000.0
EPS = 1e-6
INW = 8448
C_QA, C_KA, C_VA, C_QB, C_KB, C_VB, C_GA, C_GB = 0, 1024, 2048, 3072, 4096, 4224, 4352, 6400
NVEC = 52

ENGINES = ("pe", "act", "dve", "pool", "sp")
SEM_EPOCH = 20000
NDMA_SEMS = 8


class Buf:
    __slots__ = ("name", "last_w", "readers", "dma_readers")

    def __init__(self, name):
        self.name = name
        self.last_w = None
        self.readers = {}
        self.dma_readers = []


class Op:
    __slots__ = ("eng", "fn", "deps", "is_dma", "ordinal", "sig", "dma_slot", "dma_val")

    def __init__(self, eng, fn, is_dma):
        self.eng = eng
        self.fn = fn
        self.deps = []
        self.is_dma = is_dma
        self.ordinal = None
        self.sig = False
        self.dma_slot = None
        self.dma_val = None


class Sched:
    def __init__(self, nc):
        self.nc = nc
        self.ops = {e: [] for e in ENGINES}
        self.ndma = {e: 0 for e in ENGINES}
        self.final_dmas = []
        self.recent_dmas = {e: [] for e in ENGINES}
        self.pending_bar = {e: None for e in ENGINES}

    def _record(self, op, reads, writes):
        deps = {}

        def add(d):
            if d is None or d is op:
                return
            deps[id(d)] = d

        bar = self.pending_bar[op.eng]
        if bar is not None:
            for d in bar:
                add(d)
            self.pending_bar[op.eng] = None
        for b in reads:
            add(b.last_w)
        for b in writes:
            add(b.last_w)
            for r in b.readers.values():
                add(r)
            for r in b.dma_readers:
                add(r)
        for b in reads:
            if op.is_dma:
                b.dma_readers.append(op)
            else:
                b.readers[op.eng] = op
        for b in writes:
            b.last_w = op
            b.readers = {}
            b.dma_readers = []
        for d in deps.values():
            if (not d.is_dma) and (not op.is_dma) and d.eng == op.eng and op.eng == "pe":
                continue
            op.deps.append(d)
            if not d.is_dma:
                d.sig = True
        self.ops[op.eng].append(op)
        return op

    def op(self, eng, fn, reads=(), writes=()):
        return self._record(Op(eng, fn, False), reads, writes)

    def dma(self, queue, fn, reads=(), writes=(), final=False):
        o = Op(queue, fn, True)
        n = self.ndma[queue]
        self.ndma[queue] = n + 1
        o.dma_slot = n % NDMA_SEMS
        o.dma_val = 16 * (n // NDMA_SEMS + 1)
        self._record(o, reads, writes)
        rd = self.recent_dmas[queue]
        rd.append(o)
        if len(rd) > NDMA_SEMS:
            rd.pop(0)
        if final:
            self.final_dmas.append(o)
        return o

    def barrier(self):
        deps = []
        for e in ENGINES:
            last = None
            for o in reversed(self.ops[e]):
                if not o.is_dma:
                    last = o
                    break
            if last is not None:
                deps.append(last)
            deps.extend(self.recent_dmas[e])
        for e in ENGINES:
            prev = self.pending_bar[e]
            self.pending_bar[e] = list(deps) + (prev or [])

    def emit(self, stack):
        nc = self.nc
        nsig = {}
        for e in ENGINES:
            k = 0
            for o in self.ops[e]:
                if (not o.is_dma) and o.sig:
                    o.ordinal = k
                    k += 1
            nsig[e] = k
        csems = {}
        for e in ENGINES:
            n_ep = (nsig[e] + SEM_EPOCH - 1) // SEM_EPOCH
            csems[e] = [stack.enter_context(nc.semaphore(f"c_{e}_{i}")) for i in range(n_ep)]
        dsems = {}
        for e in ENGINES:
            if self.ndma[e]:
                dsems[e] = [stack.enter_context(nc.semaphore(f"d_{e}_{i}")) for i in range(NDMA_SEMS)]

        def target(d):
            if d.is_dma:
                return dsems[d.eng][d.dma_slot], d.dma_val
            return csems[d.eng][d.ordinal // SEM_EPOCH], d.ordinal % SEM_EPOCH + 1

        block = stack.enter_context(nc.Block())
        engmap = {"pe": "tensor", "act": "scalar", "dve": "vector", "pool": "gpsimd", "sp": "sync"}

        def body(ename, eng):
            waited = {}

            def wait(sem, val):
                key = id(sem)
                if waited.get(key, 0) >= val:
                    return
                waited[key] = val
                eng.wait_ge(sem, val)

            for o in self.ops[ename]:
                if o.is_dma and o.dma_val > 16:
                    wait(dsems[ename][o.dma_slot], o.dma_val - 16)
                for d in o.deps:
                    s, v = target(d)
                    wait(s, v)
                inst = o.fn(eng)
                if o.is_dma:
                    inst.then_inc(dsems[ename][o.dma_slot], 16)
                elif o.sig:
                    inst.then_inc(csems[ename][o.ordinal // SEM_EPOCH], 1)
            if ename == "sp":
                for d in self.final_dmas:
                    s, v = target(d)
                    wait(s, v)

        for ename in ENGINES:
            getattr(block, engmap[ename])(lambda eng, _n=ename: body(_n, eng))


class Tile:
    def __init__(self, t, bufs):
        self.t = t
        self.b = bufs


def build_program(debug=False):
    nc = bass.Bass("TRN2", target_bir_lowering=False)

    def din(name, shape):
        return nc.dram_tensor(name, list(shape), F32, kind="ExternalInput").ap()

    xc = din("xc", [SEQ, D])
    xo = din("xo", [2048, D])
    xp = din("xp", [8 * 128, D])
    w_in = din("w_in", [D, INW])
    w_ba = din("w_ba", [1024, D])
    w_bb = din("w_bb", [1024, D])
    w_out = din("w_out", [D, D])
    w_gu = din("w_gu", [D, 2 * DFF])
    w_dn = din("w_dn", [DFF, D])
    vecs_d = din("vecs", [128, NVEC])
    mconst_d = din("mconst", [128, 8 * 4 * 16])
    ident_d = din("ident", [128, 128])
    ones_d = din("onesbd", [128, 256])
    indoh_d = din("indoh", [128, 16 * 128])
    biasA_d = din("biasA", [3 * 8 * 128, 512])
    biasB_d = din("biasB", [3 * 2 * 128, 1024])
    y_d = nc.dram_tensor("y", [2048, D], F32, kind="ExternalOutput").ap()
    dbg = {}
    if debug:
        dbg["yab"] = nc.dram_tensor("dbg_yab", [128, 16 * 1024], F32, kind="ExternalOutput").ap()

    w_in_v = w_in.rearrange("(c p) n -> p c n", p=128)
    w_ba_v = w_ba.rearrange("(c p) n -> p c n", p=128)
    w_bb_v = w_bb.rearrange("(c p) n -> p c n", p=128)
    w_out_v = w_out.rearrange("(c p) n -> p c n", p=128)
    w_gu_v = w_gu.rearrange("(c p) n -> p c n", p=128)
    w_dn_v = w_dn.rearrange("(c p) n -> p c n", p=128)
    xc_v = xc.rearrange("(t p) d -> t p d", p=128)
    xo_v = xo.rearrange("(t p) d -> t p d", p=128)
    xp_v = xp.rearrange("(t p) d -> t p d", p=128)
    y_v = y_d.rearrange("(t p) d -> t p d", p=128)
    biasA_v = biasA_d.rearrange("(i p) n -> i p n", p=128)
    biasB_v = biasB_d.rearrange("(i p) n -> i p n", p=128)

    st = ExitStack()
    with st:
        S = Sched(nc)
        ARENA = 212800
        st.enter_context(nc.sbuf_tensor("arena", [128, ARENA], U8))
        abase = nc.sbuf_base - ARENA
        uid = [0]

        class Region:
            def __init__(self, lo, hi):
                self.lo, self.hi, self.p = lo, hi, lo

            def alloc(self, name, shape, dt, nbufs=1, parts=None):
                esz = 2 if dt == BF16 else 4
                nbytes = int(np.prod(shape[1:])) * esz
                nbytes = (nbytes + 31) // 32 * 32
                off = self.p
                self.p += nbytes
                assert self.p <= self.hi, (name, self.p, self.hi)
                uid[0] += 1
                t = nc.alloc_sbuf_tensor_at(f"{name}_{uid[0]}", list(shape), dt, offset=abase + off)
                return Tile(t, [Buf(f"{name}{i}") for i in range(nbufs)])

            def rest(self):
                return Region(self.p, self.hi)

        top = Region(0, ARENA)

        PB = [st.enter_context(nc.psum_tensor(f"pb{i}", [128, 512], F32)) for i in range(6)]
        PBb = [Buf(f"pb{i}") for i in range(6)]
        TP = [st.enter_context(nc.psum_tensor(f"tp{i}", [128, 8, 128], BF16)) for i in range(2)]
        TPb = [Buf(f"tp{i}") for i in range(2)]

        vecs = top.alloc("vecs", [128, NVEC], F32)
        mconst = top.alloc("mconst", [128, 8, 4, 16], F32)
        ident = top.alloc("ident", [128, 128], BF16)
        ones_bf = top.alloc("ones_bf", [128, 128], BF16)
        onesf = top.alloc("onesf", [128, 256], F32)
        indoh = top.alloc("indoh", [128, 16, 128], BF16)
        gv = top.alloc("gv", [128, 4], F32)
        expsink = top.alloc("expsink", [128, 8], F32)
        fb = top.alloc("fb", [128, 8, 8, 16], F32)
        ssq = top.alloc("ssq", [128, 4], F32, nbufs=2)
        yab = top.alloc("yab", [128, 16, 1024], BF16, nbufs=16 * 4)

        def yab_b(chunk, blk):
            return yab.b[chunk * 4 + blk]

        def ACT(reads, writes, **kw):
            S.op("act", lambda e: e.activation(**kw), reads, writes)

        def DVE(meth, reads, writes, **kw):
            S.op("dve", lambda e: getattr(e, meth)(**kw), reads, writes)

        def TR(reads, writes, **kw):
            S.op("pe", lambda e: e.transpose(**kw), reads, writes)

        def DMA(q, out, in_, reads=(), writes=(), final=False):
            S.dma(q, lambda e: e.dma_start(out=out, in_=in_), reads, writes, final)

        def mm(out, lhsT, rhs, start, stop, reads, writes, **kw):
            S.op("pe", lambda e: e.matmul(out, lhsT=lhsT, rhs=rhs, start=start, stop=stop, **kw), reads, writes)

        def bc(ap2d, n):
            a = ap2d.shape[1]
            return ap2d.rearrange("p (a b) -> p a b", b=1).broadcast_to([128, a, n])

        DMA("sp", vecs.t[:], vecs_d[:, :], writes=vecs.b)
        DMA("sp", mconst.t[:].rearrange("p a b c -> p (a b c)"), mconst_d[:, :], writes=mconst.b)
        DMA("sp", onesf.t[:], ones_d[:, :], writes=onesf.b)
        DMA("pool", ident.t[:], ident_d[:, :], writes=ident.b)
        DMA("pool", ones_bf.t[:], ones_d[:, 0:128], writes=ones_bf.b)
        DMA("pool", indoh.t[:].rearrange("p a b -> p (a b)"), indoh_d[:, :], writes=indoh.b)
        DVE("tensor_scalar", vecs.b, gv.b, out=gv.t[:, 0:1], in0=vecs.t[:, 32:33], scalar1=128.0 ** -0.5, scalar2=None, op0=ALU.mult)
        DVE("tensor_copy", vecs.b, gv.b, out=gv.t[:, 1:2], in_=vecs.t[:, 33:34])
        DVE("tensor_scalar", vecs.b, gv.b, out=gv.t[:, 2:3], in0=vecs.t[:, 34:35], scalar1=64.0 ** -0.5, scalar2=None, op0=ALU.mult)
        DVE("tensor_copy", vecs.b, gv.b, out=gv.t[:, 3:4], in_=vecs.t[:, 35:36])
        ACT(vecs.b, expsink.b, out=expsink.t[:], in_=vecs.t[:, 44:52], func=AF.Exp)
        for h in range(8):
            DVE("tensor_scalar", vecs.b + mconst.b, fb.b, out=fb.t[:, :, h, :], in0=mconst.t[:, :, 3, :], scalar1=vecs.t[:, 36 + h:37 + h], scalar2=None, op0=ALU.mult)

        tp_rr = [0]
        ssq_rr = [0]

        def norm_front(x_ap, x_bufs, xb_ap, xb_bufs, junk=None):
            k = ssq_rr[0] % 2
            ssq_rr[0] += 1
            sb = [ssq.b[k]]
            s0 = ssq.t[:, 2 * k:2 * k + 1]
            s1 = ssq.t[:, 2 * k + 1:2 * k + 2]
            if junk is None:
                ACT(list(x_bufs), list(xb_bufs) + sb, out=xb_ap, in_=x_ap, func=AF.Square, accum_out=s0)
            else:
                ACT(list(x_bufs), junk.b + sb, out=junk.t[:], in_=x_ap, func=AF.Square, accum_out=s0)
            ACT(sb, sb, out=s1, in_=s0, func=AF.Ln, scale=1.0 / D, bias=EPS)
            ACT(sb, sb, out=s1, in_=s1, func=AF.Exp, scale=-0.5)
            DVE("tensor_scalar", list(x_bufs) + sb, list(xb_bufs), out=xb_ap, in0=x_ap, scalar1=s1, scalar2=None, op0=ALU.mult)

        def norm_back(xb_ap, xb_bufs, gcol0, hT, hT_bufs, tok0):
            for half in range(2):
                ti = tp_rr[0] % 2
                tp_rr[0] += 1
                for kc in range(8):
                    c = half * 8 + kc
                    TR(list(xb_bufs) + ident.b, [TPb[ti]], out=TP[ti][:, kc, :], in_=xb_ap[:, c * 128:(c + 1) * 128], identity=ident.t[:])
                DVE("tensor_tensor", [TPb[ti]] + vecs.b, list(hT_bufs), out=hT.t[:, half * 8:(half + 1) * 8, tok0:tok0 + 128], in0=TP[ti][:],
                    in1=bc(vecs.t[:, gcol0 + half * 8:gcol0 + half * 8 + 8], 128), op=ALU.mult)

        def norm_tile(x_ap, x_bufs, xb, gcol0, hT, hT_bufs, tok0):
            norm_front(x_ap, x_bufs, xb.t[:], xb.b)
            norm_back(xb.t[:], xb.b, gcol0, hT, hT_bufs, tok0)

        def slot(ring, i):
            return ring.t[:, i * 2048:(i + 1) * 2048]

        def slot128(ring, i):
            return slot(ring, i).rearrange("p (c n) -> p c n", n=128)

        class Pipe:
            def __init__(self):
                self.pend = None
                self.bg = []

            def after_x(self, tail):
                if self.pend is not None:
                    self.pend()
                self.pend = tail
                if self.bg:
                    self.bg.pop(0)()

            def flush(self):
                if self.pend is not None:
                    self.pend()
                    self.pend = None

            def drain_bg(self):
                while self.bg:
                    self.bg.pop(0)()

        def qk_tail(pj_ap, pjb, N, sq, rt, ssb, ones_ap, invd, gcol, out_ap, obufs, accs=None):
            def run():
                mm(PB[ssb][:, 0:N], ones_ap, sq.t[:, 0:N], True, True, sq.b + onesf.b, [PBb[ssb]])
                ACT([PBb[ssb]], rt.b, out=rt.t[:, 0:N], in_=PB[ssb][:, 0:N], func=AF.Ln, scale=invd, bias=EPS)
                ACT(rt.b, rt.b, out=rt.t[:, 0:N], in_=rt.t[:, 0:N], func=AF.Exp, scale=-0.5)
                if accs is None:
                    DVE("scalar_tensor_tensor", [pjb] + gv.b + rt.b, list(obufs), out=out_ap, in0=pj_ap, scalar=gv.t[:, gcol:gcol + 1],
                        in1=rt.t[:, 0:N], op0=ALU.mult, op1=ALU.mult)
                else:
                    for (o_ap, c0, c1, acc_ap, abufs) in accs:
                        DVE("scalar_tensor_tensor", [pjb] + gv.b + rt.b, list(obufs) + list(abufs), out=o_ap, in0=pj_ap[:, c0:c1], scalar=gv.t[:, gcol:gcol + 1],
                            in1=rt.t[:, c0:c1], op0=ALU.mult, op1=ALU.mult, accum_out=acc_ap)
            return run

        wtab_d = nc.dram_tensor("wtab", [128, 18432], BF16, kind="Internal").ap()
        wtab_b = Buf("wtab")
        P0 = top.rest()
        stA = P0.alloc("stA", [128, 8, 3, 512], F32, nbufs=2)
        stB = P0.alloc("stB", [128, 2, 3, 1024], F32)
        wA0 = P0.alloc("wA0", [128, 12288], BF16, nbufs=2)
        wB0 = P0.alloc("wB0", [128, 6144], BF16)
        bA4 = biasA_d.rearrange("(i h p) n -> p h i n", i=3, h=8, p=128)
        bB4 = biasB_d.rearrange("(k h p) n -> p h k n", k=3, h=2, p=128)
        for hh in range(2):
            for i in range(3):
                DMA("sp", stA.t[:, 4 * hh:4 * hh + 4, i, :], bA4[:, 4 * hh:4 * hh + 4, i, :], writes=[stA.b[hh]])
        for k in range(3):
            DMA("sp", stB.t[:, :, k, :], bB4[:, :, k, :], writes=stB.b)
        for hh in range(2):
            ACT([stA.b[hh]], [wA0.b[hh]], out=wA0.t[:, 6144 * hh:6144 * (hh + 1)], in_=stA.t[:, 4 * hh:4 * hh + 4, :, :].rearrange("p h i n -> p (h i n)"), func=AF.Exp)
        ACT(stB.b, wB0.b, out=wB0.t[:], in_=stB.t[:].rearrange("p h k n -> p (h k n)"), func=AF.Exp)
        for hh in range(2):
            DMA("sp", wtab_d[:, 6144 * hh:6144 * (hh + 1)], wA0.t[:, 6144 * hh:6144 * (hh + 1)], reads=[wA0.b[hh]], writes=[wtab_b])
        DMA("sp", wtab_d[:, 12288:18432], wB0.t[:], reads=wB0.b, writes=[wtab_b])

        for th in range(2):
            for hp in range(2):
                S.barrier()
                hpR = top.rest()
                kAT = hpR.alloc("kAT", [128, 4, SEQ], BF16, nbufs=4 * 8)
                vA = hpR.alloc("vA", [128, 32, 512], BF16, nbufs=8)
                ksum = hpR.alloc("ksum", [128, 4, 16], F32)
                kmT = hpR.alloc("kmT", [128, 4, 16], BF16)
                A = hpR.rest()
                ring = A.alloc("ringA", [128, 8 * 2048], BF16, nbufs=8)
                xt = A.alloc("xtA", [128, 2, D], F32, nbufs=8)
                xb2 = A.alloc("xbA", [128, 2, D], BF16, nbufs=2)
                hTc2 = [A.alloc(f"hTc{i}", [128, 16, 512], BF16, nbufs=4) for i in range(2)]
                sq2 = [A.alloc(f"sqA{i}", [128, 512], F32) for i in range(2)]
                rtA = A.alloc("rtA", [128, 512], F32)
                rt2 = [rtA, rtA]
                junk = None
                DVE("memset", [], ksum.b, ap=ksum.t[:], constant=0.0)
                for hl in range(4):
                    c0 = C_KA + (4 * hp + hl) * 128
                    DMA("pool", slot128(ring, hl), w_in_v[:, :, c0:c0 + 128], writes=[ring.b[hl]])
                c0 = C_VA + hp * 512
                wv_view = ring.t[:, 4 * 2048:8 * 2048].rearrange("p (c n) -> p c n", n=512)
                DMA("pool", wv_view, w_in_v[:, :, c0:c0 + 512], writes=ring.b[4:8])
                n_ct = 4 if th == 0 else 8
                G = n_ct * 4

                def ev(g, xt=xt, xb2=xb2, hTc2=hTc2, junk=junk, G=G):
                    gb = g - 2
                    if 0 <= gb < G:
                        k = gb % 2
                        hT = hTc2[(gb // 4) % 2]
                        norm_back(xb2.t[:, k, :], [xb2.b[k]], 0, hT, [hT.b[gb % 4]], (gb % 4) * 128)
                    if g < G:
                        k = g % 2
                        for q4 in range(4):
                            DMA("sp", xt.t[:, k, q4 * 512:(q4 + 1) * 512], xc_v[g][:, q4 * 512:(q4 + 1) * 512], writes=[xt.b[k * 4 + q4]])
                        norm_front(xt.t[:, k, :], xt.b[k * 4:k * 4 + 4], xb2.t[:, k, :], [xb2.b[k]], junk=junk)

                for g in range(6):
                    ev(g)
                nxt = [6]
                nb = [0]

                def boundary(nxt=nxt, nb=nb, G=G):
                    if nb[0] % 2 == 0 and nxt[0] < G + 2:
                        ev(nxt[0])
                        nxt[0] += 1
                    nb[0] += 1

                pipe = Pipe()
                pj_rr = 0
                pv_rr = 0
                yi = 0
                for ct in range(n_ct):
                    hTc = hTc2[ct % 2]
                    for hl in range(4):
                        boundary()
                        pi = pj_rr % 2
                        pj_rr += 1
                        wk = slot128(ring, hl)
                        for kc in range(16):
                            mm(PB[pi][:, 0:512], wk[:, kc, :], hTc.t[:, kc, :], kc == 0, kc == 15, [ring.b[hl]] + hTc.b, [PBb[pi]])
                        sq, rt = sq2[yi % 2], rt2[yi % 2]
                        ssb = (2, 5)[yi % 2]
                        yi += 1
                        ACT([PBb[pi]], sq.b, out=sq.t[:, 0:512], in_=PB[pi][:, 0:512], func=AF.Square)
                        accs = []
                        for bb in range(2):
                            blk = ct * 2 + bb
                            accs.append((kAT.t[:, hl, ct * 512 + bb * 256:ct * 512 + (bb + 1) * 256], bb * 256, (bb + 1) * 256, ksum.t[:, hl, blk:blk + 1], ksum.b))
                        pipe.after_x(qk_tail(PB[pi], PBb[pi], 512, sq, rt, ssb, onesf.t[:, 0:128], 1.0 / 128, 1, None, [kAT.b[hl * 8 + ct]], accs))
                    for s in range(4):
                        boundary()
                        pi = 3 + pv_rr % 2
                        pv_rr += 1
                        for kc in range(16):
                            mm(PB[pi][:, 0:512], hTc.t[:, kc, s * 128:(s + 1) * 128], wv_view[:, kc, :], kc == 0, kc == 15, [hTc.b[s]] + ring.b[4:8], [PBb[pi]])
                        ACT([PBb[pi]], [vA.b[ct]], out=vA.t[:, ct * 4 + s, :], in_=PB[pi][:, 0:512], func=AF.Copy)
                        if s == 0:
                            pipe.after_x(None)
                while nxt[0] < G + 2:
                    ev(nxt[0])
                    nxt[0] += 1
                pipe.flush()
                DVE("tensor_scalar", ksum.b, kmT.b, out=kmT.t[:], in0=ksum.t[:], scalar1=1.0 / 256, scalar2=None, op0=ALU.mult)

                S.barrier()
                Bp = hpR.rest()
                WA = Bp.alloc("WA", [128, 4, 3, 512], BF16)
                WB = Bp.alloc("WB", [128, 3, 1024], BF16)
                QAT = Bp.alloc("QAT", [128, 2, 4, 256], BF16, nbufs=4)
                QBT = Bp.alloc("QBT", [128, 2, 4, 256], BF16, nbufs=4)
                kB2T = Bp.alloc("kB2T", [128, 2, 384], BF16, nbufs=2)
                vB = Bp.alloc("vB", [128, 2, 3, 64], BF16, nbufs=2)
                MT = Bp.alloc("MT", [128, 2, 4, 256], BF16, nbufs=2)
                B1 = Bp.rest()
                ring = B1.alloc("ringB", [128, 4 * 2048], BF16, nbufs=4)
                xt = B1.alloc("xtB", [128, D], F32, nbufs=4)
                xb2 = B1.alloc("xbB", [128, 2, D], BF16, nbufs=2)
                hTo2 = [B1.alloc(f"hTo{i}", [128, 16, 384], BF16, nbufs=3) for i in range(2)]
                sq2 = [B1.alloc(f"sqB{i}", [128, 512], F32) for i in range(2)]
                rtB = B1.alloc("rtB", [128, 512], F32)
                rt2 = [rtB, rtB]
                Mbf = B1.alloc("Mbf", [128, 8, 128], BF16)
                gm = B1.alloc("gm", [128, 8, 16], F32)
                t8 = B1.alloc("t8", [128, 8, 8], F32)
                selt = B1.alloc("selt", [128, 8, 16], F32)
                m1t = B1.alloc("m1t", [128, 8, 16], F32)
                junkB = None
                B2 = Bp.rest()
                Pm = B2.alloc("Pm", [128, 3, 512], BF16, nbufs=3)
                Eb = B2.alloc("Eb", [128, 2, 2, 512], BF16, nbufs=4)
                Ps = B2.alloc("Ps", [128, 2, 2, 512], BF16, nbufs=4)
                rden = B2.alloc("rden", [128, 2, 512], F32, nbufs=2)
                DMA("sp", WA.t[:].rearrange("p a b c -> p (a b c)"), wtab_d[:, 6144 * hp:6144 * (hp + 1)], reads=[wtab_b], writes=WA.b)
                DMA("sp", WB.t[:].rearrange("p a b -> p (a b)"), wtab_d[:, 12288 + 3072 * hp:12288 + 3072 * (hp + 1)], reads=[wtab_b], writes=WB.b)
                DVE("memset", [], Mbf.b, ap=Mbf.t[:], constant=0.0)
                ring_rr = [0]
                xcnt = [0]

                def ring_next(ring_rr=ring_rr):
                    i = ring_rr[0] % 4
                    ring_rr[0] += 1
                    return i

                for rnd in range(2):
                    def b_tasks(bi, rnd=rnd, xt=xt, xb2=xb2, hTo2=hTo2, xcnt=xcnt):
                        j = 4 * th + 2 * rnd + bi
                        ks = []

                        def F(r):
                            k = xcnt[0] % 2
                            xcnt[0] += 1
                            ks.append(k)
                            src_ap = xp_v[j] if r == 0 else xo_v[j * 2 + r - 1]
                            for q4 in range(4):
                                DMA("sp", xt.t[:, q4 * 512:(q4 + 1) * 512], src_ap[:, q4 * 512:(q4 + 1) * 512], writes=[xt.b[q4]])
                            norm_front(xt.t[:], xt.b, xb2.t[:, k, :], [xb2.b[k]], junk=junkB)

                        def Bk(r):
                            k = ks[r]
                            hT = hTo2[bi]
                            norm_back(xb2.t[:, k, :], [xb2.b[k]], 0, hT, [hT.b[r]], r * 128)

                        return [lambda: F(0), lambda: F(1), lambda: (Bk(0), F(2)), lambda: Bk(1), lambda: Bk(2)]

                    pipe = Pipe()
                    pipe.bg = b_tasks(0)
                    pipe.drain_bg()
                    pj_rr = 0
                    yi = 0
                    pend_gate = None
                    for bi in range(2):
                        j = 4 * th + 2 * rnd + bi
                        hTo = hTo2[bi]
                        if bi == 0:
                            pipe.bg = b_tasks(1)
                        projs = [("qa", 0), ("qa", 1), ("qb", 0), ("qb", 1), ("kb", 0), ("vb", 0)]
                        for pidx, (kind, p) in enumerate(projs):
                            pi = pj_rr % 3
                            pj_rr += 1
                            if kind in ("qa", "qb"):
                                for hh in range(2):
                                    si = ring_next()
                                    if kind == "qa":
                                        c0 = C_QA + (4 * hp + 2 * p + hh) * 128
                                    else:
                                        c0 = C_QB + 512 * hp + 128 * (2 * p + hh)
                                    wq = slot128(ring, si)
                                    DMA("pool", wq, w_in_v[:, :, c0:c0 + 128], writes=[ring.b[si]])
                                    for kc in range(16):
                                        mm(PB[pi][:, hh * 256:(hh + 1) * 256], wq[:, kc, :], hTo.t[:, kc, 128:384], kc == 0, kc == 15, [ring.b[si]] + hTo.b[1:3], [PBb[pi]])
                                sq, rt = sq2[yi % 2], rt2[yi % 2]
                                ssb = (3, 4)[yi % 2]
                                yi += 1
                                ACT([PBb[pi]], sq.b, out=sq.t[:, 0:512], in_=PB[pi][:, 0:512], func=AF.Square)
                                if kind == "qa":
                                    tail = qk_tail(PB[pi][:, 0:512], PBb[pi], 512, sq, rt, ssb, onesf.t[:, 0:128], 1.0 / 128, 0,
                                                   QAT.t[:, bi, 2 * p:2 * p + 2, :].rearrange("p a q -> p (a q)"), [QAT.b[bi * 2 + p]])
                                else:
                                    tail = qk_tail(PB[pi][:, 0:512], PBb[pi], 512, sq, rt, ssb, onesf.t[:, 128:256], 1.0 / 64, 2,
                                                   QBT.t[:, bi, 2 * p:2 * p + 2, :].rearrange("p a q -> p (a q)"), [QBT.b[bi * 2 + p]])
                                pipe.after_x(tail)
                            elif kind == "kb":
                                si = ring_next()
                                c0 = C_KB + 64 * hp
                                kview = slot128(ring, si)
                                DMA("pool", kview[:, :, 0:64], w_in_v[:, :, c0:c0 + 64], writes=[ring.b[si]])
                                DMA("pool", kview[:, :, 64:128], w_in_v[:, :, c0:c0 + 64], writes=[ring.b[si]])
                                for kc in range(16):
                                    mm(PB[pi][:, 0:384], kview[:, kc, :], hTo.t[:, kc, 0:384], kc == 0, kc == 15, [ring.b[si]] + hTo.b, [PBb[pi]])
                                sq, rt = sq2[yi % 2], rt2[yi % 2]
                                ssb = (3, 4)[yi % 2]
                                yi += 1
                                ACT([PBb[pi]], sq.b, out=sq.t[:, 0:384], in_=PB[pi][:, 0:384], func=AF.Square)
                                pipe.after_x(qk_tail(PB[pi][:, 0:384], PBb[pi], 384, sq, rt, ssb, onesf.t[:, 128:256], 1.0 / 64, 3, kB2T.t[:, bi, :], [kB2T.b[bi]]))
                            else:
                                si = ring_next()
                                c0 = C_VB + 64 * hp
                                vview = slot(ring, si)[:, 0:1024].rearrange("p (c n) -> p c n", n=64)
                                DMA("pool", vview, w_in_v[:, :, c0:c0 + 64], writes=[ring.b[si]])
                                for r in range(3):
                                    for kc in range(16):
                                        mm(PB[pi][:, r * 64:(r + 1) * 64], hTo.t[:, kc, r * 128:(r + 1) * 128], vview[:, kc, :], kc == 0, kc == 15, [ring.b[si], hTo.b[r]], [PBb[pi]])
                                ACT([PBb[pi]], [vB.b[bi]], out=vB.t[:, bi, :, :].rearrange("p r d -> p (r d)"), in_=PB[pi][:, 0:192], func=AF.Copy)
                                pipe.after_x(None)
                            if pidx == 1 and pend_gate is not None:
                                pend_gate()
                                pend_gate = None
                        pipe.flush()
                        def gate_front(bi=bi, j=j):
                            for hl in range(4):
                                for qs in range(2):
                                    i = hl * 2 + qs
                                    mm(PB[5][:, i * 16:(i + 1) * 16], QAT.t[:, bi, hl, qs * 128:(qs + 1) * 128], kmT.t[:, hl, :], True, True, [QAT.b[bi * 2 + hl // 2]] + kmT.b, [PBb[5]])
                            g3 = PB[5][:, 0:128].rearrange("p (i n) -> p i n", n=16)
                            DVE("tensor_tensor", [PBb[5]] + mconst.b, gm.b, out=gm.t[:], in0=g3, in1=mconst.t[:, j, 0:1, :].broadcast_to([128, 8, 16]), op=ALU.min)
                            for i in range(8):
                                DVE("max", gm.b, t8.b, out=t8.t[:, i, :], in_=gm.t[:, i, :])
                            DVE("tensor_tensor", gm.b + t8.b, selt.b, out=selt.t[:], in0=gm.t[:], in1=t8.t[:, :, 2:3].broadcast_to([128, 8, 16]), op=ALU.is_ge)
                            DVE("tensor_tensor", selt.b + mconst.b, selt.b, out=selt.t[:], in0=selt.t[:], in1=mconst.t[:, j, 1:2, :].broadcast_to([128, 8, 16]), op=ALU.mult)
                            DVE("tensor_tensor", selt.b + mconst.b, selt.b, out=selt.t[:], in0=selt.t[:], in1=mconst.t[:, j, 2:3, :].broadcast_to([128, 8, 16]), op=ALU.add)
                            fb4 = fb.t[:, j, 4 * hp:4 * hp + 4, :].rearrange("p h (o n) -> p h o n", o=1).broadcast_to([128, 4, 2, 16])
                            DVE("tensor_tensor", selt.b + fb.b, m1t.b, out=m1t.t[:].rearrange("p (h o) n -> p h o n", o=2), in0=selt.t[:].rearrange("p (h o) n -> p h o n", o=2), in1=fb4, op=ALU.mult)
                            DVE("tensor_scalar", selt.b, selt.b, out=selt.t[:], in0=selt.t[:], scalar1=-NEG, scalar2=NEG, op0=ALU.mult, op1=ALU.add)
                            DVE("tensor_tensor", selt.b + m1t.b, Mbf.b, out=Mbf.t[:, :, 0:16], in0=selt.t[:], in1=m1t.t[:], op=ALU.add)

                        def gate_back(bi=bi, Mbf=Mbf, MT=MT):
                            for i in range(8):
                                TR(Mbf.b + ident.b, [TPb[1]], out=TP[1][:, i, :], in_=Mbf.t[:, i, :], identity=ident.t[:])
                            ACT([TPb[1]], [MT.b[bi]], out=MT.t[:, bi, :, :].rearrange("p h q -> p (h q)"), in_=TP[1][:].rearrange("p i q -> p (i q)"), func=AF.Copy)

                        if bi == 0:
                            gate_front()
                            pend_gate = gate_back
                        else:
                            late_gate = (gate_front, gate_back)
                    pipe.drain_bg()
                    if pend_gate is not None:
                        pend_gate()
                        pend_gate = None

                    S.barrier()
                    late_gate[0]()
                    units = []
                    s_rr = [0]
                    p_rr = [0]
                    for bi in range(2):
                        jl = 2 * rnd + bi
                        j = 4 * th + jl
                        tok0 = jl * 256
                        for s in range(2):
                            for ri in range(2):
                                def mk_swa(bi=bi, j=j, jl=jl, tok0=tok0, s=s, ri=ri):
                                    r = s + ri
                                    kind = 1 if ri == 0 else 0
                                    if j == 0 and s == 0 and ri == 0:
                                        kind = 2
                                    it = s * 2 + ri
                                    sb0 = (0, 2)[it % 2]
                                    ek = it % 2

                                    def qk():
                                        for e_ in range(2):
                                            mm(PB[sb0 + e_][:, 0:512].rearrange("p (c q) -> p c q", c=4), kB2T.t[64 * e_:64 * e_ + 64, bi, r * 128:(r + 1) * 128],
                                               QBT.t[64 * e_:64 * e_ + 64, bi, :, s * 128:(s + 1) * 128], True, True, [kB2T.b[bi]] + QBT.b[bi * 2:bi * 2 + 2], [PBb[sb0 + e_]])

                                    def rest():
                                        for e_ in range(2):
                                            ACT([PBb[sb0 + e_]], [Eb.b[ek * 2 + e_]], out=Eb.t[:, ek, e_, :], in_=PB[sb0 + e_][:, 0:512], func=AF.Exp)
                                        for e_ in range(2):
                                            DVE("tensor_tensor", [Eb.b[ek * 2 + e_]] + WB.b, [Ps.b[ek * 2 + e_]], out=Ps.t[:, ek, e_, :], in0=Eb.t[:, ek, e_, :], in1=WB.t[:, kind, 512 * e_:512 * (e_ + 1)], op=ALU.mult)
                                        for e_ in range(2):
                                            mm(PB[4][64 * e_:64 * e_ + 64, 0:512], vB.t[:, bi, r, :], Ps.t[:, ek, e_, :], ri == 0, ri == 1, [vB.b[bi], Ps.b[ek * 2 + e_]], [PBb[4]])
                                            mm(PB[5][64 * e_:64 * e_ + 64, 0:512], ones_bf.t[:, 0:64], Ps.t[:, ek, e_, :], ri == 0, ri == 1, ones_bf.b + [Ps.b[ek * 2 + e_]], [PBb[5]])
                                        if ri == 1:
                                            rk = s % 2
                                            r4 = rden.t[:, rk, :].rearrange("p (c q) -> p c q", c=4)
                                            DVE("tensor_tensor", [PBb[5]] + expsink.b, [rden.b[rk]], out=r4, in0=PB[5][:, 0:512].rearrange("p (c q) -> p c q", c=4),
                                                in1=bc(expsink.t[:, 4 * hp:4 * hp + 4], 128), op=ALU.add)
                                            DVE("reciprocal", [rden.b[rk]], [rden.b[rk]], out=rden.t[:, rk, :], in_=rden.t[:, rk, :])
                                            ch0 = 8 + 4 * hp
                                            DVE("tensor_tensor", [PBb[4], rden.b[rk]], [yab_b(ch0 + c, jl) for c in range(4)],
                                                out=yab.t[:, ch0:ch0 + 4, tok0 + s * 128:tok0 + (s + 1) * 128], in0=PB[4][:, 0:512].rearrange("p (c q) -> p c q", c=4), in1=r4, op=ALU.mult)
                                    return qk, rest
                                units.append(mk_swa())
                        nkb = 2 * j + 2
                        for hl in range(4):
                            for kb in range(nkb):
                                def mk_moba(bi=bi, j=j, jl=jl, tok0=tok0, hl=hl, kb=kb, nkb=nkb):
                                    sbk = s_rr[0] % 4
                                    s_rr[0] += 1
                                    pi = p_rr[0] % 3
                                    p_rr[0] += 1
                                    odb = 4 + hl % 2

                                    def qk():
                                        for kt in range(2):
                                            ktile = kb * 2 + kt
                                            mm(PB[sbk][:, kt * 256:(kt + 1) * 256], kAT.t[:, hl, ktile * 128:(ktile + 1) * 128], QAT.t[:, bi, hl, :], True, False,
                                               [kAT.b[hl * 8 + ktile // 4], QAT.b[bi * 2 + hl // 2]], [PBb[sbk]])
                                            mm(PB[sbk][:, kt * 256:(kt + 1) * 256], indoh.t[:, kb, :], MT.t[:, bi, hl, :], False, True, indoh.b + [MT.b[bi]], [PBb[sbk]])

                                    def rest():
                                        ACT([PBb[sbk]], [Pm.b[pi]], out=Pm.t[:, pi, :], in_=PB[sbk][:, 0:512], func=AF.Exp)
                                        isp = kb - (2 * j - 1)
                                        if 0 <= isp <= 2:
                                            DVE("tensor_tensor", [Pm.b[pi]] + WA.b, [Pm.b[pi]], out=Pm.t[:, pi, :], in0=Pm.t[:, pi, :], in1=WA.t[:, hl, isp, :], op=ALU.mult)
                                        for kt in range(2):
                                            ktile = kb * 2 + kt
                                            first = (kb == 0 and kt == 0)
                                            last = (kb == nkb - 1 and kt == 1)
                                            mm(PB[odb][:, 0:256], vA.t[:, ktile, hl * 128:(hl + 1) * 128], Pm.t[:, pi, kt * 256:(kt + 1) * 256], first, last,
                                               [vA.b[ktile // 4], Pm.b[pi]], [PBb[odb]])
                                            mm(PB[odb][:, 256:512], ones_bf.t[:], Pm.t[:, pi, kt * 256:(kt + 1) * 256], False, last,
                                               ones_bf.b + [Pm.b[pi]], [PBb[odb]], skip_group_check=True)
                                        if kb == nkb - 1:
                                            rk = hl % 2
                                            DVE("reciprocal", [PBb[odb]], [rden.b[rk]], out=rden.t[:, rk, 0:256], in_=PB[odb][:, 256:512])
                                            DVE("tensor_tensor", [PBb[odb], rden.b[rk]], [yab_b(4 * hp + hl, jl)], out=yab.t[:, 4 * hp + hl, tok0:tok0 + 256], in0=PB[odb][:, 0:256], in1=rden.t[:, rk, 0:256], op=ALU.mult)
                                    return qk, rest
                                units.append(mk_moba())
                    units[0][0]()
                    for ui in range(len(units)):
                        if ui + 1 < len(units):
                            units[ui + 1][0]()
                        units[ui][1]()
                        if ui == 5:
                            late_gate[1]()
                    if rnd == 0:
                        S.barrier()

            S.barrier()
            C = top.rest()
            x1 = C.alloc("x1", [128, 4, D], F32, nbufs=16)
            aT = C.alloc("aT", [128, 44, 512], BF16, nbufs=44)
            hT = C.alloc("hT", [128, 16, 512], BF16, nbufs=4)
            mT = C.alloc("mT", [128, 16, 512], BF16, nbufs=16)
            xb = C.alloc("xbC", [128, D], BF16)
            ring = C.alloc("ringC", [128, 8 * 2048], BF16, nbufs=8)
            sg = C.alloc("sg", [128, 2, 512], F32, nbufs=2)
            mm1 = C.alloc("mm1", [128, 2, 512], F32, nbufs=2)
            rr = [0]
            pb = [0]

            def rnext():
                i = rr[0] % 8
                rr[0] += 1
                return i

            def pnext():
                i = pb[0] % 6
                pb[0] += 1
                return i

            for tl in range(2):
                t = 2 * th + tl
                tk0 = tl * 512
                for s in range(4):
                    for q4 in range(4):
                        DMA("sp", x1.t[:, s, q4 * 512:(q4 + 1) * 512], xo_v[t * 4 + s][:, q4 * 512:(q4 + 1) * 512], writes=[x1.b[s * 4 + q4]])
                    norm_tile(x1.t[:, s, :], x1.b[s * 4:(s + 1) * 4], xb, 0, hT, [hT.b[s]], s * 128)
                for fc in range(16):
                    sa = rnext()
                    DMA("pool", slot128(ring, sa), w_in_v[:, :, C_GA + fc * 128:C_GA + (fc + 1) * 128], writes=[ring.b[sa]])
                    sb_ = rnext()
                    DMA("pool", slot128(ring, sb_), w_in_v[:, :, C_GB + fc * 128:C_GB + (fc + 1) * 128], writes=[ring.b[sb_]])
                    sc = rnext()
                    cview = slot128(ring, sc)
                    DMA("pool", cview[:, 0:8, :], w_ba_v[:, :, fc * 128:(fc + 1) * 128], writes=[ring.b[sc]])
                    DMA("pool", cview[:, 8:16, :], w_bb_v[:, :, fc * 128:(fc + 1) * 128], writes=[ring.b[sc]])
                    pg = [pnext() for _ in range(4)]
                    for br, (si, ch0) in enumerate(((sa, 0), (sb_, 8))):
                        g_b = pg[2 * br]
                        z_b = pg[2 * br + 1]
                        wg = slot128(ring, si)
                        for kc in range(16):
                            mm(PB[g_b][:, :], wg[:, kc, :], hT.t[:, kc, :], kc == 0, kc == 15, [ring.b[si]] + hT.b, [PBb[g_b]])
                        for c in range(8):
                            mm(PB[z_b][:, :], cview[:, 8 * br + c, :], yab.t[:, ch0 + c, tk0:tk0 + 512], c == 0, c == 7,
                               [ring.b[sc], yab_b(ch0 + c, 2 * tl), yab_b(ch0 + c, 2 * tl + 1)], [PBb[z_b]])
                        ACT([PBb[g_b]], [sg.b[br]], out=sg.t[:, br, :], in_=PB[g_b][:, :], func=AF.Sigmoid)
                        DVE("tensor_tensor", [sg.b[br], PBb[z_b]], [mm1.b[br]], out=mm1.t[:, br, :], in0=sg.t[:, br, :], in1=PB[z_b][:, :], op=ALU.mult)
                    DVE("tensor_tensor", mm1.b, [mT.b[fc]], out=mT.t[:, fc, :], in0=mm1.t[:, 0, :], in1=mm1.t[:, 1, :], op=ALU.add)
                for cg in range(4):
                    sl = [rnext() for _ in range(4)]
                    for i, si in enumerate(sl):
                        DMA("pool", slot(ring, si).rearrange("p (c n) -> p c n", n=512), w_out_v[:, 4 * i:4 * i + 4, cg * 512:(cg + 1) * 512], writes=[ring.b[si]])
                    for s in range(4):
                        pi = pnext()
                        for kc in range(16):
                            si = sl[kc // 4]
                            mm(PB[pi][:, :], mT.t[:, kc, s * 128:(s + 1) * 128], slot(ring, si)[:, (kc % 4) * 512:(kc % 4 + 1) * 512], kc == 0, kc == 15, [mT.b[kc], ring.b[si]], [PBb[pi]])
                        xs = x1.t[:, s, cg * 512:(cg + 1) * 512]
                        DVE("tensor_tensor", [x1.b[s * 4 + cg], PBb[pi]], [x1.b[s * 4 + cg]], out=xs, in0=xs, in1=PB[pi][:, :], op=ALU.add)
                for s in range(4):
                    norm_tile(x1.t[:, s, :], x1.b[s * 4:(s + 1) * 4], xb, 16, hT, [hT.b[s]], s * 128)
                for fc in range(DFF // 128):
                    sgi = rnext()
                    DMA("pool", slot128(ring, sgi), w_gu_v[:, :, fc * 128:(fc + 1) * 128], writes=[ring.b[sgi]])
                    sui = rnext()
                    DMA("pool", slot128(ring, sui), w_gu_v[:, :, DFF + fc * 128:DFF + (fc + 1) * 128], writes=[ring.b[sui]])
                    g_b = pnext()
                    u_b = pnext()
                    wg = slot128(ring, sgi)
                    wu = slot128(ring, sui)
                    for kc in range(16):
                        mm(PB[g_b][:, :], wg[:, kc, :], hT.t[:, kc, :], kc == 0, kc == 15, [ring.b[sgi]] + hT.b, [PBb[g_b]])
                    for kc in range(16):
                        mm(PB[u_b][:, :], wu[:, kc, :], hT.t[:, kc, :], kc == 0, kc == 15, [ring.b[sui]] + hT.b, [PBb[u_b]])
                    k = fc % 2
                    ACT([PBb[g_b]], [sg.b[k]], out=sg.t[:, k, :], in_=PB[g_b][:, :], func=AF.Silu)
                    DVE("tensor_tensor", [sg.b[k], PBb[u_b]], [aT.b[fc]], out=aT.t[:, fc, :], in0=sg.t[:, k, :], in1=PB[u_b][:, :], op=ALU.mult)
                for fg in range(DFF // 512):
                    sl = [rnext() for _ in range(4)]
                    for i, si in enumerate(sl):
                        DMA("pool", slot(ring, si), w_dn_v[:, fg * 4 + i, :], writes=[ring.b[si]])
                    for s in range(4):
                        for cg in range(4):
                            pi = pnext()
                            for i, si in enumerate(sl):
                                mm(PB[pi][:, :], aT.t[:, fg * 4 + i, s * 128:(s + 1) * 128], slot(ring, si)[:, cg * 512:(cg + 1) * 512], i == 0, i == 3, [aT.b[fg * 4 + i], ring.b[si]], [PBb[pi]])
                            xs = x1.t[:, s, cg * 512:(cg + 1) * 512]
                            DVE("tensor_tensor", [x1.b[s * 4 + cg], PBb[pi]], [x1.b[s * 4 + cg]], out=xs, in0=xs, in1=PB[pi][:, :], op=ALU.add)
                for s in range(4):
                    for q4 in range(4):
                        DMA("sp", y_v[t * 4 + s][:, q4 * 512:(q4 + 1) * 512], x1.t[:, s, q4 * 512:(q4 + 1) * 512], reads=[x1.b[s * 4 + q4]], final=True)

        S.emit(st)
    return nc


def _t5_bucket(dist):
    n = np.maximum(dist, 0).astype(np.int32)
    nf = np.maximum(n, 1).astype(np.float32)
    large = 16 + (np.log(nf / np.float32(16)) / np.float32(math.log(128 / 16)) * np.float32(16)).astype(np.int32)
    large = np.minimum(large, 31)
    return np.where(n < 16, n, large)


def _host_consts(rel_bias, sinks, g):
    ta = rel_bias[:, :8]
    tb = rel_bias[:, 8:]
    negf = np.float32(NEG)
    kk = np.arange(256)[:, None]
    qq = np.arange(256)[None, :]
    d_prev = 256 + qq - kk
    d_own = qq - kk
    bk_prev = _t5_bucket(d_prev)
    bk_own = _t5_bucket(d_own)
    prev = np.stack([ta[bk_prev, h] for h in range(8)], 0)
    own = np.stack([np.where(d_own >= 0, ta[bk_own, h], negf) for h in range(8)], 0)
    zero = np.zeros_like(prev)
    negall = np.full_like(prev, negf)
    kinds = (prev, own, negall) if g == 0 else (zero, prev, own)
    biasA = np.stack(kinds, 0).astype(np.float32)
    biasA = biasA.reshape(3, 8, 2, 128, 256).transpose(0, 1, 3, 2, 4).reshape(3 * 8 * 128, 512)
    k1 = np.arange(128)[:, None]
    q1 = np.arange(128)[None, :]
    do = q1 - k1
    dp = 128 + q1 - k1
    bo = _t5_bucket(do)
    bp = _t5_bucket(dp)
    biasB = np.empty((3, 2, 128, 2, 4, 128), np.float32)
    for hp in range(2):
        for e in range(2):
            for c in range(4):
                h = 8 * hp + 2 * c + e
                o = np.where(do >= 0, tb[bo, h], negf)
                p = np.where(dp < 128, tb[bp, h], negf)
                biasB[0, hp, :, e, c, :] = o
                biasB[1, hp, :, e, c, :] = p
                biasB[2, hp, :, e, c, :] = p if g == 1 else negf
    biasB = biasB.reshape(3 * 2 * 128, 1024)
    mc = np.zeros((8, 4, 16), np.float32)
    for j in range(8):
        ownb = 2 * j + g
        n = np.arange(16)
        mc[j, 0] = np.where(n < ownb, 1e30, -1e30)
        mc[j, 1] = (n < ownb)
        mc[j, 2] = (n == ownb)
        mc[j, 3] = (n < ownb - 1)
    mconst = np.broadcast_to(mc.reshape(1, -1), (128, 8 * 4 * 16)).astype(np.float32).copy()
    return biasA, biasB, mconst


def _prep_inputs(x, norm1_g, w_in, q_norm_a, k_norm_a, q_norm_b, k_norm_b, rel_bias, sinks,
                 w_branch_a, w_branch_b, w_out, norm2_g, w_gate_up, w_down):
    f = lambda a: np.ascontiguousarray(np.asarray(a, dtype=np.float32))
    x = f(x)
    w_in0, w_ba, w_bb, w_o, w_gu, w_dn = f(w_in[0]), f(w_branch_a[0]), f(w_branch_b[0]), f(w_out[0]), f(w_gate_up[0]), f(w_down[0])
    rel_bias = f(rel_bias)
    sinks0 = f(sinks[0])
    vecs_base = np.zeros((128, NVEC), np.float32)
    vecs_base[:, 0:16] = f(norm1_g[0]).reshape(16, 128).T
    vecs_base[:, 16:32] = f(norm2_g[0]).reshape(16, 128).T
    vecs_base[:, 32] = f(q_norm_a[0])
    vecs_base[:, 33] = f(k_norm_a[0])
    vecs_base[:, 34] = np.tile(f(q_norm_b[0]), 2)
    vecs_base[:, 35] = np.tile(f(k_norm_b[0]), 2)
    vecs_base[:, 36:44] = rel_bias[31, 0:8][None, :]
    for hp in range(2):
        for c in range(4):
            vecs_base[0:64, 44 + 4 * hp + c] = sinks0[8 * hp + 2 * c]
            vecs_base[64:128, 44 + 4 * hp + c] = sinks0[8 * hp + 2 * c + 1]
    ident = np.eye(128, dtype=np.float32)
    onesbd = np.ones((128, 256), np.float32)
    bd = np.zeros((128, 128), np.float32)
    bd[:64, :64] = 1.0
    bd[64:, 64:] = 1.0
    onesbd[:, 128:] = bd
    indoh = np.zeros((128, 16, 128), np.float32)
    for n in range(16):
        indoh[n, n, :] = 1.0
    indoh = indoh.reshape(128, 16 * 128)
    consts = {g: _host_consts(rel_bias, sinks0, g) for g in range(2)}
    in_maps = []
    for c in range(8):
        b, g = c // 2, c % 2
        xb = x[b]
        xblk = xb.reshape(NB, LB, D)
        xo = np.ascontiguousarray(xblk[g::2].reshape(2048, D))
        xp = np.zeros((8, 128, D), np.float32)
        for j in range(8):
            ob = 2 * j + g
            if ob > 0:
                xp[j] = xb[ob * LB - 128:ob * LB]
        biasA, biasB, mconst = consts[g]
        in_maps.append({
            "xc": xb, "xo": xo, "xp": xp.reshape(8 * 128, D),
            "w_in": w_in0, "w_ba": w_ba, "w_bb": w_bb, "w_out": w_o, "w_gu": w_gu, "w_dn": w_dn,
            "vecs": vecs_base, "mconst": mconst, "ident": ident, "onesbd": onesbd, "indoh": indoh,
            "biasA": biasA, "biasB": biasB,
        })
    return in_maps


_NC_CACHE = {}


def kernel(**inputs):
    in_maps = _prep_inputs(**inputs)
    if "nc" not in _NC_CACHE:
        _NC_CACHE["nc"] = build_program()
    nc = _NC_CACHE["nc"]
    res = run_bass_kernel_spmd(nc, in_maps, core_ids=list(range(8)))
    out = np.empty((4, SEQ, D), np.float32)
    for c in range(8):
        b, g = c // 2, c % 2
        y = np.asarray(res.results[c]["y"], dtype=np.float32).reshape(8, LB, D)
        out[b].reshape(NB, LB, D)[g::2] = y
    return out
```
